# Optimizing a Trainium2 kernel written in Bass

```python
import math
import jax, jax.numpy as jnp
from jax import lax
import numpy as np

D_MODEL = 1024
BATCH = 4
SEQ = 8192
DEPTH = 4
DEC_BATCH = 8
DEC_SEQ = 16
PAST_LEN = 4096

CHUNK = 64
N_MIXERS = 2
N_HEADS = 8
HEAD_DIM = D_MODEL // N_HEADS // 2
V_DIM = 2 * HEAD_DIM
D_FF = 4 * D_MODEL
CONV_WIDTH = 3
Q_BLOCK = 128
N_ATTN_LAYERS = (DEPTH + 1) // 2
N_CONV_LAYERS = DEPTH // 2
DEEPNORM_ALPHA = (2.0 * DEPTH) ** 0.25
DEEPNORM_BETA = (8.0 * DEPTH) ** -0.25
LN_EPS = 1e-5
SUBLN_EPS = 1e-5

kernel_name = "hybrid_diffattn_shortconv_stream_step"


def alibi_slopes(n_heads):
    return jnp.asarray([2.0 ** (-8.0 * (h + 1) / n_heads) for h in range(n_heads)], dtype=jnp.float32)


def lambda_init_of(depth):
    return 0.8 - 0.6 * math.exp(-0.3 * depth)


def layer_norm(x, g, b):
    xf = x.astype(jnp.float32)
    mu = jnp.mean(xf, axis=-1, keepdims=True)
    var = jnp.mean(jnp.square(xf - mu), axis=-1, keepdims=True)
    y = (xf - mu) * lax.rsqrt(var + LN_EPS) * g.astype(jnp.float32) + b.astype(jnp.float32)
    return y.astype(x.dtype)


def diff_lambda(lq1, lk1, lq2, lk2, lam_init):
    f32 = lambda a: a.astype(jnp.float32)
    return jnp.exp(jnp.sum(f32(lq1) * f32(lk1))) - jnp.exp(jnp.sum(f32(lq2) * f32(lk2))) + lam_init


def diff_attn_project(x, w_qkv):
    b, t, _ = x.shape
    q, k, v = jnp.split(x @ w_qkv, 3, axis=-1)
    q = q.reshape(b, t, N_HEADS, 2, HEAD_DIM)
    k = k.reshape(b, t, N_HEADS, 2, HEAD_DIM)
    v = v.reshape(b, t, N_HEADS, V_DIM)
    return q, k, v


def diff_attend(q, k, v, q_pos, k_pos, lam):
    s = jnp.einsum('bqhcd,bkhcd->bhcqk', q, k).astype(jnp.float32) * (HEAD_DIM ** -0.5)
    dist = jnp.abs(q_pos[:, None] - k_pos[None, :]).astype(jnp.float32)
    bias = -alibi_slopes(N_HEADS)[:, None, None, None] * dist
    visible = (k_pos[None, :] // CHUNK) <= (q_pos[:, None] // CHUNK)
    s = jnp.where(visible, s + bias, -jnp.inf)
    p = jax.nn.softmax(s, axis=-1)
    a = p[:, :, 0] - lam * p[:, :, 1]
    return jnp.einsum('bhqk,bkhe->bqhe', a.astype(v.dtype), v)


def diff_attn_out(o, lam_init, g_subln, w_o):
    b, t = o.shape[:2]
    of = o.astype(jnp.float32)
    of = of * lax.rsqrt(jnp.mean(jnp.square(of), axis=-1, keepdims=True) + SUBLN_EPS)
    of = of * g_subln.astype(jnp.float32) * (1.0 - lam_init)
    return of.astype(o.dtype).reshape(b, t, N_HEADS * V_DIM) @ w_o


def attn_prompt(q, k, v, lam):
    b, t = q.shape[:2]
    nb = t // Q_BLOCK
    pos = jnp.arange(t, dtype=jnp.int32)
    qb = jnp.moveaxis(q.reshape(b, nb, Q_BLOCK, N_HEADS, 2, HEAD_DIM), 1, 0)
    pb = pos.reshape(nb, Q_BLOCK)
    ob = lax.map(lambda args: diff_attend(args[0], k, v, args[1], pos, lam), (qb, pb))
    return jnp.moveaxis(ob, 0, 1).reshape(b, t, N_HEADS, V_DIM)


def attn_sample(q, k, v, cache_k, cache_v, lam):
    past = cache_k.shape[1]
    t = q.shape[1]
    k_all = jnp.concatenate([cache_k.astype(k.dtype), k], axis=1)
    v_all = jnp.concatenate([cache_v.astype(v.dtype), v], axis=1)
    k_pos = jnp.arange(past + t, dtype=jnp.int32)
    q_pos = past + jnp.arange(t, dtype=jnp.int32)
    return diff_attend(q, k_all, v_all, q_pos, k_pos, lam)


def short_conv(x, hist, w_in, conv_w, w_out):
    bg, cg, h = jnp.split(x @ w_in, 3, axis=-1)
    u = cg * h
    up = jnp.concatenate([hist.astype(u.dtype), u], axis=1)
    n = u.shape[1]
    z = sum(conv_w[j] * up[:, j:j + n] for j in range(CONV_WIDTH))
    return (bg * z) @ w_out, up[:, -(CONV_WIDTH - 1):]


def sq_relu_mlp(x, w_up, w_down):
    return jnp.square(jax.nn.relu(x @ w_up)) @ w_down


def setup_inputs(seed: int = 0) -> dict:
    key = jax.random.key(seed)
    ks = jax.random.split(key, 20)
    f32 = jnp.float32
    d = D_MODEL
    nrm = lambda k, shape, scale: jax.random.normal(k, shape, f32) * scale
    v_col_scale = jnp.concatenate([jnp.ones((2 * d,), f32), jnp.full((d,), DEEPNORM_BETA, f32)])
    return {
        "x_prompt": nrm(ks[0], (BATCH, SEQ, d), 1.0),
        "x_sample": nrm(ks[1], (DEC_BATCH, DEC_SEQ, d), 1.0),
        "cache_k": nrm(ks[2], (N_ATTN_LAYERS, DEC_BATCH, PAST_LEN, N_HEADS, 2, HEAD_DIM), 1.0),
        "cache_v": nrm(ks[3], (N_ATTN_LAYERS, DEC_BATCH, PAST_LEN, N_HEADS, V_DIM), DEEPNORM_BETA),
        "state_conv": nrm(ks[4], (N_CONV_LAYERS, DEC_BATCH, CONV_WIDTH - 1, d), 1.0),
        "w_qkv": nrm(ks[5], (N_ATTN_LAYERS, d, 3 * d), d ** -0.5) * v_col_scale,
        "lambda_q1": nrm(ks[6], (N_ATTN_LAYERS, HEAD_DIM), 0.1),
        "lambda_k1": nrm(ks[7], (N_ATTN_LAYERS, HEAD_DIM), 0.1),
        "lambda_q2": nrm(ks[8], (N_ATTN_LAYERS, HEAD_DIM), 0.1),
        "lambda_k2": nrm(ks[9], (N_ATTN_LAYERS, HEAD_DIM), 0.1),
        "g_subln": 1.0 + nrm(ks[10], (N_ATTN_LAYERS, V_DIM), 0.02),
        "w_attn_out": nrm(ks[11], (N_ATTN_LAYERS, d, d), DEEPNORM_BETA * d ** -0.5),
        "w_conv_in": nrm(ks[12], (N_CONV_LAYERS, d, 3 * d), d ** -0.5),
        "w_conv": nrm(ks[13], (N_CONV_LAYERS, CONV_WIDTH, d), CONV_WIDTH ** -0.5),
        "w_conv_out": nrm(ks[14], (N_CONV_LAYERS, d, d), DEEPNORM_BETA * d ** -0.5),
        "w_up": nrm(ks[15], (DEPTH, d, D_FF), DEEPNORM_BETA * d ** -0.5),
        "w_down": nrm(ks[16], (DEPTH, D_FF, d), DEEPNORM_BETA * D_FF ** -0.5),
        "ln_g": 1.0 + nrm(ks[17], (DEPTH, 2, d), 0.02),
        "ln_b": nrm(ks[18], (DEPTH, 2, d), 0.02),
    }


def reference(x_prompt, x_sample, cache_k, cache_v, state_conv, w_qkv, lambda_q1, lambda_k1,
              lambda_q2, lambda_k2, g_subln, w_attn_out, w_conv_in, w_conv, w_conv_out,
              w_up, w_down, ln_g, ln_b):
    xp, xs = x_prompt, x_sample
    kp_list, vp_list, cp_list = [], [], []
    ks_list, vs_list, cs_list = [], [], []
    for i in range(DEPTH):
        li = i // N_MIXERS
        if i % N_MIXERS == 0:
            lam_init = lambda_init_of(i)
            lam = diff_lambda(lambda_q1[li], lambda_k1[li], lambda_q2[li], lambda_k2[li], lam_init)
            q, k, v = diff_attn_project(xp, w_qkv[li])
            mp = diff_attn_out(attn_prompt(q, k, v, lam), lam_init, g_subln[li], w_attn_out[li])
            kp_list.append(k)
            vp_list.append(v)
            q, k, v = diff_attn_project(xs, w_qkv[li])
            ms = diff_attn_out(attn_sample(q, k, v, cache_k[li], cache_v[li], lam), lam_init,
                               g_subln[li], w_attn_out[li])
            ks_list.append(k)
            vs_list.append(v)
        else:
            zero_hist = jnp.zeros((xp.shape[0], CONV_WIDTH - 1, D_MODEL), xp.dtype)
            mp, cp = short_conv(xp, zero_hist, w_conv_in[li], w_conv[li], w_conv_out[li])
            ms, cs = short_conv(xs, state_conv[li], w_conv_in[li], w_conv[li], w_conv_out[li])
            cp_list.append(cp)
            cs_list.append(cs)
        xp = layer_norm(DEEPNORM_ALPHA * xp + mp, ln_g[i, 0], ln_b[i, 0])
        xs = layer_norm(DEEPNORM_ALPHA * xs + ms, ln_g[i, 0], ln_b[i, 0])
        xp = layer_norm(DEEPNORM_ALPHA * xp + sq_relu_mlp(xp, w_up[i], w_down[i]), ln_g[i, 1], ln_b[i, 1])
        xs = layer_norm(DEEPNORM_ALPHA * xs + sq_relu_mlp(xs, w_up[i], w_down[i]), ln_g[i, 1], ln_b[i, 1])
    return (xp, xs, jnp.stack(kp_list), jnp.stack(vp_list), jnp.stack(cp_list),
            jnp.stack(ks_list), jnp.stack(vs_list), jnp.stack(cs_list))
```

```python
import math
import os
import numpy as np
import concourse.bass as bass
import concourse.mybir as mybir
from concourse.bass_utils import run_bass_kernel_spmd

F32 = mybir.dt.float32
BF16 = mybir.dt.bfloat16
ALU = mybir.AluOpType
AF = mybir.ActivationFunctionType

D = 1024
SEQ = 8192
NH = 8
DEPTH = 4
PAST = 4096
DEC = 16
TT = 512
ALPHA = (2.0 * DEPTH) ** 0.25
LN_EPS = 1e-5
SLOPES = [2.0 ** (-8.0 * (h + 1) / NH) for h in range(NH)]
NSLOT_W = 3
NSLOT_KV = 4
VW = 129
CAST_AHEAD = 5
ALIBI_THR = 125.0
KVC = 1024


def lam_init_of(i):
    return 0.8 - 0.6 * math.exp(-0.3 * i)


ENGS = ("pe", "act", "dve", "pool", "sp")


class Res:
    __slots__ = ("name", "w", "r", "const")

    def __init__(self, name):
        self.name = name
        self.w = None
        self.r = {}
        self.const = False


class DSem:
    __slots__ = ("key", "count")

    def __init__(self, key):
        self.key = key
        self.count = 0


class Sched:
    def __init__(self, nc):
        self.nc = nc
        self.streams = {e: [] for e in ENGS}
        self.sems = {}
        self.ecnt = {e: 0 for e in ENGS}
        self.known = {e: {} for e in ENGS}
        self.selfsync = {"pe": False, "act": True, "dve": True, "pool": True, "sp": True}
        for e in ENGS:
            self._sem("E_" + e)
        self.dsems = []
        self.n_ops = 0
        self.dead = False
        self.stop = int(os.environ.get('K_STOP', '99'))

    def _sem(self, key):
        if key not in self.sems:
            self.sems[key] = self.nc.alloc_semaphore(name=key)
        return key

    def dsem(self, name):
        d = DSem(self._sem("D_" + name))
        self.dsems.append(d)
        return d

    def _wait(self, eng, ev):
        if ev is None:
            return
        key, val = ev
        if key == "E_" + eng and not self.selfsync[eng]:
            return
        k = self.known[eng]
        if k.get(key, 0) >= val:
            return
        k[key] = val
        self.streams[eng].append(("wait", key, val))

    def _deps(self, eng, reads, writes):
        for r in reads:
            self._wait(eng, r.w)
        for w in writes:
            self._wait(eng, w.w)
            for key, val in w.r.items():
                self._wait(eng, (key, val))

    def _record(self, ev, reads, writes):
        for r in reads:
            if r.const:
                continue
            if r.r.get(ev[0], 0) < ev[1]:
                r.r[ev[0]] = ev[1]
        for w in writes:
            w.w = ev
            w.r = {}

    def stage(self, n):
        if n >= self.stop:
            self.dead = True

    def op(self, eng, fn, reads=(), writes=()):
        if self.dead:
            return None
        self._deps(eng, reads, writes)
        self.ecnt[eng] += 1
        ev = ("E_" + eng, self.ecnt[eng])
        self.streams[eng].append(("op", fn, ev[0], 1))
        self._record(ev, reads, writes)
        self.n_ops += 1
        return ev

    def dma(self, eng, dsem, out, in_, reads=(), writes=(), **kw):
        if self.dead:
            return None
        self._deps(eng, reads, writes)
        dsem.count += 16
        ev = (dsem.key, dsem.count)

        def fn(e, out=out, in_=in_, kw=kw):
            return e.dma_start(out=out, in_=in_, **kw)
        self.streams[eng].append(("op", fn, ev[0], 16))
        self._record(ev, reads, writes)
        self.n_ops += 1
        return ev

    def finish(self):
        for d in self.dsems:
            if d.count:
                self._wait("sp", (d.key, d.count))
        for e in ENGS:
            if e != "sp" and self.ecnt[e]:
                self._wait("sp", ("E_" + e, self.ecnt[e]))

    def _replay(self, eng, e):
        for item in self.streams[eng]:
            if item[0] == "wait":
                e.wait_ge(self.sems[item[1]], item[2])
            else:
                _, fn, key, amt = item
                fn(e).then_inc(self.sems[key], amt)

    def emit(self):
        self.finish()
        with self.nc.Block() as block:
            @block.tensor
            def _(e):
                self._replay("pe", e)

            @block.scalar
            def _(e):
                self._replay("act", e)

            @block.vector
            def _(e):
                self._replay("dve", e)

            @block.gpsimd
            def _(e):
                self._replay("pool", e)

            @block.sync
            def _(e):
                self._replay("sp", e)


class Buf:
    def __init__(self, S, t, name, dma=False):
        self.t = t
        self.name_ = name
        self.res = Res(name)
        self.ds = S.dsem(name) if dma else None


NPS = SEQ // TT + 1
NSS = 3
PAIRS = [[0, 1], [2, 3], [4, 5], [6, 7]]


def build_program(n_psteps=NPS, n_ssteps=NSS, exchange=True):
    nc = bass.Bass("TRN2", target_bir_lowering=False)
    S = Sched(nc)

    def din(name, shape, dt=F32):
        return nc.dram_tensor(name, list(shape), dt, kind="ExternalInput").ap()

    def dout(name, shape, dt=F32):
        return nc.dram_tensor(name, list(shape), dt, kind="ExternalOutput").ap()

    def dscr(name, shape, dt=BF16):
        return nc.dram_tensor(name, list(shape), dt).ap()

    NROW = NPS * TT
    xp = din("xp", [NROW, D])
    xs = din("xs", [NSS, DEC, D])
    ck = din("ck", [NSS, PAST, D])
    cv = din("cv", [NSS, PAST, D])
    sc = din("sc", [NSS, 2, D])
    w_qkv = din("w_qkv", [D, 3 * D])
    w_ao = din("w_ao", [D, D])
    w_ci = din("w_ci", [D, 3 * D])
    w_cw = din("w_cw", [3, D])
    w_co = din("w_co", [D, D])
    w_up = din("w_up", [2, D, 4 * D])
    w_dn = din("w_dn", [2, 4 * D, D])
    lam4 = din("lam4", [4, 64])
    gsub = din("gsub", [128, 1])
    ln_g = din("ln_g", [4, D])
    ln_b = din("ln_b", [4, D])
    linit_d = din("linit", [128, 2])
    sel_d = din("sel", [128, 1])
    hm_d = din("hm", [128, NPS])
    ones0_d = din("ones0", [128, 128])
    ident_d = din("ident", [128, 128])
    bias_d = din("biastab", [128, NH * 64])
    mtab_d = din("mtab", [128, 4 * TT])
    qrel_d = din("qrel", [128, TT])
    ftab_d = din("ftab", [128, NH * 4])

    yp = dout("yp", [NROW, D])
    ys = dout("ys", [NSS, DEC, D])
    kp = dout("kp", [NROW, D])
    vp = dout("vp", [NROW, D])
    cpo = dout("cpo", [2, 2, D])
    ks = dout("ks", [NSS, DEC, D])
    vs = dout("vs", [NSS, DEC, D])
    cs = dout("cs", [NSS, 2, D])

    wscr = {}
    KT_p = dscr("KT_p", [NH, 128, NROW])
    V_p = dscr("V_p", [NH, NROW, 128])
    NKS = PAST + 128
    KT_s = [dscr(f"KT_s{j}", [NH, 128, NKS]) for j in range(NSS)]
    V_s = [dscr(f"V_s{j}", [NH, NKS, 128]) for j in range(NSS)]
    send = dscr("send", [TT, D], F32)
    recv = dscr("recv", [2 * TT, D], F32)
    send_res, recv_res = Res("send"), Res("recv")
    scr_res = {}

    def sres(key):
        if key not in scr_res:
            scr_res[key] = Res("scr" + str(key))
        return scr_res[key]

    A = nc.alloc_sbuf_tensor

    def sb(name, shape, dt=F32, dma=False):
        return Buf(S, A("sb_" + name, list(shape), dt), name, dma)

    x_tm = sb("x_tm", [128, 4, D], F32, dma=True)
    xres = [Res(f"x_tm{s}") for s in range(4)]
    xT = sb("xT", [128, 8, TT], BF16)
    xTres = [Res(f"xT{k}") for k in range(8)]
    ring = [sb(f"wring{i}", [128, 8, 1024], BF16, dma=True) for i in range(NSLOT_W)]
    QT = [sb(f"QT{c}", [128, NH, TT], BF16) for c in range(2)]
    QTres = [Res(f"QTh{h}") for h in range(NH)]
    KTc = [sb(f"KTc{i}", [128, KVC], BF16, dma=True) for i in range(NSLOT_KV)]
    Vc = [sb(f"Vc{i}", [128, KVC // 128, VW], BF16, dma=True) for i in range(NSLOT_KV)]
    Pt = [sb(f"P{i}", [128, TT], BF16) for i in range(4)]
    OnT = sb("OnT", [128, NH, TT], BF16)
    OnTres = [Res(f"OnT{h}") for h in range(NH)]
    Ap = sb("Ap", [128, 8 * VW], F32)
    At = sb("At", [128, 8 * VW], F32)
    Onf = sb("Onf", [128, 4, 128], F32)
    Otm = sb("Otm", [128, 4, 128], F32)
    rr = sb("rr", [128, 8, 1], F32)
    ss4 = sb("ss4", [128, 4, 1], F32)
    glb = sb("glb", [128, 128], F32, dma=True)
    ftab = sb("ftab", [128, NH * 4], F32, dma=True)
    tmpf = [sb(f"tmpf{i}", [128, TT], F32) for i in range(2)]
    dtmp = tmpf
    rbuf = tmpf
    zc = tmpf
    ktb = sb("ktb", [128, NH, TT], BF16, dma=True)
    hT = [OnT, ktb]
    biast = sb("biast", [128, NH * 64], F32, dma=True)
    mtab = sb("mtab", [128, 4 * TT], F32, dma=True)
    qrel = sb("qrel", [128, TT], F32, dma=True)
    ident = sb("ident", [128, 128], F32, dma=True)
    ones = sb("ones", [128, 128], BF16)
    ones0f = sb("ones0f", [128, 128], F32, dma=True)
    ones0 = sb("ones0", [128, 128], BF16)
    epst = sb("epst", [128, 1], F32)
    gbt = [sb(f"gb{i}", [128, 2, D], F32, dma=True) for i in range(2)]
    kst = [sb(f"kst{i}", [128, 512], F32, dma=True) for i in range(2)]
    vst = [sb(f"vst{i}", [128, 512], F32, dma=True) for i in range(2)]
    vbb = [sb(f"vbb{i}", [128, 512], BF16, dma=True) for i in range(2)]
    uext = sb("uext", [128, 8, TT + 4], F32, dma=True)
    ures = [Res(f"uext{k}") for k in range(8)]
    halo = sb("halo", [128, 8, 2], F32, dma=True)
    cwt = sb("cwt", [128, 3, 8], F32, dma=True)
    lamt = sb("lamt", [128, 4, 64], F32, dma=True)
    lamw = sb("lamw", [128, 2, 64], F32)
    lams = sb("lams", [128, 2], F32)
    neglam = sb("neglam", [128, 1], F32)
    glt = sb("glt", [128, 1], F32, dma=True)
    linit = sb("linit", [128, 2], F32, dma=True)
    selt = sb("selt", [128, 1], F32, dma=True)
    hmt = sb("hmt", [128, NPS], F32, dma=True)
    stt = sb("stt", [128, 4, 12], F32)
    mvt = sb("mvt", [128, 4, 2], F32)
    rs4 = sb("rs4", [128, 4, 1], F32)
    nm4 = sb("nm4", [128, 4, 1], F32)
    print("sbuf bytes remaining:", nc.sbuf_bytes_remaining)

    class _View:
        pass
    cache_ld = []
    for i in range(2):
        v = _View()
        v.t = x_tm.t[:, i, :]
        v.res = xres[i]
        v.ds = S.dsem(f"cld{i}")
        cache_ld.append(v)
    rview = uext.t[:].rearrange("p a b -> p (a b)")[:, 0:4 * D].rearrange("p (s d) -> p s d", s=4)

    banks = [Buf(S, nc.alloc_psum_tensor(f"bank{i}", [128, 512], F32), f"bank{i}") for i in range(8)]
    bank_rr = [0]

    def next_bank():
        b = banks[bank_rr[0] % 8]
        bank_rr[0] += 1
        return b

    S.dma("sp", ident.ds, ident.t[:], ident_d, writes=[ident.res])
    S.dma("sp", biast.ds, biast.t[:], bias_d, writes=[biast.res])
    S.dma("sp", mtab.ds, mtab.t[:], mtab_d, writes=[mtab.res])
    S.dma("sp", qrel.ds, qrel.t[:], qrel_d, writes=[qrel.res])
    S.dma("sp", ones0f.ds, ones0f.t[:], ones0_d, writes=[ones0f.res])
    S.dma("sp", linit.ds, linit.t[:], linit_d, writes=[linit.res])
    S.dma("sp", selt.ds, selt.t[:], sel_d, writes=[selt.res])
    S.dma("sp", hmt.ds, hmt.t[:], hm_d, writes=[hmt.res])
    S.dma("sp", glt.ds, glt.t[:], gsub, writes=[glt.res])
    S.dma("sp", ftab.ds, ftab.t[:], ftab_d, writes=[ftab.res])
    S.dma("sp", glb.ds, glb.t[:], gsub.rearrange("e o -> (e o)").partition_broadcast(128), writes=[glb.res])
    for i_ in range(NSLOT_KV):
        S.op("pool", lambda e, i_=i_: e.memset(Vc[i_].t[:, :, 128:129], 1.0), writes=[Vc[i_].res])
    S.op("dve", lambda e: e.memset(ones.t[:], 1.0), writes=[ones.res])
    S.op("dve", lambda e: e.memset(epst.t[:], LN_EPS), writes=[epst.res])
    S.op("dve", lambda e: e.tensor_copy(ones0.t[:], ones0f.t[:]), reads=[ones0f.res], writes=[ones0.res])
    S.op("pool", lambda e: e.memset(halo.t[:], 0.0), writes=[halo.res])
    for c in range(2):
        S.op("pool", lambda e, c=c: e.memset(QT[c].t[:], 0.0), writes=QTres)
    for j_ in range(3):
        S.dma("sp", cwt.ds, cwt.t[:, j_, :], w_cw[j_].rearrange("(dc p) -> p dc", p=128),
              writes=[cwt.res], allow_slow_non_contiguous=True)
    S.dma("sp", lamt.ds, lamt.t[:].rearrange("p b c -> p (b c)"),
          lam4.rearrange("b c -> (b c)").partition_broadcast(128), writes=[lamt.res])
    S.op("dve", lambda e: e.tensor_tensor(out=lamw.t[:], in0=lamt.t[:, 0:4:2, :], in1=lamt.t[:, 1:4:2, :], op=ALU.mult),
         reads=[lamt.res], writes=[lamw.res])
    S.op("dve", lambda e: e.tensor_reduce(out=lams.t[:], in_=lamw.t[:], axis=mybir.AxisListType.X, op=ALU.add),
         reads=[lamw.res], writes=[lams.res])
    S.op("act", lambda e: e.activation(out=lams.t[:], in_=lams.t[:], func=AF.Exp), reads=[lams.res], writes=[lams.res])
    S.op("dve", lambda e: e.scalar_tensor_tensor(out=neglam.t[:], in0=lams.t[:, 1:2], scalar=linit.t[:, 0:1],
                                                 in1=lams.t[:, 0:1], op0=ALU.add, op1=ALU.subtract),
         reads=[lams.res, linit.res], writes=[neglam.res])
    S.op("dve", lambda e: e.tensor_scalar(out=glt.t[:], in0=glt.t[:], scalar1=linit.t[:, 1:2], scalar2=None, op0=ALU.mult),
         reads=[glt.res, linit.res], writes=[glt.res])
    S.op("dve", lambda e: e.tensor_scalar(out=glb.t[:], in0=glb.t[:], scalar1=linit.t[:, 1:2], scalar2=None, op0=ALU.mult),
         reads=[glb.res, linit.res], writes=[glb.res])
    for r in (ident, biast, mtab, qrel, ones, ones0, epst, cwt, selt, hmt, neglam, ftab, glb):
        r.res.const = True

    layer_chunks = []
    wres = {}
    cast_jobs = []
    cast_state = {'n': 0}

    def emit_casts(upto):
        while cast_state['n'] < min(upto, len(cast_jobs)):
            ds_, dst_, src_, res_ = cast_jobs[cast_state['n']]
            S.dma('pool', ds_, dst_, src_, writes=[res_])
            cast_state['n'] += 1

    wds = S.dsem("wcast")
    for i in range(2):
        lst = []
        if i == 0:
            srcs = [(w_qkv[:, 0:D], "q"), (w_qkv[:, D:2 * D], "k"), (w_qkv[:, 2 * D:3 * D], "v"), (w_ao, "o")]
        else:
            srcs = [(w_ci[:, 2 * D:3 * D], "h"), (w_ci[:, D:2 * D], "c"), (w_ci[:, 0:D], "b"), (w_co, "o")]
        up = lambda j: (w_up[i, :, j * D:(j + 1) * D], f"u{j}")
        dn = lambda j: (w_dn[i, j * D:(j + 1) * D, :], f"d{j}")
        srcs += [up(0), up(1), dn(0), up(2), dn(1), up(3), dn(2), dn(3)]
        for src, nm in srcs:
            key = (i, nm)
            wscr[key] = dscr(f"w_{i}_{nm}", [128, 8, 1024])
            wres[key] = Res(f"w_{i}_{nm}")
            cast_jobs.append((S.dsem(f"wc_{i}_{nm}"), wscr[key], src.rearrange("(kc p) n -> p kc n", p=128), wres[key]))
            lst.append(key)
        layer_chunks.append(lst)

    steps = [("p", t) for t in range(n_psteps)] + [("s", j) for j in range(n_ssteps)]
    wseq = []
    for _ in steps:
        for i in range(2):
            wseq.extend(layer_chunks[i])
    wstate = {"loaded": 0, "use": 0}

    emit_casts(CAST_AHEAD)

    def wget():
        k = wstate["use"]
        wstate["use"] += 1
        emit_casts(k + 1 + CAST_AHEAD)
        while wstate["loaded"] < len(wseq) and wstate["loaded"] < k + NSLOT_W:
            j = wstate["loaded"]
            slot = ring[j % NSLOT_W]
            S.dma("sp", slot.ds, slot.t[:], wscr[wseq[j]], reads=[wres[wseq[j]]], writes=[slot.res])
            wstate["loaded"] += 1
        return ring[k % NSLOT_W]

    pro_blocks = []
    cv_jobs = []
    cv_state = {'n': 0}

    def emit_cv(count):
        while count > 0 and cv_state['n'] < len(cv_jobs):
            dst_, src_, res_ = cv_jobs[cv_state['n']]
            S.dma('pool', cvds, dst_, src_, writes=[res_])
            cv_state['n'] += 1
            count -= 1
        if cv_state['n'] == len(cv_jobs):
            for dst_, src_, res_ in cv_jobs:
                res_.w = (cvds.key, cvds.count)

    if n_ssteps:
        cvds = S.dsem("cvcast")
        for j in range(n_ssteps):
            for q4 in range(4):
                cv_jobs.append((V_s[j][:, q4 * 1024:(q4 + 1) * 1024, :].rearrange("h k e -> k h e"),
                                cv[j, q4 * 1024:(q4 + 1) * 1024, :].rearrange("k (h e) -> k h e", h=NH),
                                sres(("V", "s", j, "hist"))))
        for j in range(n_ssteps):
            for kb in range(PAST // 128):
                pro_blocks.append((j, kb))
    class _V2:
        def __init__(self, t, res):
            self.t, self.res = t, res
    pl_bufs = [(_V2(Ap.t[:, 0:512], Ap.res), _V2(Ap.t[:, 512:1024], Ap.res)),
               (_V2(At.t[:, 0:512], At.res), _V2(At.t[:, 512:1024], At.res))]
    pl_ds = [(S.dsem("pl00"), S.dsem("pl01")), (S.dsem("pl10"), S.dsem("pl11"))]
    kts = [sb(f"kts{i}", [128, NH, 128], BF16, dma=True) for i in range(2)]
    pro_state = {"n": 0, "loaded": []}

    def pro_prefetch():
        while len(pro_state["loaded"]) < 2 and pro_state["n"] < len(pro_blocks):
            j, kb = pro_blocks[pro_state["n"]]
            par = pro_state["n"] % 2
            pro_state["n"] += 1
            for half in range(2):
                buf = pl_bufs[par][half]
                S.dma("sp", pl_ds[par][half], buf.t[:, :], ck[j, kb * 128:(kb + 1) * 128, half * 512:(half + 1) * 512],
                      writes=[buf.res])
            pro_state["loaded"].append((j, kb, par))

    def pro_consume():
        for (j, kb, par) in pro_state["loaded"]:
            kt = kts[par]
            for half in range(2):
                buf = pl_bufs[par][half]
                b = next_bank()

                def tr(e, buf=buf, b=b):
                    ins = None
                    for hh in range(4):
                        ins = e.transpose(b.t[:, hh * 128:(hh + 1) * 128], buf.t[:, hh * 128:(hh + 1) * 128], ident.t[:])
                    return ins
                S.op("pe", tr, reads=[buf.res, ident.res], writes=[b.res])
                if half == 0:
                    S.op("act", lambda e, b=b, half=half, kt=kt: e.activation(
                        out=kt.t[:, half * 4:(half + 1) * 4, :],
                        in_=b.t[:].rearrange("p (h k) -> p h k", h=4), func=AF.Copy),
                        reads=[b.res], writes=[kt.res])
                else:
                    S.op("dve", lambda e, b=b, half=half, kt=kt: e.tensor_copy(
                        kt.t[:, half * 4:(half + 1) * 4, :],
                        b.t[:].rearrange("p (h k) -> p h k", h=4)),
                        reads=[b.res], writes=[kt.res])
            S.dma("sp", kt.ds, KT_s[j][:, :, kb * 128:(kb + 1) * 128].rearrange("h p k -> p h k"),
                  kt.t[:, :, :], reads=[kt.res], writes=[sres(("KT", "s", j, "hist"))])
        pro_state["loaded"] = []

    def pro_flush():
        while pro_state["n"] < len(pro_blocks) or pro_state["loaded"]:
            pro_prefetch()
            pro_consume()

    def make_xT(T, rows, nsub):
        for kc in range(8):
            b = next_bank()

            def tr(e, b=b, kc=kc):
                ins = None
                for s in range(nsub):
                    ins = e.transpose(b.t[:, s * 128:s * 128 + rows], x_tm.t[0:rows, s, kc * 128:(kc + 1) * 128],
                                      ident.t[0:rows, 0:rows])
                return ins
            S.op("pe", tr, reads=xres[:nsub] + [ident.res], writes=[b.res])
            if kc % 2 == 0:
                S.op("act", lambda e, b=b, kc=kc: e.activation(out=xT.t[:, kc, 0:T], in_=b.t[:, 0:T], func=AF.Copy),
                     reads=[b.res], writes=[xTres[kc]])
            else:
                S.op("dve", lambda e, b=b, kc=kc: e.tensor_copy(xT.t[:, kc, 0:T], b.t[:, 0:T]),
                     reads=[b.res], writes=[xTres[kc]])

    gb_state = {"n": 0}

    def layer_norm(idx, T, rows, nsub):
        gb = gbt[gb_state["n"] % 2]
        gb_state["n"] += 1
        S.dma("sp", gb.ds, gb.t[:, 0, :], ln_g[idx].partition_broadcast(128), writes=[gb.res])
        S.dma("sp", gb.ds, gb.t[:, 1, :], ln_b[idx].partition_broadcast(128), writes=[gb.res])
        for s in range(nsub):
            for hf in range(2):
                S.op("dve", lambda e, s=s, hf=hf: e.bn_stats(out=stt.t[0:rows, s, hf * 6:(hf + 1) * 6],
                                                            in_=x_tm.t[0:rows, s, hf * 512:(hf + 1) * 512]),
                     reads=[xres[s]], writes=[stt.res])
        for s in range(nsub):
            S.op("dve", lambda e, s=s: e.bn_aggr(out=mvt.t[0:rows, s, :], in_=stt.t[0:rows, s, :]),
                 reads=[stt.res], writes=[mvt.res])
        S.op("act", lambda e: e.activation(out=rs4.t[0:rows, 0:nsub, :], in_=mvt.t[0:rows, 0:nsub, 1:2], func=AF.Ln,
                                           bias=epst.t[0:rows, 0:1]),
             reads=[mvt.res, epst.res], writes=[rs4.res])
        S.op("act", lambda e: e.activation(out=rs4.t[0:rows, 0:nsub, :], in_=rs4.t[0:rows, 0:nsub, :], func=AF.Exp, scale=-0.5),
             reads=[rs4.res], writes=[rs4.res])
        S.op("dve", lambda e: e.scalar_tensor_tensor(out=nm4.t[0:rows, 0:nsub, :], in0=mvt.t[0:rows, 0:nsub, 0:1], scalar=-1.0,
                                                     in1=rs4.t[0:rows, 0:nsub, :], op0=ALU.mult, op1=ALU.mult),
             reads=[mvt.res, rs4.res], writes=[nm4.res])
        for s in range(nsub):
            S.op("act", lambda e, s=s: e.activation(out=x_tm.t[0:rows, s, :], in_=x_tm.t[0:rows, s, :], func=AF.Identity,
                                                    scale=rs4.t[0:rows, s, :], bias=nm4.t[0:rows, s, :]),
                 reads=[xres[s], rs4.res, nm4.res], writes=[xres[s]])
        for s in range(nsub):
            S.op("dve", lambda e, s=s, gb=gb: e.tensor_tensor(out=x_tm.t[0:rows, s, :], in0=x_tm.t[0:rows, s, :],
                                                             in1=gb.t[0:rows, 0, :], op=ALU.mult),
                 reads=[xres[s], gb.res], writes=[xres[s]])
            eng = "dve" if s % 2 == 0 else "pool"
            for hf in range(2):
                S.op(eng, lambda e, s=s, gb=gb, hf=hf: e.tensor_tensor(
                    out=x_tm.t[0:rows, s, hf * 512:(hf + 1) * 512], in0=x_tm.t[0:rows, s, hf * 512:(hf + 1) * 512],
                    in1=gb.t[0:rows, 1, hf * 512:(hf + 1) * 512], op=ALU.add),
                    reads=[xres[s], gb.res], writes=[xres[s]])

    def out_proj_residual(srcT, srcres, W, T, rows, nsub, first=True):
        for s in range(nsub):
            for n in range(2):
                b = next_bank()

                def mm(e, b=b, s=s, n=n):
                    ins = None
                    for k in range(8):
                        ins = e.matmul(b.t[0:rows, :], lhsT=srcT.t[:, k, s * 128:s * 128 + rows],
                                       rhs=W.t[:, k, n * 512:(n + 1) * 512], start=(k == 0), stop=(k == 7))
                    return ins
                S.op("pe", mm, reads=list(srcres) + [W.res], writes=[b.res])
                if first:
                    S.op("dve", lambda e, b=b, s=s, n=n: e.scalar_tensor_tensor(
                        out=x_tm.t[0:rows, s, n * 512:(n + 1) * 512], in0=x_tm.t[0:rows, s, n * 512:(n + 1) * 512],
                        scalar=ALPHA, in1=b.t[0:rows, :], op0=ALU.mult, op1=ALU.add),
                        reads=[b.res, xres[s]], writes=[xres[s]])
                else:
                    S.op("dve", lambda e, b=b, s=s, n=n: e.tensor_tensor(
                        out=x_tm.t[0:rows, s, n * 512:(n + 1) * 512], in0=x_tm.t[0:rows, s, n * 512:(n + 1) * 512],
                        in1=b.t[0:rows, :], op=ALU.add),
                        reads=[b.res, xres[s]], writes=[xres[s]])

    evac_rr = [0]

    def evac_copy(dst_ap, b, T, writes, src_ap=None):
        src = b.t[:, 0:T] if src_ap is None else src_ap
        evac_rr[0] += 1
        if evac_rr[0] % 2 == 0:
            S.op("act", lambda e: e.activation(out=dst_ap, in_=src, func=AF.Copy), reads=[b.res], writes=writes)
        else:
            S.op("dve", lambda e: e.tensor_copy(dst_ap, src), reads=[b.res], writes=writes)

    st_rr = [0]

    def proj_fm(W, T, dst_fn):
        xr = list(xTres)
        for h in range(NH):
            b = next_bank()

            def mm(e, b=b, h=h):
                ins = None
                for k in range(8):
                    ins = e.matmul(b.t[:, 0:T], lhsT=W.t[:, k, h * 128:(h + 1) * 128], rhs=xT.t[:, k, 0:T],
                                   start=(k == 0), stop=(k == 7))
                return ins
            S.op("pe", mm, reads=xr + [W.res], writes=[b.res])
            dst_fn(h, b)

    def attn_layer(step):
        kind, tidx = step
        if kind == "p":
            T, rows, nsub = TT, 128, 4
            t0 = tidx * TT
            nhist = t0
            KTd, Vd = KT_p, V_p
            kout = kp[t0:t0 + T, :]
            vout = vp[t0:t0 + T, :]
            own_key = ("p", tidx)
        else:
            T, rows, nsub = DEC, DEC, 1
            t0 = PAST
            nhist = PAST
            KTd, Vd = KT_s[tidx], V_s[tidx]
            kout = ks[tidx]
            vout = vs[tidx]
            own_key = ("s", tidx)
        xr = list(xTres)
        Wq = wget()

        def q_dst(h, b):
            S.op("act", lambda e: e.activation(out=QT[0].t[0:64, h, 0:T], in_=b.t[0:64, 0:T], func=AF.Copy),
                 reads=[b.res], writes=[QTres[h]])
            S.op("dve", lambda e: e.tensor_copy(QT[1].t[64:128, h, 0:T], b.t[64:128, 0:T]),
                 reads=[b.res], writes=[QTres[h]])
        proj_fm(Wq, T, q_dst)
        Wk = wget()
        proj_fm(Wk, T, lambda h, b: evac_copy(ktb.t[:, h, 0:T], b, T, [ktb.res]))
        ktres = sres(("KT",) + own_key)
        S.dma("sp", ktb.ds, KTd[:, :, t0:t0 + T].rearrange("h p k -> p h k"), ktb.t[:, :, 0:T],
              reads=[ktb.res], writes=[ktres])
        for s in range(nsub):
            for n in range(2):
                b = next_bank()

                def mm(e, b=b, s=s, n=n, W=Wk):
                    ins = None
                    for k in range(8):
                        ins = e.matmul(b.t[0:rows, :], lhsT=xT.t[:, k, s * 128:s * 128 + rows],
                                       rhs=W.t[:, k, n * 512:(n + 1) * 512], start=(k == 0), stop=(k == 7))
                    return ins
                S.op("pe", mm, reads=xr + [Wk.res], writes=[b.res])
                st = kst[st_rr[0] % 2]
                st_rr[0] += 1
                evac_copy(st.t[0:rows, :], b, 512, [st.res], src_ap=b.t[0:rows, :])
                S.dma("sp", st.ds, kout[s * 128:s * 128 + rows, n * 512:(n + 1) * 512], st.t[0:rows, :], reads=[st.res])
        Wv = wget()
        vres = sres(("V",) + own_key)
        for s in range(nsub):
            for n in range(2):
                b = next_bank()

                def mm(e, b=b, s=s, n=n, W=Wv):
                    ins = None
                    for k in range(8):
                        ins = e.matmul(b.t[0:rows, :], lhsT=xT.t[:, k, s * 128:s * 128 + rows],
                                       rhs=W.t[:, k, n * 512:(n + 1) * 512], start=(k == 0), stop=(k == 7))
                    return ins
                S.op("pe", mm, reads=xr + [Wv.res], writes=[b.res])
                st = vst[st_rr[0] % 2]
                vb = vbb[st_rr[0] % 2]
                st_rr[0] += 1
                S.op("act", lambda e, b=b, st=st: e.activation(out=st.t[0:rows, :], in_=b.t[0:rows, :], func=AF.Copy),
                     reads=[b.res], writes=[st.res])
                S.op("pool", lambda e, st=st, vb=vb: e.tensor_copy(vb.t[0:rows, :], st.t[0:rows, :]),
                     reads=[st.res], writes=[vb.res])
                S.dma("sp", st.ds, vout[s * 128:s * 128 + rows, n * 512:(n + 1) * 512], st.t[0:rows, :], reads=[st.res])
                S.dma("sp", vb.ds, Vd[n * 4:(n + 1) * 4, t0 + s * 128:t0 + s * 128 + rows, :].rearrange("h k e -> k h e"),
                      vb.t[0:rows, :].rearrange("k (h e) -> k h e", h=4), reads=[vb.res], writes=[vres])
        npast = nhist // 128
        keep = []
        for h in range(NH):
            kh = 0
            for r in range(1, npast + 1):
                if SLOPES[h] * (1 + (r - 1) * 128) <= ALIBI_THR:
                    kh = r
            keep.append(kh)
        loads = []
        head_chunks = []
        for h in range(NH):
            first_kb = npast - keep[h]
            chs = []
            if keep[h]:
                for c in range(first_kb // (KVC // 128), (npast + KVC // 128 - 1) // (KVC // 128)):
                    kb0 = max(c * (KVC // 128), first_kb)
                    kb1 = min((c + 1) * (KVC // 128), npast)
                    loads.append((h, "hist", kb0 * 128, (kb1 - kb0) * 128))
                    chs.append((kb0, kb1))
            head_chunks.append(chs)
            loads.append((h, "diag", t0, T))
        lstate = {"n": 0}

        def hist_res(kindKV, k0, n):
            if kind == "p":
                return [sres((kindKV, "p", tt)) for tt in range(k0 // TT, (k0 + n + TT - 1) // TT)]
            return [sres((kindKV, "s", tidx, "hist"))]

        def issue_load(idx):
            h, lk, k0, n = loads[idx]
            slot = idx % NSLOT_KV
            ktc, vc = KTc[slot], Vc[slot]
            if lk == "hist":
                rk, rv = hist_res("KT", k0, n), hist_res("V", k0, n)
            else:
                rk, rv = [ktres], [vres]
            S.dma("sp", ktc.ds, ktc.t[:, 0:n], KTd[h, :, k0:k0 + n], reads=rk, writes=[ktc.res])
            if n >= 128:
                S.dma("sp", vc.ds, vc.t[:, 0:n // 128, 0:128], Vd[h, k0:k0 + n, :].rearrange("(kb p) e -> p kb e", p=128),
                      reads=rv, writes=[vc.res])
            else:
                S.dma("sp", vc.ds, vc.t[0:n, 0, 0:128], Vd[h, k0:k0 + n, :], reads=rv, writes=[vc.res])

        def get_load(idx):
            while lstate["n"] < len(loads) and lstate["n"] < idx + NSLOT_KV - 1:
                issue_load(lstate["n"])
                lstate["n"] += 1
            slot = idx % NSLOT_KV
            return KTc[slot], Vc[slot]

        Sb = banks[0:4]

        blocks = []
        lidx = 0
        for h in range(NH):
            done = 0
            for (kb0, kb1) in head_chunks[h]:
                for kabs in range(kb0, kb1):
                    blocks.append(dict(h=h, typ="past", lidx=lidx, kb=kabs - kb0, kabs=kabs, first=(done == 0),
                                       last=(done == keep[h] - 1)))
                    done += 1
                lidx += 1
            ndb = (T + 127) // 128
            for j in range(ndb):
                blocks.append(dict(h=h, typ="diag", lidx=lidx, j=j, first=(j == 0), last=(j == ndb - 1), haspast=bool(keep[h])))
            lidx += 1

        def emit_qk(i, bl):
            ktc, vc = get_load(bl["lidx"])
            bl["ktc"], bl["vc"] = ktc, vc
            par = i % 2
            sb0, sb1 = Sb[par * 2], Sb[par * 2 + 1]
            bl["sb"] = (sb0, sb1)
            bl["p"] = (Pt[par * 2], Pt[par * 2 + 1])
            h = bl["h"]
            if bl["typ"] == "past":
                kb = bl["kb"]

                def qk(e):
                    e.matmul(sb0.t[:, 0:T], lhsT=ktc.t[:, kb * 128:(kb + 1) * 128], rhs=QT[0].t[:, h, 0:T],
                             start=True, stop=True)
                    return e.matmul(sb1.t[:, 0:T], lhsT=ktc.t[:, kb * 128:(kb + 1) * 128], rhs=QT[1].t[:, h, 0:T],
                                    start=True, stop=True)
            else:
                j = bl["j"]
                nk = min(128, T - j * 128)
                q0 = j * 128

                def qk(e):
                    e.matmul(sb0.t[0:nk, q0:T], lhsT=ktc.t[:, j * 128:j * 128 + nk], rhs=QT[0].t[:, h, q0:T],
                             start=True, stop=True)
                    return e.matmul(sb1.t[0:nk, q0:T], lhsT=ktc.t[:, j * 128:j * 128 + nk], rhs=QT[1].t[:, h, q0:T],
                                    start=True, stop=True)
            S.op("pe", qk, reads=[ktc.res, QTres[h]], writes=[sb0.res, sb1.res])

        def emit_exp(bl):
            h = bl["h"]
            slope = SLOPES[h]
            if bl["typ"] == "past":
                rel = bl["kabs"] - npast + 64
                for (sbx, px) in zip(bl["sb"], bl["p"]):
                    S.op("act", lambda e, sbx=sbx, px=px: e.activation(
                        out=px.t[:, 0:T], in_=sbx.t[:, 0:T], func=AF.Exp,
                        bias=biast.t[:, h * 64 + rel:h * 64 + rel + 1], scale=0.125),
                        reads=[sbx.res, biast.res], writes=[px.res])
            else:
                j = bl["j"]
                nk = min(128, T - j * 128)
                q0 = j * 128
                for ci, (sbx, px) in enumerate(zip(bl["sb"], bl["p"])):
                    dt_ = dtmp[ci]
                    S.op("dve", lambda e, sbx=sbx, dt_=dt_: e.scalar_tensor_tensor(
                        out=dt_.t[0:nk, q0:T], in0=mtab.t[0:nk, j * TT + q0:j * TT + T], scalar=slope,
                        in1=sbx.t[0:nk, q0:T], op0=ALU.mult, op1=ALU.add),
                        reads=[sbx.res, mtab.res], writes=[dt_.res])
                    S.op("act", lambda e, px=px, dt_=dt_: e.activation(
                        out=px.t[0:nk, q0:T], in_=dt_.t[0:nk, q0:T], func=AF.Exp, scale=0.125),
                        reads=[dt_.res], writes=[px.res])

        nqs = nsub
        OB = banks[4:7]
        TB = banks[7]

        def otile(t):
            return OB[t // 3], (t % 3) * 160

        def apv(buf, t):
            return buf.t[0:rows, t * VW:(t + 1) * VW]

        bank_started = {}

        def emit_pv(bl):
            vc = bl["vc"]
            pp = bl["p"]
            first, last = bl["first"], bl["last"]
            if bl["typ"] == "past":
                kb = bl["kb"]
                nk, qs0 = 128, 0
                if kind == "p" and bl["kabs"] < TT // 128:
                    pass
            else:
                kb = bl["j"]
                nk = min(128, T - kb * 128)
                qs0 = kb
            if first:
                bank_started.clear()
            use0 = (bl["typ"] == "past" and kind == "p" and bl["kabs"] < TT // 128)
            plan = []
            for c in range(2):
                for qs in range(qs0, nqs):
                    t = c * 4 + qs
                    bk, off = otile(t)
                    st = bk.name_ not in bank_started
                    bank_started[bk.name_] = True
                    plan.append((c, qs, bk, off, st))

            def pv(e):
                ins = None
                for (c, qs, bk, off, st) in plan:
                    q0 = qs * 128
                    qn = min(128, T - q0)
                    if use0:
                        ins = e.matmul(bk.t[0:qn, off:off + 128], lhsT=pp[c].t[0:nk, q0:q0 + qn], rhs=vc.t[0:nk, kb, 0:128],
                                       start=st, stop=last, skip_group_check=True)
                        ins = e.matmul(bk.t[0:qn, off + 128:off + VW], lhsT=pp[c].t[0:nk, q0:q0 + qn], rhs=ones0.t[0:nk, 0:1],
                                       start=False, stop=last, skip_group_check=True)
                    else:
                        ins = e.matmul(bk.t[0:qn, off:off + VW], lhsT=pp[c].t[0:nk, q0:q0 + qn], rhs=vc.t[0:nk, kb, :],
                                       start=st, stop=last, skip_group_check=True)
                return ins
            S.op("pe", pv, reads=[vc.res, pp[0].res, pp[1].res, ones0.res], writes=[bk.res for bk in OB])

        def emit_past_evac(h):
            for t in range(8):
                c, qs = t // 4, t % 4
                if qs >= nqs:
                    continue
                bk, off = otile(t)
                eng = "dve" if (t // 3) != 1 else "act"
                fcol = ftab.t[0:rows, h * 4 + qs:h * 4 + qs + 1]
                if eng == "dve":
                    S.op("dve", lambda e, t=t, bk=bk, off=off, fcol=fcol: e.tensor_scalar(
                        out=apv(Ap, t), in0=bk.t[0:rows, off:off + VW], scalar1=fcol, scalar2=None, op0=ALU.mult),
                        reads=[bk.res, ftab.res], writes=[Ap.res])
                else:
                    S.op("act", lambda e, t=t, bk=bk, off=off, fcol=fcol: e.activation(
                        out=apv(Ap, t), in_=bk.t[0:rows, off:off + VW], func=AF.Copy, scale=fcol),
                        reads=[bk.res, ftab.res], writes=[Ap.res])

        def emit_finalize(h, haspast):
            for t in range(8):
                c, qs = t // 4, t % 4
                if qs >= nqs:
                    continue
                bk, off = otile(t)
                if haspast:
                    S.op("dve", lambda e, t=t, bk=bk, off=off: e.tensor_tensor(
                        out=apv(At, t), in0=bk.t[0:rows, off:off + VW], in1=apv(Ap, t), op=ALU.add),
                        reads=[bk.res, Ap.res], writes=[At.res])
                elif (t // 3) != 1:
                    S.op("dve", lambda e, t=t, bk=bk, off=off: e.tensor_copy(apv(At, t), bk.t[0:rows, off:off + VW]),
                         reads=[bk.res], writes=[At.res])
                else:
                    S.op("act", lambda e, t=t, bk=bk, off=off: e.activation(out=apv(At, t), in_=bk.t[0:rows, off:off + VW],
                                                                           func=AF.Copy),
                         reads=[bk.res], writes=[At.res])
            Atv = At.t[0:rows, :].rearrange("p (t w) -> p t w", w=VW)
            S.op("dve", lambda e: e.reciprocal(rr.t[0:rows, :, :], Atv[:, :, 128:129]), reads=[At.res], writes=[rr.res])
            S.op("dve", lambda e: e.tensor_scalar(out=rr.t[0:rows, 4:8, :], in0=rr.t[0:rows, 4:8, :],
                                                  scalar1=neglam.t[0:rows, 0:1], scalar2=None, op0=ALU.mult),
                 reads=[rr.res, neglam.res], writes=[rr.res])
            for qs in range(nqs):
                S.op("pool", lambda e, qs=qs: e.tensor_scalar(out=Otm.t[0:rows, qs, :], in0=Atv[:, qs, 0:128],
                                                             scalar1=rr.t[0:rows, qs, :], scalar2=None, op0=ALU.mult),
                     reads=[At.res, rr.res], writes=[Otm.res])
            for qs in range(nqs):
                S.op("dve", lambda e, qs=qs: e.scalar_tensor_tensor(out=Otm.t[0:rows, qs, :], in0=Atv[:, 4 + qs, 0:128],
                                                                   scalar=rr.t[0:rows, 4 + qs, :], in1=Otm.t[0:rows, qs, :],
                                                                   op0=ALU.mult, op1=ALU.add),
                     reads=[At.res, rr.res, Otm.res], writes=[Otm.res])
            for qs in range(nqs):
                S.op("act", lambda e, qs=qs: e.activation(out=Onf.t[0:rows, qs, :], in_=Otm.t[0:rows, qs, :], func=AF.Square,
                                                         accum_out=ss4.t[0:rows, qs, :]),
                     reads=[Otm.res], writes=[Onf.res, ss4.res])
            S.op("act", lambda e: e.activation(out=ss4.t[0:rows, 0:nqs, :], in_=ss4.t[0:rows, 0:nqs, :], func=AF.Ln,
                                               scale=1.0 / 128.0, bias=epst.t[0:rows, 0:1]),
                 reads=[ss4.res, epst.res], writes=[ss4.res])
            S.op("act", lambda e: e.activation(out=ss4.t[0:rows, 0:nqs, :], in_=ss4.t[0:rows, 0:nqs, :], func=AF.Exp, scale=-0.5),
                 reads=[ss4.res], writes=[ss4.res])
            for qs in range(nqs):
                S.op("dve", lambda e, qs=qs: e.scalar_tensor_tensor(out=Onf.t[0:rows, qs, :], in0=Otm.t[0:rows, qs, :],
                                                                   scalar=ss4.t[0:rows, qs, :], in1=glb.t[0:rows, :],
                                                                   op0=ALU.mult, op1=ALU.mult),
                     reads=[Otm.res, ss4.res, glb.res, Onf.res], writes=[Onf.res])

            def tr(e):
                ins = None
                for qs in range(nqs):
                    ins = e.transpose(TB.t[:, qs * 128:qs * 128 + rows], Onf.t[0:rows, qs, :], ident.t[0:rows, 0:rows])
                return ins
            S.op("pe", tr, reads=[Onf.res, ident.res], writes=[TB.res])
            if h % 2 == 0:
                S.op("act", lambda e: e.activation(out=OnT.t[:, h, 0:T], in_=TB.t[:, 0:T], func=AF.Copy),
                     reads=[TB.res], writes=[OnTres[h]])
            else:
                S.op("dve", lambda e: e.tensor_copy(OnT.t[:, h, 0:T], TB.t[:, 0:T]), reads=[TB.res], writes=[OnTres[h]])

        nb = len(blocks)
        emit_qk(0, blocks[0])
        for i, bl in enumerate(blocks):
            if i + 1 < nb:
                emit_qk(i + 1, blocks[i + 1])
            emit_exp(bl)
            emit_pv(bl)
            if bl["typ"] == "past" and bl["last"]:
                emit_past_evac(bl["h"])
            if bl["typ"] == "diag" and bl["last"]:
                emit_finalize(bl["h"], bl["haspast"])
        S.stage(6)
        Wo = wget()
        out_proj_residual(OnT, OnTres, Wo, T, rows, nsub, first=True)
        S.stage(7)

    def conv_layer(step):
        kind, tidx = step
        if kind == "p":
            T, rows, nsub = TT, 128, 4
        else:
            T, rows, nsub = DEC, DEC, 1
        hl = halo
        if kind == "s":
            for j_ in range(2):
                S.dma("sp", hl.ds, hl.t[:, :, j_], sc[tidx, j_].rearrange("(dc p) -> p dc", p=128), writes=[hl.res],
                      allow_slow_non_contiguous=True)
            S.op("pool", lambda e: e.tensor_copy(uext.t[:, :, 0:2], hl.t[:]), reads=[hl.res], writes=ures)
        else:
            S.op("pool", lambda e: e.tensor_scalar(out=uext.t[:, :, 0:2], in0=hl.t[:], scalar1=hmt.t[:, tidx:tidx + 1],
                                                   scalar2=None, op0=ALU.mult),
                 reads=[hl.res, hmt.res], writes=ures)
        Wh = wget()
        proj_fm(Wh, T, lambda dc, b: S.op("act", lambda e: e.activation(out=uext.t[:, dc, 2:2 + T], in_=b.t[:, 0:T],
                                                                         func=AF.Copy),
                                          reads=[b.res], writes=[ures[dc]]))
        Wc = wget()
        proj_fm(Wc, T, lambda dc, b: S.op("dve", lambda e: e.tensor_tensor(out=uext.t[:, dc, 2:2 + T], in0=b.t[:, 0:T],
                                                                            in1=uext.t[:, dc, 2:2 + T], op=ALU.mult),
                                          reads=[b.res, ures[dc]], writes=[ures[dc]]))
        S.op("pool", lambda e: e.tensor_copy(hl.t[:], uext.t[:, :, T:T + 2]), reads=ures, writes=[hl.res])
        dsts = []
        if kind == "p" and tidx == SEQ // TT - 1:
            dsts.append(cpo[0])
        if kind == "p" and tidx == SEQ // TT:
            dsts.append(cpo[1])
        if kind == "s":
            dsts.append(cs[tidx])
        for dst in dsts:
            for j_ in range(2):
                S.dma("sp", hl.ds, dst[j_].rearrange("(dc p) -> p dc", p=128), hl.t[:, :, j_], reads=[hl.res],
                      allow_slow_non_contiguous=True)
        Wb = wget()

        def b_dst(dc, b):
            z = zc[dc % 2]
            S.op("act", lambda e: e.activation(out=z.t[:, 0:T], in_=uext.t[:, dc, 0:T], func=AF.Copy,
                                               scale=cwt.t[:, 0, dc:dc + 1]),
                 reads=[ures[dc], cwt.res], writes=[z.res])
            for jj in (1, 2):
                S.op("dve", lambda e, jj=jj: e.scalar_tensor_tensor(
                    out=z.t[:, 0:T], in0=uext.t[:, dc, jj:jj + T], scalar=cwt.t[:, jj, dc:dc + 1],
                    in1=z.t[:, 0:T], op0=ALU.mult, op1=ALU.add),
                    reads=[ures[dc], cwt.res, z.res], writes=[z.res])
            S.op("dve", lambda e: e.tensor_tensor(out=OnT.t[:, dc, 0:T], in0=b.t[:, 0:T], in1=z.t[:, 0:T], op=ALU.mult),
                 reads=[b.res, z.res], writes=[OnTres[dc]])
        proj_fm(Wb, T, b_dst)
        Wo = wget()
        out_proj_residual(OnT, OnTres, Wo, T, rows, nsub, first=True)

    def mlp_layer(step):
        kind, tidx = step
        if kind == "p":
            T, rows, nsub = TT, 128, 4
        else:
            T, rows, nsub = DEC, DEC, 1

        def up(j):
            Wu = wget()
            hb = hT[j % 2]

            def dst(fc, b):
                rb = rbuf[fc % 2]
                hres = OnTres[fc] if hb is OnT else ktb.res
                S.op("act", lambda e: e.activation(out=rb.t[:, 0:T], in_=b.t[:, 0:T], func=AF.Relu),
                     reads=[b.res], writes=[rb.res])
                S.op("pool", lambda e: e.tensor_tensor(out=hb.t[:, fc, 0:T], in0=rb.t[:, 0:T], in1=rb.t[:, 0:T],
                                                       op=ALU.mult),
                     reads=[rb.res], writes=[hres])
            proj_fm(Wu, T, dst)

        def down(j):
            Wd = wget()
            hb = hT[j % 2]
            out_proj_residual(hb, OnTres if hb is OnT else [ktb.res], Wd, T, rows, nsub, first=(j == 0))
        up(0); up(1); down(0); up(2); down(1); up(3); down(2); down(3)

    xds = x_tm.ds
    ccds = S.dsem("cc")
    for si, step in enumerate(steps):
        kind, tidx = step
        if kind == "p":
            T, rows, nsub = TT, 128, 4
            src = xp[tidx * TT:(tidx + 1) * TT, :].rearrange("(s p) d -> p s d", p=128)
            S.dma("sp", xds, x_tm.t[:, :, :], src, writes=xres)
        else:
            T, rows, nsub = DEC, DEC, 1
            S.dma("sp", xds, x_tm.t[0:DEC, 0, :], xs[tidx], writes=xres)
        if exchange and si > 0:
            if kind == "p":
                S.dma("sp", uext.ds, rview, recv[0:TT, :].rearrange("(s p) d -> p s d", p=128), reads=[recv_res], writes=ures)
            else:
                S.dma("sp", uext.ds, rview[0:DEC, 0, :], recv[0:DEC, :], reads=[recv_res], writes=ures)
            for s in range(nsub):
                S.op("dve", lambda e, s=s, rows=rows: e.scalar_tensor_tensor(
                    out=x_tm.t[0:rows, s, :], in0=rview[0:rows, s, :], scalar=selt.t[0:rows, 0:1],
                    in1=x_tm.t[0:rows, s, :], op0=ALU.mult, op1=ALU.add),
                    reads=ures + [xres[s], selt.res], writes=[xres[s]])
        make_xT(T, rows, nsub)
        S.stage(3)
        pro = (kind == "p")
        attn_layer(step)
        if pro:
            pro_prefetch()
        layer_norm(0, T, rows, nsub)
        make_xT(T, rows, nsub)
        S.stage(8)
        mlp_layer(step)
        S.stage(9)
        if pro:
            pro_consume()
            pro_prefetch()
        layer_norm(1, T, rows, nsub)
        make_xT(T, rows, nsub)
        conv_layer(step)
        if pro:
            pro_consume()
            pro_prefetch()
        layer_norm(2, T, rows, nsub)
        make_xT(T, rows, nsub)
        mlp_layer(step)
        if pro:
            pro_consume()
        layer_norm(3, T, rows, nsub)
        if kind == "p":
            S.dma("sp", xds, yp[tidx * TT:(tidx + 1) * TT, :].rearrange("(s p) d -> p s d", p=128), x_tm.t[:, :, :],
                  reads=xres)
        else:
            S.dma("sp", xds, ys[tidx], x_tm.t[0:DEC, 0, :], reads=xres)
        if exchange and si < len(steps) - 1:
            if kind == "p":
                S.dma("sp", xds, send.rearrange("(s p) d -> p s d", p=128), x_tm.t[:, :, :], reads=xres, writes=[send_res])
            else:
                S.dma("sp", xds, send[0:DEC, :], x_tm.t[0:DEC, 0, :], reads=xres, writes=[send_res])
            S._deps("pool", [send_res], [recv_res])
            ccds.count += 1
            ev = (ccds.key, ccds.count)
            S.streams["pool"].append(("op", lambda e: e.collective_compute(
                "AllGather", ALU.bypass, replica_groups=PAIRS, ins=[send], outs=[recv]), ev[0], 1))
            S._record(ev, [send_res], [recv_res])
        if kind == "p":
            emit_cv(1)
            if si + 1 < len(steps) and steps[si + 1][0] == "s":
                pro_flush()
                emit_cv(10 ** 6)
    S.emit()
    return nc, S


def _tables():
    biast = np.zeros((128, NH * 64), np.float32)
    p = np.arange(128, dtype=np.float64)
    for h in range(NH):
        for r in range(64):
            rel = r - 64
            biast[:, h * 64 + r] = (SLOPES[h] * (rel * 128 + p)).astype(np.float32)
    mt = np.zeros((128, 4 * TT), np.float32)
    q = np.arange(TT)
    for j in range(4):
        k = j * 128 + np.arange(128)
        vis = (k[:, None] // 64) <= (q[None, :] // 64)
        m = -np.abs(q[None, :] - k[:, None]).astype(np.float64)
        mt[:, j * TT:(j + 1) * TT] = (8.0 * np.where(vis, m, -1.0e5)).astype(np.float32)
    qrel = np.broadcast_to(np.arange(TT, dtype=np.float32)[None, :], (128, TT)).copy()
    return biast, mt, qrel


def _ftab():
    ft = np.zeros((128, NH * 4), np.float32)
    p = np.arange(128, dtype=np.float64)
    for h in range(NH):
        for qs in range(4):
            ft[:, h * 4 + qs] = np.exp(-SLOPES[h] * (qs * 128 + p)).astype(np.float32)
    return ft


_CACHE = {}


def make_in_maps(x_prompt, x_sample, cache_k, cache_v, state_conv, w_qkv, lambda_q1, lambda_k1,
                 lambda_q2, lambda_k2, g_subln, w_attn_out, w_conv_in, w_conv, w_conv_out,
                 w_up, w_down, ln_g, ln_b):
    f = lambda a: np.ascontiguousarray(np.asarray(a, dtype=np.float32))
    biast, mt, qrel = _tables()
    xp = np.asarray(x_prompt); xs = np.asarray(x_sample)
    ck = np.asarray(cache_k).reshape(2, 8, PAST, D); cv = np.asarray(cache_v).reshape(2, 8, PAST, D)
    sc = np.asarray(state_conv)
    lng = np.asarray(ln_g); lnb = np.asarray(ln_b)
    lam = np.stack([np.asarray(lambda_q1), np.asarray(lambda_k1), np.asarray(lambda_q2), np.asarray(lambda_k2)], axis=1)
    in_maps = []
    for c in range(8):
        p, r = c // 2, c % 2
        m = {"ident": np.eye(128, dtype=np.float32), "biastab": biast, "mtab": mt, "qrel": qrel, "ftab": _ftab()}
        xpc = np.zeros((NPS * TT, D), np.float32)
        xsc = np.zeros((NSS, DEC, D), np.float32)
        ckc = np.zeros((NSS, PAST, D), np.float32)
        cvc = np.zeros((NSS, PAST, D), np.float32)
        scc = np.zeros((NSS, 2, D), np.float32)
        if r == 0:
            xpc[:SEQ] = xp[p]
            xsc[0:2] = xs[2 * p:2 * p + 2]
        for j in range(2):
            ckc[j + r] = ck[r, 2 * p + j]
            cvc[j + r] = cv[r, 2 * p + j]
            scc[j + r] = sc[r, 2 * p + j]
        m.update(xp=xpc, xs=xsc, ck=ckc, cv=cvc, sc=scc)
        m["w_qkv"] = f(np.asarray(w_qkv)[r]); m["w_ao"] = f(np.asarray(w_attn_out)[r])
        m["w_ci"] = f(np.asarray(w_conv_in)[r]); m["w_cw"] = f(np.asarray(w_conv)[r]); m["w_co"] = f(np.asarray(w_conv_out)[r])
        m["w_up"] = f(np.asarray(w_up)[2 * r:2 * r + 2]); m["w_dn"] = f(np.asarray(w_down)[2 * r:2 * r + 2])
        m["lam4"] = f(lam[r]); m["gsub"] = f(np.asarray(g_subln)[r].reshape(128, 1))
        m["ln_g"] = f(lng[2 * r:2 * r + 2].reshape(4, D)); m["ln_b"] = f(lnb[2 * r:2 * r + 2].reshape(4, D))
        li0 = lam_init_of(2 * r)
        m["linit"] = np.tile(np.array([[-li0, 1.0 - li0]], np.float32), (128, 1))
        m["sel"] = np.full((128, 1), float(r), np.float32)
        hm = np.ones((128, NPS), np.float32)
        hm[:, 0] = 0.0
        if r == 1:
            hm[:, 1] = 0.0
        m["hm"] = hm
        m["ones0"] = np.full((128, 128), 1.0 - r, np.float32)
        in_maps.append(m)
    return in_maps


def assemble(res):
    yp = np.zeros((4, SEQ, D), np.float32); ys = np.zeros((8, DEC, D), np.float32)
    kp = np.zeros((2, 4, SEQ, D), np.float32); vp = np.zeros((2, 4, SEQ, D), np.float32)
    cp = np.zeros((2, 4, 2, D), np.float32)
    ks = np.zeros((2, 8, DEC, D), np.float32); vs = np.zeros((2, 8, DEC, D), np.float32)
    cs = np.zeros((2, 8, 2, D), np.float32)
    for p in range(4):
        a, b = res[2 * p], res[2 * p + 1]
        yp[p] = b["yp"][TT:TT + SEQ]
        kp[0, p] = a["kp"][0:SEQ]; kp[1, p] = b["kp"][TT:TT + SEQ]
        vp[0, p] = a["vp"][0:SEQ]; vp[1, p] = b["vp"][TT:TT + SEQ]
        cp[0, p] = a["cpo"][0]; cp[1, p] = b["cpo"][1]
        for j in range(2):
            ys[2 * p + j] = b["ys"][1 + j]
            ks[0, 2 * p + j] = a["ks"][j]; ks[1, 2 * p + j] = b["ks"][1 + j]
            vs[0, 2 * p + j] = a["vs"][j]; vs[1, 2 * p + j] = b["vs"][1 + j]
            cs[0, 2 * p + j] = a["cs"][j]; cs[1, 2 * p + j] = b["cs"][1 + j]
    return (yp, ys, kp.reshape(2, 4, SEQ, NH, 2, 64), vp.reshape(2, 4, SEQ, NH, 128), cp,
            ks.reshape(2, 8, DEC, NH, 2, 64), vs.reshape(2, 8, DEC, NH, 128), cs)


def kernel(**inputs):
    if "nc" not in _CACHE:
        _CACHE["nc"] = build_program()[0]
    nc = _CACHE["nc"]
    in_maps = make_in_maps(**inputs)
    res = run_bass_kernel_spmd(nc, in_maps, core_ids=list(range(8))).results
    return assemble(res)
```

```python
import math
import os
import numpy as np
import concourse.bass as bass
import concourse.mybir as mybir
from concourse.bass_utils import run_bass_kernel_spmd

F32 = mybir.dt.float32
BF16 = mybir.dt.bfloat16
ALU = mybir.AluOpType
AF = mybir.ActivationFunctionType

D = 1024
SEQ = 8192
NH = 8
DEPTH = 4
PAST = 4096
DEC = 16
TT = 512
ALPHA = (2.0 * DEPTH) ** 0.25
LN_EPS = 1e-5
SLOPES = [2.0 ** (-8.0 * (h + 1) / NH) for h in range(NH)]
NSLOT_W = 3
NSLOT_KV = 4
VW = 129
CAST_AHEAD = 5
ALIBI_THR = 125.0
KVC = 1024


def lam_init_of(i):
    return 0.8 - 0.6 * math.exp(-0.3 * i)


ENGS = ("pe", "act", "dve", "pool", "sp")


class Res:
    __slots__ = ("name", "w", "r", "const")

    def __init__(self, name):
        self.name = name
        self.w = None
        self.r = {}
        self.const = False


class DSem:
    __slots__ = ("key", "count")

    def __init__(self, key):
        self.key = key
        self.count = 0


class Sched:
    def __init__(self, nc):
        self.nc = nc
        self.streams = {e: [] for e in ENGS}
        self.sems = {}
        self.ecnt = {e: 0 for e in ENGS}
        self.known = {e: {} for e in ENGS}
        self.selfsync = {"pe": False, "act": True, "dve": True, "pool": True, "sp": True}
        for e in ENGS:
            self._sem("E_" + e)
        self.dsems = []
        self.n_ops = 0
        self.dead = False
        self.stop = int(os.environ.get('K_STOP', '99'))

    def _sem(self, key):
        if key not in self.sems:
            self.sems[key] = self.nc.alloc_semaphore(name=key)
        return key

    def dsem(self, name):
        d = DSem(self._sem("D_" + name))
        self.dsems.append(d)
        return d

    def _wait(self, eng, ev):
        if ev is None:
            return
        key, val = ev
        if key == "E_" + eng and not self.selfsync[eng]:
            return
        k = self.known[eng]
        if k.get(key, 0) >= val:
            return
        k[key] = val
        self.streams[eng].append(("wait", key, val))

    def _deps(self, eng, reads, writes):
        for r in reads:
            self._wait(eng, r.w)
        for w in writes:
            self._wait(eng, w.w)
            for key, val in w.r.items():
                self._wait(eng, (key, val))

    def _record(self, ev, reads, writes):
        for r in reads:
            if r.const:
                continue
            if r.r.get(ev[0], 0) < ev[1]:
                r.r[ev[0]] = ev[1]
        for w in writes:
            w.w = ev
            w.r = {}

    def stage(self, n):
        if n >= self.stop:
            self.dead = True

    def op(self, eng, fn, reads=(), writes=()):
        if self.dead:
            return None
        self._deps(eng, reads, writes)
        self.ecnt[eng] += 1
        ev = ("E_" + eng, self.ecnt[eng])
        self.streams[eng].append(("op", fn, ev[0], 1))
        self._record(ev, reads, writes)
        self.n_ops += 1
        return ev

    def dma(self, eng, dsem, out, in_, reads=(), writes=(), **kw):
        if self.dead:
            return None
        self._deps(eng, reads, writes)
        dsem.count += 16
        ev = (dsem.key, dsem.count)

        def fn(e, out=out, in_=in_, kw=kw):
            return e.dma_start(out=out, in_=in_, **kw)
        self.streams[eng].append(("op", fn, ev[0], 16))
        self._record(ev, reads, writes)
        self.n_ops += 1
        return ev

    def finish(self):
        for d in self.dsems:
            if d.count:
                self._wait("sp", (d.key, d.count))
        for e in ENGS:
            if e != "sp" and self.ecnt[e]:
                self._wait("sp", ("E_" + e, self.ecnt[e]))

    def _replay(self, eng, e):
        for item in self.streams[eng]:
            if item[0] == "wait":
                e.wait_ge(self.sems[item[1]], item[2])
            else:
                _, fn, key, amt = item
                fn(e).then_inc(self.sems[key], amt)

    def emit(self):
        self.finish()
        with self.nc.Block() as block:
            @block.tensor
            def _(e):
                self._replay("pe", e)

            @block.scalar
            def _(e):
                self._replay("act", e)

            @block.vector
            def _(e):
                self._replay("dve", e)

            @block.gpsimd
            def _(e):
                self._replay("pool", e)

            @block.sync
            def _(e):
                self._replay("sp", e)


class Buf:
    def __init__(self, S, t, name, dma=False):
        self.t = t
        self.name_ = name
        self.res = Res(name)
        self.ds = S.dsem(name) if dma else None


NPS = SEQ // TT + 1
NSS = 3
PAIRS = [[0, 1], [2, 3], [4, 5], [6, 7]]


def build_program(n_psteps=NPS, n_ssteps=NSS, exchange=True):
    nc = bass.Bass("TRN2", target_bir_lowering=False)
    S = Sched(nc)

    def din(name, shape, dt=F32):
        return nc.dram_tensor(name, list(shape), dt, kind="ExternalInput").ap()

    def dout(name, shape, dt=F32):
        return nc.dram_tensor(name, list(shape), dt, kind="ExternalOutput").ap()

    def dscr(name, shape, dt=BF16):
        return nc.dram_tensor(name, list(shape), dt).ap()

    NROW = NPS * TT
    xp = din("xp", [NROW, D])
    xs = din("xs", [NSS, DEC, D])
    ck = din("ck", [NSS, PAST, D])
    cv = din("cv", [NSS, PAST, D])
    sc = din("sc", [NSS, 2, D])
    w_qkv = din("w_qkv", [D, 3 * D])
    w_ao = din("w_ao", [D, D])
    w_ci = din("w_ci", [D, 3 * D])
    w_cw = din("w_cw", [3, D])
    w_co = din("w_co", [D, D])
    w_up = din("w_up", [2, D, 4 * D])
    w_dn = din("w_dn", [2, 4 * D, D])
    lam4 = din("lam4", [4, 64])
    gsub = din("gsub", [128, 1])
    ln_g = din("ln_g", [4, D])
    ln_b = din("ln_b", [4, D])
    linit_d = din("linit", [128, 2])
    sel_d = din("sel", [128, 1])
    hm_d = din("hm", [128, NPS])
    ones0_d = din("ones0", [128, 128])
    ident_d = din("ident", [128, 128])
    bias_d = din("biastab", [128, NH * 64])
    mtab_d = din("mtab", [128, 4 * TT])
    qrel_d = din("qrel", [128, TT])
    ftab_d = din("ftab", [128, NH * 4])

    yp = dout("yp", [NROW, D])
    ys = dout("ys", [NSS, DEC, D])
    kp = dout("kp", [NROW, D])
    vp = dout("vp", [NROW, D])
    cpo = dout("cpo", [2, 2, D])
    ks = dout("ks", [NSS, DEC, D])
    vs = dout("vs", [NSS, DEC, D])
    cs = dout("cs", [NSS, 2, D])

    wscr = {}
    KT_p = dscr("KT_p", [NH, 128, NROW])
    V_p = dscr("V_p", [NH, NROW, 128])
    NKS = PAST + 128
    KT_s = [dscr(f"KT_s{j}", [NH, 128, NKS]) for j in range(NSS)]
    V_s = [dscr(f"V_s{j}", [NH, NKS, 128]) for j in range(NSS)]
    send = dscr("send", [TT, D], F32)
    recv = dscr("recv", [2 * TT, D], F32)
    send_res, recv_res = Res("send"), Res("recv")
    scr_res = {}

    def sres(key):
        if key not in scr_res:
            scr_res[key] = Res("scr" + str(key))
        return scr_res[key]

    A = nc.alloc_sbuf_tensor

    def sb(name, shape, dt=F32, dma=False):
        return Buf(S, A("sb_" + name, list(shape), dt), name, dma)

    x_tm = sb("x_tm", [128, 4, D], F32, dma=True)
    xres = [Res(f"x_tm{s}") for s in range(4)]
    xT = sb("xT", [128, 8, TT], BF16)
    xTres = [Res(f"xT{k}") for k in range(8)]
    ring = [sb(f"wring{i}", [128, 8, 1024], BF16, dma=True) for i in range(NSLOT_W)]
    QT = [sb(f"QT{c}", [128, NH, TT], BF16) for c in range(2)]
    QTres = [Res(f"QTh{h}") for h in range(NH)]
    KTc = [sb(f"KTc{i}", [128, KVC], BF16, dma=True) for i in range(NSLOT_KV)]
    Vc = [sb(f"Vc{i}", [128, KVC // 128, VW], BF16, dma=True) for i in range(NSLOT_KV)]
    Pt = [sb(f"P{i}", [128, TT], BF16) for i in range(4)]
    OnT = sb("OnT", [128, NH, TT], BF16)
    OnTres = [Res(f"OnT{h}") for h in range(NH)]
    Ap = sb("Ap", [128, 8 * VW], F32)
    At = sb("At", [128, 8 * VW], F32)
    Onf = sb("Onf", [128, 4, 128], F32)
    Otm = sb("Otm", [128, 4, 128], F32)
    rr = sb("rr", [128, 8, 1], F32)
    ss4 = sb("ss4", [128, 4, 1], F32)
    glb = sb("glb", [128, 128], F32, dma=True)
    ftab = sb("ftab", [128, NH * 4], F32, dma=True)
    tmpf = [sb(f"tmpf{i}", [128, TT], F32) for i in range(2)]
    dtmp = tmpf
    rbuf = tmpf
    zc = tmpf
    ktb = sb("ktb", [128, NH, TT], BF16, dma=True)
    hT = [OnT, ktb]
    biast = sb("biast", [128, NH * 64], F32, dma=True)
    mtab = sb("mtab", [128, 4 * TT], F32, dma=True)
    qrel = sb("qrel", [128, TT], F32, dma=True)
    ident = sb("ident", [128, 128], F32, dma=True)
    ones = sb("ones", [128, 128], BF16)
    ones0f = sb("ones0f", [128, 128], F32, dma=True)
    ones0 = sb("ones0", [128, 128], BF16)
    epst = sb("epst", [128, 1], F32)
    gbt = [sb(f"gb{i}", [128, 2, D], F32, dma=True) for i in range(2)]
    kst = [sb(f"kst{i}", [128, 512], F32, dma=True) for i in range(2)]
    vst = [sb(f"vst{i}", [128, 512], F32, dma=True) for i in range(2)]
    vbb = [sb(f"vbb{i}", [128, 512], BF16, dma=True) for i in range(2)]
    uext = sb("uext", [128, 8, TT + 4], F32, dma=True)
    ures = [Res(f"uext{k}") for k in range(8)]
    halo = sb("halo", [128, 8, 2], F32, dma=True)
    cwt = sb("cwt", [128, 3, 8], F32, dma=True)
    lamt = sb("lamt", [128, 4, 64], F32, dma=True)
    lamw = sb("lamw", [128, 2, 64], F32)
    lams = sb("lams", [128, 2], F32)
    neglam = sb("neglam", [128, 1], F32)
    glt = sb("glt", [128, 1], F32, dma=True)
    linit = sb("linit", [128, 2], F32, dma=True)
    selt = sb("selt", [128, 1], F32, dma=True)
    hmt = sb("hmt", [128, NPS], F32, dma=True)
    stt = sb("stt", [128, 4, 12], F32)
    mvt = sb("mvt", [128, 4, 2], F32)
    rs4 = sb("rs4", [128, 4, 1], F32)
    nm4 = sb("nm4", [128, 4, 1], F32)
    print("sbuf bytes remaining:", nc.sbuf_bytes_remaining)

    class _View:
        pass
    cache_ld = []
    for i in range(2):
        v = _View()
        v.t = x_tm.t[:, i, :]
        v.res = xres[i]
        v.ds = S.dsem(f"cld{i}")
        cache_ld.append(v)
    rview = uext.t[:].rearrange("p a b -> p (a b)")[:, 0:4 * D].rearrange("p (s d) -> p s d", s=4)

    banks = [Buf(S, nc.alloc_psum_tensor(f"bank{i}", [128, 512], F32), f"bank{i}") for i in range(8)]
    bank_rr = [0]

    def next_bank():
        b = banks[bank_rr[0] % 8]
        bank_rr[0] += 1
        return b

    S.dma("sp", ident.ds, ident.t[:], ident_d, writes=[ident.res])
    S.dma("sp", biast.ds, biast.t[:], bias_d, writes=[biast.res])
    S.dma("sp", mtab.ds, mtab.t[:], mtab_d, writes=[mtab.res])
    S.dma("sp", qrel.ds, qrel.t[:], qrel_d, writes=[qrel.res])
    S.dma("sp", ones0f.ds, ones0f.t[:], ones0_d, writes=[ones0f.res])
    S.dma("sp", linit.ds, linit.t[:], linit_d, writes=[linit.res])
    S.dma("sp", selt.ds, selt.t[:], sel_d, writes=[selt.res])
    S.dma("sp", hmt.ds, hmt.t[:], hm_d, writes=[hmt.res])
    S.dma("sp", glt.ds, glt.t[:], gsub, writes=[glt.res])
    S.dma("sp", ftab.ds, ftab.t[:], ftab_d, writes=[ftab.res])
    S.dma("sp", glb.ds, glb.t[:], gsub.rearrange("e o -> (e o)").partition_broadcast(128), writes=[glb.res])
    for i_ in range(NSLOT_KV):
        S.op("pool", lambda e, i_=i_: e.memset(Vc[i_].t[:, :, 128:129], 1.0), writes=[Vc[i_].res])
    S.op("dve", lambda e: e.memset(ones.t[:], 1.0), writes=[ones.res])
    S.op("dve", lambda e: e.memset(epst.t[:], LN_EPS), writes=[epst.res])
    S.op("dve", lambda e: e.tensor_copy(ones0.t[:], ones0f.t[:]), reads=[ones0f.res], writes=[ones0.res])
    S.op("pool", lambda e: e.memset(halo.t[:], 0.0), writes=[halo.res])
    for c in range(2):
        S.op("pool", lambda e, c=c: e.memset(QT[c].t[:], 0.0), writes=QTres)
    for j_ in range(3):
        S.dma("sp", cwt.ds, cwt.t[:, j_, :], w_cw[j_].rearrange("(dc p) -> p dc", p=128),
              writes=[cwt.res], allow_slow_non_contiguous=True)
    S.dma("sp", lamt.ds, lamt.t[:].rearrange("p b c -> p (b c)"),
          lam4.rearrange("b c -> (b c)").partition_broadcast(128), writes=[lamt.res])
    S.op("dve", lambda e: e.tensor_tensor(out=lamw.t[:], in0=lamt.t[:, 0:4:2, :], in1=lamt.t[:, 1:4:2, :], op=ALU.mult),
         reads=[lamt.res], writes=[lamw.res])
    S.op("dve", lambda e: e.tensor_reduce(out=lams.t[:], in_=lamw.t[:], axis=mybir.AxisListType.X, op=ALU.add),
         reads=[lamw.res], writes=[lams.res])
    S.op("act", lambda e: e.activation(out=lams.t[:], in_=lams.t[:], func=AF.Exp), reads=[lams.res], writes=[lams.res])
    S.op("dve", lambda e: e.scalar_tensor_tensor(out=neglam.t[:], in0=lams.t[:, 1:2], scalar=linit.t[:, 0:1],
                                                 in1=lams.t[:, 0:1], op0=ALU.add, op1=ALU.subtract),
         reads=[lams.res, linit.res], writes=[neglam.res])
    S.op("dve", lambda e: e.tensor_scalar(out=glt.t[:], in0=glt.t[:], scalar1=linit.t[:, 1:2], scalar2=None, op0=ALU.mult),
         reads=[glt.res, linit.res], writes=[glt.res])
    S.op("dve", lambda e: e.tensor_scalar(out=glb.t[:], in0=glb.t[:], scalar1=linit.t[:, 1:2], scalar2=None, op0=ALU.mult),
         reads=[glb.res, linit.res], writes=[glb.res])
    for r in (ident, biast, mtab, qrel, ones, ones0, epst, cwt, selt, hmt, neglam, ftab, glb):
        r.res.const = True

    layer_chunks = []
    wres = {}
    cast_jobs = []
    cast_state = {'n': 0}

    def emit_casts(upto):
        while cast_state['n'] < min(upto, len(cast_jobs)):
            ds_, dst_, src_, res_ = cast_jobs[cast_state['n']]
            S.dma('pool', ds_, dst_, src_, writes=[res_])
            cast_state['n'] += 1

    wds = S.dsem("wcast")
    for i in range(2):
        lst = []
        if i == 0:
            srcs = [(w_qkv[:, 0:D], "q"), (w_qkv[:, D:2 * D], "k"), (w_qkv[:, 2 * D:3 * D], "v"), (w_ao, "o")]
        else:
            srcs = [(w_ci[:, 2 * D:3 * D], "h"), (w_ci[:, D:2 * D], "c"), (w_ci[:, 0:D], "b"), (w_co, "o")]
        up = lambda j: (w_up[i, :, j * D:(j + 1) * D], f"u{j}")
        dn = lambda j: (w_dn[i, j * D:(j + 1) * D, :], f"d{j}")
        srcs += [up(0), up(1), dn(0), up(2), dn(1), up(3), dn(2), dn(3)]
        for src, nm in srcs:
            key = (i, nm)
            wscr[key] = dscr(f"w_{i}_{nm}", [128, 8, 1024])
            wres[key] = Res(f"w_{i}_{nm}")
            cast_jobs.append((S.dsem(f"wc_{i}_{nm}"), wscr[key], src.rearrange("(kc p) n -> p kc n", p=128), wres[key]))
            lst.append(key)
        layer_chunks.append(lst)

    steps = [("p", t) for t in range(n_psteps)] + [("s", j) for j in range(n_ssteps)]
    wseq = []
    for _ in steps:
        for i in range(2):
            wseq.extend(layer_chunks[i])
    wstate = {"loaded": 0, "use": 0}

    emit_casts(CAST_AHEAD)

    def wget():
        k = wstate["use"]
        wstate["use"] += 1
        emit_casts(k + 1 + CAST_AHEAD)
        while wstate["loaded"] < len(wseq) and wstate["loaded"] < k + NSLOT_W:
            j = wstate["loaded"]
            slot = ring[j % NSLOT_W]
            S.dma("sp", slot.ds, slot.t[:], wscr[wseq[j]], reads=[wres[wseq[j]]], writes=[slot.res])
            wstate["loaded"] += 1
        return ring[k % NSLOT_W]

    pro_blocks = []
    cv_jobs = []
    cv_state = {'n': 0}

    def emit_cv(count):
        while count > 0 and cv_state['n'] < len(cv_jobs):
            dst_, src_, res_ = cv_jobs[cv_state['n']]
            S.dma('pool', cvds, dst_, src_, writes=[res_])
            cv_state['n'] += 1
            count -= 1
        if cv_state['n'] == len(cv_jobs):
            for dst_, src_, res_ in cv_jobs:
                res_.w = (cvds.key, cvds.count)

    if n_ssteps:
        cvds = S.dsem("cvcast")
        for j in range(n_ssteps):
            for q4 in range(4):
                cv_jobs.append((V_s[j][:, q4 * 1024:(q4 + 1) * 1024, :].rearrange("h k e -> k h e"),
                                cv[j, q4 * 1024:(q4 + 1) * 1024, :].rearrange("k (h e) -> k h e", h=NH),
                                sres(("V", "s", j, "hist"))))
        for j in range(n_ssteps):
            for kb in range(PAST // 128):
                pro_blocks.append((j, kb))
    class _V2:
        def __init__(self, t, res):
            self.t, self.res = t, res
    pl_bufs = [(_V2(Ap.t[:, 0:512], Ap.res), _V2(Ap.t[:, 512:1024], Ap.res)),
               (_V2(At.t[:, 0:512], At.res), _V2(At.t[:, 512:1024], At.res))]
    pl_ds = [(S.dsem("pl00"), S.dsem("pl01")), (S.dsem("pl10"), S.dsem("pl11"))]
    kts = [sb(f"kts{i}", [128, NH, 128], BF16, dma=True) for i in range(2)]
    pro_state = {"n": 0, "loaded": []}

    def pro_prefetch():
        while len(pro_state["loaded"]) < 2 and pro_state["n"] < len(pro_blocks):
            j, kb = pro_blocks[pro_state["n"]]
            par = pro_state["n"] % 2
            pro_state["n"] += 1
            for half in range(2):
                buf = pl_bufs[par][half]
                S.dma("sp", pl_ds[par][half], buf.t[:, :], ck[j, kb * 128:(kb + 1) * 128, half * 512:(half + 1) * 512],
                      writes=[buf.res])
            pro_state["loaded"].append((j, kb, par))

    def pro_consume():
        for (j, kb, par) in pro_state["loaded"]:
            kt = kts[par]
            for half in range(2):
                buf = pl_bufs[par][half]
                b = next_bank()

                def tr(e, buf=buf, b=b):
                    ins = None
                    for hh in range(4):
                        ins = e.transpose(b.t[:, hh * 128:(hh + 1) * 128], buf.t[:, hh * 128:(hh + 1) * 128], ident.t[:])
                    return ins
                S.op("pe", tr, reads=[buf.res, ident.res], writes=[b.res])
                if half == 0:
                    S.op("act", lambda e, b=b, half=half, kt=kt: e.activation(
                        out=kt.t[:, half * 4:(half + 1) * 4, :],
                        in_=b.t[:].rearrange("p (h k) -> p h k", h=4), func=AF.Copy),
                        reads=[b.res], writes=[kt.res])
                else:
                    S.op("dve", lambda e, b=b, half=half, kt=kt: e.tensor_copy(
                        kt.t[:, half * 4:(half + 1) * 4, :],
                        b.t[:].rearrange("p (h k) -> p h k", h=4)),
                        reads=[b.res], writes=[kt.res])
            S.dma("sp", kt.ds, KT_s[j][:, :, kb * 128:(kb + 1) * 128].rearrange("h p k -> p h k"),
                  kt.t[:, :, :], reads=[kt.res], writes=[sres(("KT", "s", j, "hist"))])
        pro_state["loaded"] = []

    def pro_flush():
        while pro_state["n"] < len(pro_blocks) or pro_state["loaded"]:
            pro_prefetch()
            pro_consume()

    def make_xT(T, rows, nsub):
        for kc in range(8):
            b = next_bank()

            def tr(e, b=b, kc=kc):
                ins = None
                for s in range(nsub):
                    ins = e.transpose(b.t[:, s * 128:s * 128 + rows], x_tm.t[0:rows, s, kc * 128:(kc + 1) * 128],
                                      ident.t[0:rows, 0:rows])
                return ins
            S.op("pe", tr, reads=xres[:nsub] + [ident.res], writes=[b.res])
            if kc % 2 == 0:
                S.op("act", lambda e, b=b, kc=kc: e.activation(out=xT.t[:, kc, 0:T], in_=b.t[:, 0:T], func=AF.Copy),
                     reads=[b.res], writes=[xTres[kc]])
            else:
                S.op("dve", lambda e, b=b, kc=kc: e.tensor_copy(xT.t[:, kc, 0:T], b.t[:, 0:T]),
                     reads=[b.res], writes=[xTres[kc]])

    gb_state = {"n": 0}

    def layer_norm(idx, T, rows, nsub):
        gb = gbt[gb_state["n"] % 2]
        gb_state["n"] += 1
        S.dma("sp", gb.ds, gb.t[:, 0, :], ln_g[idx].partition_broadcast(128), writes=[gb.res])
        S.dma("sp", gb.ds, gb.t[:, 1, :], ln_b[idx].partition_broadcast(128), writes=[gb.res])
        for s in range(nsub):
            for hf in range(2):
                S.op("dve", lambda e, s=s, hf=hf: e.bn_stats(out=stt.t[0:rows, s, hf * 6:(hf + 1) * 6],
                                                            in_=x_tm.t[0:rows, s, hf * 512:(hf + 1) * 512]),
                     reads=[xres[s]], writes=[stt.res])
        for s in range(nsub):
            S.op("dve", lambda e, s=s: e.bn_aggr(out=mvt.t[0:rows, s, :], in_=stt.t[0:rows, s, :]),
                 reads=[stt.res], writes=[mvt.res])
        S.op("act", lambda e: e.activation(out=rs4.t[0:rows, 0:nsub, :], in_=mvt.t[0:rows, 0:nsub, 1:2], func=AF.Ln,
                                           bias=epst.t[0:rows, 0:1]),
             reads=[mvt.res, epst.res], writes=[rs4.res])
        S.op("act", lambda e: e.activation(out=rs4.t[0:rows, 0:nsub, :], in_=rs4.t[0:rows, 0:nsub, :], func=AF.Exp, scale=-0.5),
             reads=[rs4.res], writes=[rs4.res])
        S.op("dve", lambda e: e.scalar_tensor_tensor(out=nm4.t[0:rows, 0:nsub, :], in0=mvt.t[0:rows, 0:nsub, 0:1], scalar=-1.0,
                                                     in1=rs4.t[0:rows, 0:nsub, :], op0=ALU.mult, op1=ALU.mult),
             reads=[mvt.res, rs4.res], writes=[nm4.res])
        for s in range(nsub):
            S.op("act", lambda e, s=s: e.activation(out=x_tm.t[0:rows, s, :], in_=x_tm.t[0:rows, s, :], func=AF.Identity,
                                                    scale=rs4.t[0:rows, s, :], bias=nm4.t[0:rows, s, :]),
                 reads=[xres[s], rs4.res, nm4.res], writes=[xres[s]])
        for s in range(nsub):
            S.op("dve", lambda e, s=s, gb=gb: e.tensor_tensor(out=x_tm.t[0:rows, s, :], in0=x_tm.t[0:rows, s, :],
                                                             in1=gb.t[0:rows, 0, :], op=ALU.mult),
                 reads=[xres[s], gb.res], writes=[xres[s]])
            eng = "dve" if s % 2 == 0 else "pool"
            for hf in range(2):
                S.op(eng, lambda e, s=s, gb=gb, hf=hf: e.tensor_tensor(
                    out=x_tm.t[0:rows, s, hf * 512:(hf + 1) * 512], in0=x_tm.t[0:rows, s, hf * 512:(hf + 1) * 512],
                    in1=gb.t[0:rows, 1, hf * 512:(hf + 1) * 512], op=ALU.add),
                    reads=[xres[s], gb.res], writes=[xres[s]])

    def out_proj_residual(srcT, srcres, W, T, rows, nsub, first=True):
        for s in range(nsub):
            for n in range(2):
                b = next_bank()

                def mm(e, b=b, s=s, n=n):
                    ins = None
                    for k in range(8):
                        ins = e.matmul(b.t[0:rows, :], lhsT=srcT.t[:, k, s * 128:s * 128 + rows],
                                       rhs=W.t[:, k, n * 512:(n + 1) * 512], start=(k == 0), stop=(k == 7))
                    return ins
                S.op("pe", mm, reads=list(srcres) + [W.res], writes=[b.res])
                if first:
                    S.op("dve", lambda e, b=b, s=s, n=n: e.scalar_tensor_tensor(
                        out=x_tm.t[0:rows, s, n * 512:(n + 1) * 512], in0=x_tm.t[0:rows, s, n * 512:(n + 1) * 512],
                        scalar=ALPHA, in1=b.t[0:rows, :], op0=ALU.mult, op1=ALU.add),
                        reads=[b.res, xres[s]], writes=[xres[s]])
                else:
                    S.op("dve", lambda e, b=b, s=s, n=n: e.tensor_tensor(
                        out=x_tm.t[0:rows, s, n * 512:(n + 1) * 512], in0=x_tm.t[0:rows, s, n * 512:(n + 1) * 512],
                        in1=b.t[0:rows, :], op=ALU.add),
                        reads=[b.res, xres[s]], writes=[xres[s]])

    evac_rr = [0]

    def evac_copy(dst_ap, b, T, writes, src_ap=None):
        src = b.t[:, 0:T] if src_ap is None else src_ap
        evac_rr[0] += 1
        if evac_rr[0] % 2 == 0:
            S.op("act", lambda e: e.activation(out=dst_ap, in_=src, func=AF.Copy), reads=[b.res], writes=writes)
        else:
            S.op("dve", lambda e: e.tensor_copy(dst_ap, src), reads=[b.res], writes=writes)

    st_rr = [0]

    def proj_fm(W, T, dst_fn):
        xr = list(xTres)
        for h in range(NH):
            b = next_bank()

            def mm(e, b=b, h=h):
                ins = None
                for k in range(8):
                    ins = e.matmul(b.t[:, 0:T], lhsT=W.t[:, k, h * 128:(h + 1) * 128], rhs=xT.t[:, k, 0:T],
                                   start=(k == 0), stop=(k == 7))
                return ins
            S.op("pe", mm, reads=xr + [W.res], writes=[b.res])
            dst_fn(h, b)

    def attn_layer(step):
        kind, tidx = step
        if kind == "p":
            T, rows, nsub = TT, 128, 4
            t0 = tidx * TT
            nhist = t0
            KTd, Vd = KT_p, V_p
            kout = kp[t0:t0 + T, :]
            vout = vp[t0:t0 + T, :]
            own_key = ("p", tidx)
        else:
            T, rows, nsub = DEC, DEC, 1
            t0 = PAST
            nhist = PAST
            KTd, Vd = KT_s[tidx], V_s[tidx]
            kout = ks[tidx]
            vout = vs[tidx]
            own_key = ("s", tidx)
        xr = list(xTres)
        Wq = wget()

        def q_dst(h, b):
            S.op("act", lambda e: e.activation(out=QT[0].t[0:64, h, 0:T], in_=b.t[0:64, 0:T], func=AF.Copy),
                 reads=[b.res], writes=[QTres[h]])
            S.op("dve", lambda e: e.tensor_copy(QT[1].t[64:128, h, 0:T], b.t[64:128, 0:T]),
                 reads=[b.res], writes=[QTres[h]])
        proj_fm(Wq, T, q_dst)
        Wk = wget()
        proj_fm(Wk, T, lambda h, b: evac_copy(ktb.t[:, h, 0:T], b, T, [ktb.res]))
        ktres = sres(("KT",) + own_key)
        S.dma("sp", ktb.ds, KTd[:, :, t0:t0 + T].rearrange("h p k -> p h k"), ktb.t[:, :, 0:T],
              reads=[ktb.res], writes=[ktres])
        for s in range(nsub):
            for n in range(2):
                b = next_bank()

                def mm(e, b=b, s=s, n=n, W=Wk):
                    ins = None
                    for k in range(8):
                        ins = e.matmul(b.t[0:rows, :], lhsT=xT.t[:, k, s * 128:s * 128 + rows],
                                       rhs=W.t[:, k, n * 512:(n + 1) * 512], start=(k == 0), stop=(k == 7))
                    return ins
                S.op("pe", mm, reads=xr + [Wk.res], writes=[b.res])
                st = kst[st_rr[0] % 2]
                st_rr[0] += 1
                evac_copy(st.t[0:rows, :], b, 512, [st.res], src_ap=b.t[0:rows, :])
                S.dma("sp", st.ds, kout[s * 128:s * 128 + rows, n * 512:(n + 1) * 512], st.t[0:rows, :], reads=[st.res])
        Wv = wget()
        vres = sres(("V",) + own_key)
        for s in range(nsub):
            for n in range(2):
                b = next_bank()

                def mm(e, b=b, s=s, n=n, W=Wv):
                    ins = None
                    for k in range(8):
                        ins = e.matmul(b.t[0:rows, :], lhsT=xT.t[:, k, s * 128:s * 128 + rows],
                                       rhs=W.t[:, k, n * 512:(n + 1) * 512], start=(k == 0), stop=(k == 7))
                    return ins
                S.op("pe", mm, reads=xr + [Wv.res], writes=[b.res])
                st = vst[st_rr[0] % 2]
                vb = vbb[st_rr[0] % 2]
                st_rr[0] += 1
                S.op("act", lambda e, b=b, st=st: e.activation(out=st.t[0:rows, :], in_=b.t[0:rows, :], func=AF.Copy),
                     reads=[b.res], writes=[st.res])
                S.op("pool", lambda e, st=st, vb=vb: e.tensor_copy(vb.t[0:rows, :], st.t[0:rows, :]),
                     reads=[st.res], writes=[vb.res])
                S.dma("sp", st.ds, vout[s * 128:s * 128 + rows, n * 512:(n + 1) * 512], st.t[0:rows, :], reads=[st.res])
                S.dma("sp", vb.ds, Vd[n * 4:(n + 1) * 4, t0 + s * 128:t0 + s * 128 + rows, :].rearrange("h k e -> k h e"),
                      vb.t[0:rows, :].rearrange("k (h e) -> k h e", h=4), reads=[vb.res], writes=[vres])
        npast = nhist // 128
        keep = []
        for h in range(NH):
            kh = 0
            for r in range(1, npast + 1):
                if SLOPES[h] * (1 + (r - 1) * 128) <= ALIBI_THR:
                    kh = r
            keep.append(kh)
        loads = []
        head_chunks = []
        for h in range(NH):
            first_kb = npast - keep[h]
            chs = []
            if keep[h]:
                for c in range(first_kb // (KVC // 128), (npast + KVC // 128 - 1) // (KVC // 128)):
                    kb0 = max(c * (KVC // 128), first_kb)
                    kb1 = min((c + 1) * (KVC // 128), npast)
                    loads.append((h, "hist", kb0 * 128, (kb1 - kb0) * 128))
                    chs.append((kb0, kb1))
            head_chunks.append(chs)
            loads.append((h, "diag", t0, T))
        lstate = {"n": 0}

        def hist_res(kindKV, k0, n):
            if kind == "p":
                return [sres((kindKV, "p", tt)) for tt in range(k0 // TT, (k0 + n + TT - 1) // TT)]
            return [sres((kindKV, "s", tidx, "hist"))]

        def issue_load(idx):
            h, lk, k0, n = loads[idx]
            slot = idx % NSLOT_KV
            ktc, vc = KTc[slot], Vc[slot]
            if lk == "hist":
                rk, rv = hist_res("KT", k0, n), hist_res("V", k0, n)
            else:
                rk, rv = [ktres], [vres]
            S.dma("sp", ktc.ds, ktc.t[:, 0:n], KTd[h, :, k0:k0 + n], reads=rk, writes=[ktc.res])
            if n >= 128:
                S.dma("sp", vc.ds, vc.t[:, 0:n // 128, 0:128], Vd[h, k0:k0 + n, :].rearrange("(kb p) e -> p kb e", p=128),
                      reads=rv, writes=[vc.res])
            else:
                S.dma("sp", vc.ds, vc.t[0:n, 0, 0:128], Vd[h, k0:k0 + n, :], reads=rv, writes=[vc.res])

        def get_load(idx):
            while lstate["n"] < len(loads) and lstate["n"] < idx + NSLOT_KV - 1:
                issue_load(lstate["n"])
                lstate["n"] += 1
            slot = idx % NSLOT_KV
            return KTc[slot], Vc[slot]

        Sb = banks[0:4]

        blocks = []
        lidx = 0
        for h in range(NH):
            done = 0
            for (kb0, kb1) in head_chunks[h]:
                for kabs in range(kb0, kb1):
                    blocks.append(dict(h=h, typ="past", lidx=lidx, kb=kabs - kb0, kabs=kabs, first=(done == 0),
                                       last=(done == keep[h] - 1)))
                    done += 1
                lidx += 1
            ndb = (T + 127) // 128
            for j in range(ndb):
                blocks.append(dict(h=h, typ="diag", lidx=lidx, j=j, first=(j == 0), last=(j == ndb - 1), haspast=bool(keep[h])))
            lidx += 1

        def emit_qk(i, bl):
            ktc, vc = get_load(bl["lidx"])
            bl["ktc"], bl["vc"] = ktc, vc
            par = i % 2
            sb0, sb1 = Sb[par * 2], Sb[par * 2 + 1]
            bl["sb"] = (sb0, sb1)
            bl["p"] = (Pt[par * 2], Pt[par * 2 + 1])
            h = bl["h"]
            if bl["typ"] == "past":
                kb = bl["kb"]

                def qk(e):
                    e.matmul(sb0.t[:, 0:T], lhsT=ktc.t[:, kb * 128:(kb + 1) * 128], rhs=QT[0].t[:, h, 0:T],
                             start=True, stop=True)
                    return e.matmul(sb1.t[:, 0:T], lhsT=ktc.t[:, kb * 128:(kb + 1) * 128], rhs=QT[1].t[:, h, 0:T],
                                    start=True, stop=True)
            else:
                j = bl["j"]
                nk = min(128, T - j * 128)
                q0 = j * 128

                def qk(e):
                    e.matmul(sb0.t[0:nk, q0:T], lhsT=ktc.t[:, j * 128:j * 128 + nk], rhs=QT[0].t[:, h, q0:T],
                             start=True, stop=True)
                    return e.matmul(sb1.t[0:nk, q0:T], lhsT=ktc.t[:, j * 128:j * 128 + nk], rhs=QT[1].t[:, h, q0:T],
                                    start=True, stop=True)
            S.op("pe", qk, reads=[ktc.res, QTres[h]], writes=[sb0.res, sb1.res])

        def emit_exp(bl):
            h = bl["h"]
            slope = SLOPES[h]
            if bl["typ"] == "past":
                rel = bl["kabs"] - npast + 64
                for (sbx, px) in zip(bl["sb"], bl["p"]):
                    S.op("act", lambda e, sbx=sbx, px=px: e.activation(
                        out=px.t[:, 0:T], in_=sbx.t[:, 0:T], func=AF.Exp,
                        bias=biast.t[:, h * 64 + rel:h * 64 + rel + 1], scale=0.125),
                        reads=[sbx.res, biast.res], writes=[px.res])
            else:
                j = bl["j"]
                nk = min(128, T - j * 128)
                q0 = j * 128
                for ci, (sbx, px) in enumerate(zip(bl["sb"], bl["p"])):
                    dt_ = dtmp[ci]
                    S.op("dve", lambda e, sbx=sbx, dt_=dt_: e.scalar_tensor_tensor(
                        out=dt_.t[0:nk, q0:T], in0=mtab.t[0:nk, j * TT + q0:j * TT + T], scalar=slope,
                        in1=sbx.t[0:nk, q0:T], op0=ALU.mult, op1=ALU.add),
                        reads=[sbx.res, mtab.res], writes=[dt_.res])
                    S.op("act", lambda e, px=px, dt_=dt_: e.activation(
                        out=px.t[0:nk, q0:T], in_=dt_.t[0:nk, q0:T], func=AF.Exp, scale=0.125),
                        reads=[dt_.res], writes=[px.res])

        nqs = nsub
        OB = banks[4:7]
        TB = banks[7]

        def otile(t):
            return OB[t // 3], (t % 3) * 160

        def apv(buf, t):
            return buf.t[0:rows, t * VW:(t + 1) * VW]

        bank_started = {}

        def emit_pv(bl):
            vc = bl["vc"]
            pp = bl["p"]
            first, last = bl["first"], bl["last"]
            if bl["typ"] == "past":
                kb = bl["kb"]
                nk, qs0 = 128, 0
                if kind == "p" and bl["kabs"] < TT // 128:
                    pass
            else:
                kb = bl["j"]
                nk = min(128, T - kb * 128)
                qs0 = kb
            if first:
                bank_started.clear()
            use0 = (bl["typ"] == "past" and kind == "p" and bl["kabs"] < TT // 128)
            plan = []
            for c in range(2):
                for qs in range(qs0, nqs):
                    t = c * 4 + qs
                    bk, off = otile(t)
                    st = bk.name_ not in bank_started
                    bank_started[bk.name_] = True
                    plan.append((c, qs, bk, off, st))

            def pv(e):
                ins = None
                for (c, qs, bk, off, st) in plan:
                    q0 = qs * 128
                    qn = min(128, T - q0)
                    if use0:
                        ins = e.matmul(bk.t[0:qn, off:off + 128], lhsT=pp[c].t[0:nk, q0:q0 + qn], rhs=vc.t[0:nk, kb, 0:128],
                                       start=st, stop=last, skip_group_check=True)
                        ins = e.matmul(bk.t[0:qn, off + 128:off + VW], lhsT=pp[c].t[0:nk, q0:q0 + qn], rhs=ones0.t[0:nk, 0:1],
                                       start=False, stop=last, skip_group_check=True)
                    else:
                        ins = e.matmul(bk.t[0:qn, off:off + VW], lhsT=pp[c].t[0:nk, q0:q0 + qn], rhs=vc.t[0:nk, kb, :],
                                       start=st, stop=last, skip_group_check=True)
                return ins
            S.op("pe", pv, reads=[vc.res, pp[0].res, pp[1].res, ones0.res], writes=[bk.res for bk in OB])

        def emit_past_evac(h):
            for t in range(8):
                c, qs = t // 4, t % 4
                if qs >= nqs:
                    continue
                bk, off = otile(t)
                eng = "dve" if (t // 3) != 1 else "act"
                fcol = ftab.t[0:rows, h * 4 + qs:h * 4 + qs + 1]
                if eng == "dve":
                    S.op("dve", lambda e, t=t, bk=bk, off=off, fcol=fcol: e.tensor_scalar(
                        out=apv(Ap, t), in0=bk.t[0:rows, off:off + VW], scalar1=fcol, scalar2=None, op0=ALU.mult),
                        reads=[bk.res, ftab.res], writes=[Ap.res])
                else:
                    S.op("act", lambda e, t=t, bk=bk, off=off, fcol=fcol: e.activation(
                        out=apv(Ap, t), in_=bk.t[0:rows, off:off + VW], func=AF.Copy, scale=fcol),
                        reads=[bk.res, ftab.res], writes=[Ap.res])

        def emit_finalize(h, haspast):
            for t in range(8):
                c, qs = t // 4, t % 4
                if qs >= nqs:
                    continue
                bk, off = otile(t)
                if haspast:
                    S.op("dve", lambda e, t=t, bk=bk, off=off: e.tensor_tensor(
                        out=apv(At, t), in0=bk.t[0:rows, off:off + VW], in1=apv(Ap, t), op=ALU.add),
                        reads=[bk.res, Ap.res], writes=[At.res])
                elif (t // 3) != 1:
                    S.op("dve", lambda e, t=t, bk=bk, off=off: e.tensor_copy(apv(At, t), bk.t[0:rows, off:off + VW]),
                         reads=[bk.res], writes=[At.res])
                else:
                    S.op("act", lambda e, t=t, bk=bk, off=off: e.activation(out=apv(At, t), in_=bk.t[0:rows, off:off + VW],
                                                                           func=AF.Copy),
                         reads=[bk.res], writes=[At.res])
            Atv = At.t[0:rows, :].rearrange("p (t w) -> p t w", w=VW)
            S.op("dve", lambda e: e.reciprocal(rr.t[0:rows, :, :], Atv[:, :, 128:129]), reads=[At.res], writes=[rr.res])
            S.op("dve", lambda e: e.tensor_scalar(out=rr.t[0:rows, 4:8, :], in0=rr.t[0:rows, 4:8, :],
                                                  scalar1=neglam.t[0:rows, 0:1], scalar2=None, op0=ALU.mult),
                 reads=[rr.res, neglam.res], writes=[rr.res])
            for qs in range(nqs):
                S.op("pool", lambda e, qs=qs: e.tensor_scalar(out=Otm.t[0:rows, qs, :], in0=Atv[:, qs, 0:128],
                                                             scalar1=rr.t[0:rows, qs, :], scalar2=None, op0=ALU.mult),
                     reads=[At.res, rr.res], writes=[Otm.res])
            for qs in range(nqs):
                S.op("dve", lambda e, qs=qs: e.scalar_tensor_tensor(out=Otm.t[0:rows, qs, :], in0=Atv[:, 4 + qs, 0:128],
                                                                   scalar=rr.t[0:rows, 4 + qs, :], in1=Otm.t[0:rows, qs, :],
                                                                   op0=ALU.mult, op1=ALU.add),
                     reads=[At.res, rr.res, Otm.res], writes=[Otm.res])
            for qs in range(nqs):
                S.op("act", lambda e, qs=qs: e.activation(out=Onf.t[0:rows, qs, :], in_=Otm.t[0:rows, qs, :], func=AF.Square,
                                                         accum_out=ss4.t[0:rows, qs, :]),
                     reads=[Otm.res], writes=[Onf.res, ss4.res])
            S.op("act", lambda e: e.activation(out=ss4.t[0:rows, 0:nqs, :], in_=ss4.t[0:rows, 0:nqs, :], func=AF.Ln,
                                               scale=1.0 / 128.0, bias=epst.t[0:rows, 0:1]),
                 reads=[ss4.res, epst.res], writes=[ss4.res])
            S.op("act", lambda e: e.activation(out=ss4.t[0:rows, 0:nqs, :], in_=ss4.t[0:rows, 0:nqs, :], func=AF.Exp, scale=-0.5),
                 reads=[ss4.res], writes=[ss4.res])
            for qs in range(nqs):
                S.op("dve", lambda e, qs=qs: e.scalar_tensor_tensor(out=Onf.t[0:rows, qs, :], in0=Otm.t[0:rows, qs, :],
                                                                   scalar=ss4.t[0:rows, qs, :], in1=glb.t[0:rows, :],
                                                                   op0=ALU.mult, op1=ALU.mult),
                     reads=[Otm.res, ss4.res, glb.res, Onf.res], writes=[Onf.res])


        def emit_head_transpose(h):
            def tr(e):
                ins = None
                for qs in range(nqs):
                    ins = e.transpose(TB.t[:, qs * 128:qs * 128 + rows], Onf.t[0:rows, qs, :], ident.t[0:rows, 0:rows])
                return ins
            S.op("pe", tr, reads=[Onf.res, ident.res], writes=[TB.res])
            if h % 2 == 0:
                S.op("act", lambda e: e.activation(out=OnT.t[:, h, 0:T], in_=TB.t[:, 0:T], func=AF.Copy),
                     reads=[TB.res], writes=[OnTres[h]])
            else:
                S.op("dve", lambda e: e.tensor_copy(OnT.t[:, h, 0:T], TB.t[:, 0:T]), reads=[TB.res], writes=[OnTres[h]])

        nb = len(blocks)
        pending = []
        emit_qk(0, blocks[0])
        for i, bl in enumerate(blocks):
            if i + 1 < nb:
                emit_qk(i + 1, blocks[i + 1])
            emit_exp(bl)
            emit_pv(bl)
            if bl["typ"] == "past" and bl["last"]:
                emit_past_evac(bl["h"])
            if pending and i >= pending[0][1]:
                emit_head_transpose(pending.pop(0)[0])
            if bl["typ"] == "diag" and bl["last"]:
                emit_finalize(bl["h"], bl["haspast"])
                pending.append((bl["h"], i + 3))
        while pending:
            emit_head_transpose(pending.pop(0)[0])
        S.stage(6)
        Wo = wget()
        out_proj_residual(OnT, OnTres, Wo, T, rows, nsub, first=True)
        S.stage(7)

    def conv_layer(step):
        kind, tidx = step
        if kind == "p":
            T, rows, nsub = TT, 128, 4
        else:
            T, rows, nsub = DEC, DEC, 1
        hl = halo
        if kind == "s":
            for j_ in range(2):
                S.dma("sp", hl.ds, hl.t[:, :, j_], sc[tidx, j_].rearrange("(dc p) -> p dc", p=128), writes=[hl.res],
                      allow_slow_non_contiguous=True)
            S.op("pool", lambda e: e.tensor_copy(uext.t[:, :, 0:2], hl.t[:]), reads=[hl.res], writes=ures)
        else:
            S.op("pool", lambda e: e.tensor_scalar(out=uext.t[:, :, 0:2], in0=hl.t[:], scalar1=hmt.t[:, tidx:tidx + 1],
                                                   scalar2=None, op0=ALU.mult),
                 reads=[hl.res, hmt.res], writes=ures)
        Wh = wget()
        proj_fm(Wh, T, lambda dc, b: S.op("act", lambda e: e.activation(out=uext.t[:, dc, 2:2 + T], in_=b.t[:, 0:T],
                                                                         func=AF.Copy),
                                          reads=[b.res], writes=[ures[dc]]))
        Wc = wget()
        proj_fm(Wc, T, lambda dc, b: S.op("dve", lambda e: e.tensor_tensor(out=uext.t[:, dc, 2:2 + T], in0=b.t[:, 0:T],
                                                                            in1=uext.t[:, dc, 2:2 + T], op=ALU.mult),
                                          reads=[b.res, ures[dc]], writes=[ures[dc]]))
        S.op("pool", lambda e: e.tensor_copy(hl.t[:], uext.t[:, :, T:T + 2]), reads=ures, writes=[hl.res])
        dsts = []
        if kind == "p" and tidx == SEQ // TT - 1:
            dsts.append(cpo[0])
        if kind == "p" and tidx == SEQ // TT:
            dsts.append(cpo[1])
        if kind == "s":
            dsts.append(cs[tidx])
        for dst in dsts:
            for j_ in range(2):
                S.dma("sp", hl.ds, dst[j_].rearrange("(dc p) -> p dc", p=128), hl.t[:, :, j_], reads=[hl.res],
                      allow_slow_non_contiguous=True)
        Wb = wget()

        def b_dst(dc, b):
            z = zc[dc % 2]
            S.op("act", lambda e: e.activation(out=z.t[:, 0:T], in_=uext.t[:, dc, 0:T], func=AF.Copy,
                                               scale=cwt.t[:, 0, dc:dc + 1]),
                 reads=[ures[dc], cwt.res], writes=[z.res])
            for jj in (1, 2):
                S.op("dve", lambda e, jj=jj: e.scalar_tensor_tensor(
                    out=z.t[:, 0:T], in0=uext.t[:, dc, jj:jj + T], scalar=cwt.t[:, jj, dc:dc + 1],
                    in1=z.t[:, 0:T], op0=ALU.mult, op1=ALU.add),
                    reads=[ures[dc], cwt.res, z.res], writes=[z.res])
            S.op("dve", lambda e: e.tensor_tensor(out=OnT.t[:, dc, 0:T], in0=b.t[:, 0:T], in1=z.t[:, 0:T], op=ALU.mult),
                 reads=[b.res, z.res], writes=[OnTres[dc]])
        proj_fm(Wb, T, b_dst)
        Wo = wget()
        out_proj_residual(OnT, OnTres, Wo, T, rows, nsub, first=True)

    def mlp_layer(step):
        kind, tidx = step
        if kind == "p":
            T, rows, nsub = TT, 128, 4
        else:
            T, rows, nsub = DEC, DEC, 1

        def up(j):
            Wu = wget()
            hb = hT[j % 2]

            def dst(fc, b):
                rb = rbuf[fc % 2]
                hres = OnTres[fc] if hb is OnT else ktb.res
                S.op("act", lambda e: e.activation(out=rb.t[:, 0:T], in_=b.t[:, 0:T], func=AF.Relu),
                     reads=[b.res], writes=[rb.res])
                S.op("pool", lambda e: e.tensor_tensor(out=hb.t[:, fc, 0:T], in0=rb.t[:, 0:T], in1=rb.t[:, 0:T],
                                                       op=ALU.mult),
                     reads=[rb.res], writes=[hres])
            proj_fm(Wu, T, dst)

        def down(j):
            Wd = wget()
            hb = hT[j % 2]
            out_proj_residual(hb, OnTres if hb is OnT else [ktb.res], Wd, T, rows, nsub, first=(j == 0))
        up(0); up(1); down(0); up(2); down(1); up(3); down(2); down(3)

    xds = x_tm.ds
    ccds = S.dsem("cc")
    for si, step in enumerate(steps):
        kind, tidx = step
        if kind == "p":
            T, rows, nsub = TT, 128, 4
            src = xp[tidx * TT:(tidx + 1) * TT, :].rearrange("(s p) d -> p s d", p=128)
            S.dma("sp", xds, x_tm.t[:, :, :], src, writes=xres)
        else:
            T, rows, nsub = DEC, DEC, 1
            S.dma("sp", xds, x_tm.t[0:DEC, 0, :], xs[tidx], writes=xres)
        if exchange and si > 0:
            if kind == "p":
                S.dma("sp", uext.ds, rview, recv[0:TT, :].rearrange("(s p) d -> p s d", p=128), reads=[recv_res], writes=ures)
            else:
                S.dma("sp", uext.ds, rview[0:DEC, 0, :], recv[0:DEC, :], reads=[recv_res], writes=ures)
            for s in range(nsub):
                S.op("dve", lambda e, s=s, rows=rows: e.scalar_tensor_tensor(
                    out=x_tm.t[0:rows, s, :], in0=rview[0:rows, s, :], scalar=selt.t[0:rows, 0:1],
                    in1=x_tm.t[0:rows, s, :], op0=ALU.mult, op1=ALU.add),
                    reads=ures + [xres[s], selt.res], writes=[xres[s]])
        make_xT(T, rows, nsub)
        S.stage(3)
        pro = (kind == "p")
        attn_layer(step)
        if pro:
            pro_prefetch()
        layer_norm(0, T, rows, nsub)
        make_xT(T, rows, nsub)
        S.stage(8)
        mlp_layer(step)
        S.stage(9)
        if pro:
            pro_consume()
            pro_prefetch()
        layer_norm(1, T, rows, nsub)
        make_xT(T, rows, nsub)
        conv_layer(step)
        if pro:
            pro_consume()
            pro_prefetch()
        layer_norm(2, T, rows, nsub)
        make_xT(T, rows, nsub)
        mlp_layer(step)
        if pro:
            pro_consume()
        layer_norm(3, T, rows, nsub)
        if kind == "p":
            S.dma("sp", xds, yp[tidx * TT:(tidx + 1) * TT, :].rearrange("(s p) d -> p s d", p=128), x_tm.t[:, :, :],
                  reads=xres)
        else:
            S.dma("sp", xds, ys[tidx], x_tm.t[0:DEC, 0, :], reads=xres)
        if exchange and si < len(steps) - 1:
            if kind == "p":
                S.dma("sp", xds, send.rearrange("(s p) d -> p s d", p=128), x_tm.t[:, :, :], reads=xres, writes=[send_res])
            else:
                S.dma("sp", xds, send[0:DEC, :], x_tm.t[0:DEC, 0, :], reads=xres, writes=[send_res])
            S._deps("pool", [send_res], [recv_res])
            ccds.count += 1
            ev = (ccds.key, ccds.count)
            S.streams["pool"].append(("op", lambda e: e.collective_compute(
                "AllGather", ALU.bypass, replica_groups=PAIRS, ins=[send], outs=[recv]), ev[0], 1))
            S._record(ev, [send_res], [recv_res])
        if kind == "p":
            emit_cv(1)
            if si + 1 < len(steps) and steps[si + 1][0] == "s":
                pro_flush()
                emit_cv(10 ** 6)
    S.emit()
    return nc, S


def _tables():
    biast = np.zeros((128, NH * 64), np.float32)
    p = np.arange(128, dtype=np.float64)
    for h in range(NH):
        for r in range(64):
            rel = r - 64
            biast[:, h * 64 + r] = (SLOPES[h] * (rel * 128 + p)).astype(np.float32)
    mt = np.zeros((128, 4 * TT), np.float32)
    q = np.arange(TT)
    for j in range(4):
        k = j * 128 + np.arange(128)
        vis = (k[:, None] // 64) <= (q[None, :] // 64)
        m = -np.abs(q[None, :] - k[:, None]).astype(np.float64)
        mt[:, j * TT:(j + 1) * TT] = (8.0 * np.where(vis, m, -1.0e5)).astype(np.float32)
    qrel = np.broadcast_to(np.arange(TT, dtype=np.float32)[None, :], (128, TT)).copy()
    return biast, mt, qrel


def _ftab():
    ft = np.zeros((128, NH * 4), np.float32)
    p = np.arange(128, dtype=np.float64)
    for h in range(NH):
        for qs in range(4):
            ft[:, h * 4 + qs] = np.exp(-SLOPES[h] * (qs * 128 + p)).astype(np.float32)
    return ft


_CACHE = {}


def make_in_maps(x_prompt, x_sample, cache_k, cache_v, state_conv, w_qkv, lambda_q1, lambda_k1,
                 lambda_q2, lambda_k2, g_subln, w_attn_out, w_conv_in, w_conv, w_conv_out,
                 w_up, w_down, ln_g, ln_b):
    f = lambda a: np.ascontiguousarray(np.asarray(a, dtype=np.float32))
    biast, mt, qrel = _tables()
    xp = np.asarray(x_prompt); xs = np.asarray(x_sample)
    ck = np.asarray(cache_k).reshape(2, 8, PAST, D); cv = np.asarray(cache_v).reshape(2, 8, PAST, D)
    sc = np.asarray(state_conv)
    lng = np.asarray(ln_g); lnb = np.asarray(ln_b)
    lam = np.stack([np.asarray(lambda_q1), np.asarray(lambda_k1), np.asarray(lambda_q2), np.asarray(lambda_k2)], axis=1)
    in_maps = []
    for c in range(8):
        p, r = c // 2, c % 2
        m = {"ident": np.eye(128, dtype=np.float32), "biastab": biast, "mtab": mt, "qrel": qrel, "ftab": _ftab()}
        xpc = np.zeros((NPS * TT, D), np.float32)
        xsc = np.zeros((NSS, DEC, D), np.float32)
        ckc = np.zeros((NSS, PAST, D), np.float32)
        cvc = np.zeros((NSS, PAST, D), np.float32)
        scc = np.zeros((NSS, 2, D), np.float32)
        if r == 0:
            xpc[:SEQ] = xp[p]
            xsc[0:2] = xs[2 * p:2 * p + 2]
        for j in range(2):
            ckc[j + r] = ck[r, 2 * p + j]
            cvc[j + r] = cv[r, 2 * p + j]
            scc[j + r] = sc[r, 2 * p + j]
        m.update(xp=xpc, xs=xsc, ck=ckc, cv=cvc, sc=scc)
        m["w_qkv"] = f(np.asarray(w_qkv)[r]); m["w_ao"] = f(np.asarray(w_attn_out)[r])
        m["w_ci"] = f(np.asarray(w_conv_in)[r]); m["w_cw"] = f(np.asarray(w_conv)[r]); m["w_co"] = f(np.asarray(w_conv_out)[r])
        m["w_up"] = f(np.asarray(w_up)[2 * r:2 * r + 2]); m["w_dn"] = f(np.asarray(w_down)[2 * r:2 * r + 2])
        m["lam4"] = f(lam[r]); m["gsub"] = f(np.asarray(g_subln)[r].reshape(128, 1))
        m["ln_g"] = f(lng[2 * r:2 * r + 2].reshape(4, D)); m["ln_b"] = f(lnb[2 * r:2 * r + 2].reshape(4, D))
        li0 = lam_init_of(2 * r)
        m["linit"] = np.tile(np.array([[-li0, 1.0 - li0]], np.float32), (128, 1))
        m["sel"] = np.full((128, 1), float(r), np.float32)
        hm = np.ones((128, NPS), np.float32)
        hm[:, 0] = 0.0
        if r == 1:
            hm[:, 1] = 0.0
        m["hm"] = hm
        m["ones0"] = np.full((128, 128), 1.0 - r, np.float32)
        in_maps.append(m)
    return in_maps


def assemble(res):
    yp = np.zeros((4, SEQ, D), np.float32); ys = np.zeros((8, DEC, D), np.float32)
    kp = np.zeros((2, 4, SEQ, D), np.float32); vp = np.zeros((2, 4, SEQ, D), np.float32)
    cp = np.zeros((2, 4, 2, D), np.float32)
    ks = np.zeros((2, 8, DEC, D), np.float32); vs = np.zeros((2, 8, DEC, D), np.float32)
    cs = np.zeros((2, 8, 2, D), np.float32)
    for p in range(4):
        a, b = res[2 * p], res[2 * p + 1]
        yp[p] = b["yp"][TT:TT + SEQ]
        kp[0, p] = a["kp"][0:SEQ]; kp[1, p] = b["kp"][TT:TT + SEQ]
        vp[0, p] = a["vp"][0:SEQ]; vp[1, p] = b["vp"][TT:TT + SEQ]
        cp[0, p] = a["cpo"][0]; cp[1, p] = b["cpo"][1]
        for j in range(2):
            ys[2 * p + j] = b["ys"][1 + j]
            ks[0, 2 * p + j] = a["ks"][j]; ks[1, 2 * p + j] = b["ks"][1 + j]
            vs[0, 2 * p + j] = a["vs"][j]; vs[1, 2 * p + j] = b["vs"][1 + j]
            cs[0, 2 * p + j] = a["cs"][j]; cs[1, 2 * p + j] = b["cs"][1 + j]
    return (yp, ys, kp.reshape(2, 4, SEQ, NH, 2, 64), vp.reshape(2, 4, SEQ, NH, 128), cp,
            ks.reshape(2, 8, DEC, NH, 2, 64), vs.reshape(2, 8, DEC, NH, 128), cs)


def kernel(**inputs):
    if "nc" not in _CACHE:
        _CACHE["nc"] = build_program()[0]
    nc = _CACHE["nc"]
    in_maps = make_in_maps(**inputs)
    res = run_bass_kernel_spmd(nc, in_maps, core_ids=list(range(8))).results
    return assemble(res)
```

```python
import math
import os
import numpy as np
import concourse.bass as bass
import concourse.mybir as mybir
from concourse.bass_utils import run_bass_kernel_spmd

F32 = mybir.dt.float32
BF16 = mybir.dt.bfloat16
ALU = mybir.AluOpType
AF = mybir.ActivationFunctionType

D = 1024
SEQ = 8192
NH = 8
DEPTH = 4
PAST = 4096
DEC = 16
TT = 512
ALPHA = (2.0 * DEPTH) ** 0.25
LN_EPS = 1e-5
SLOPES = [2.0 ** (-8.0 * (h + 1) / NH) for h in range(NH)]
NSLOT_W = 3
NSLOT_KV = 4
VW = 129
CAST_AHEAD = 5
ALIBI_THR = 125.0
KVC = 1024


def lam_init_of(i):
    return 0.8 - 0.6 * math.exp(-0.3 * i)


ENGS = ("pe", "act", "dve", "pool", "sp")


class Res:
    __slots__ = ("name", "w", "r", "const")

    def __init__(self, name):
        self.name = name
        self.w = None
        self.r = {}
        self.const = False


class DSem:
    __slots__ = ("key", "count")

    def __init__(self, key):
        self.key = key
        self.count = 0


class Sched:
    def __init__(self, nc):
        self.nc = nc
        self.streams = {e: [] for e in ENGS}
        self.sems = {}
        self.ecnt = {e: 0 for e in ENGS}
        self.known = {e: {} for e in ENGS}
        self.selfsync = {"pe": False, "act": True, "dve": True, "pool": True, "sp": True}
        for e in ENGS:
            self._sem("E_" + e)
        self.dsems = []
        self.n_ops = 0
        self.dead = False
        self.stop = int(os.environ.get('K_STOP', '99'))

    def _sem(self, key):
        if key not in self.sems:
            self.sems[key] = self.nc.alloc_semaphore(name=key)
        return key

    def dsem(self, name):
        d = DSem(self._sem("D_" + name))
        self.dsems.append(d)
        return d

    def _wait(self, eng, ev):
        if ev is None:
            return
        key, val = ev
        if key == "E_" + eng and not self.selfsync[eng]:
            return
        k = self.known[eng]
        if k.get(key, 0) >= val:
            return
        k[key] = val
        self.streams[eng].append(("wait", key, val))

    def _deps(self, eng, reads, writes):
        for r in reads:
            self._wait(eng, r.w)
        for w in writes:
            self._wait(eng, w.w)
            for key, val in w.r.items():
                self._wait(eng, (key, val))

    def _record(self, ev, reads, writes):
        for r in reads:
            if r.const:
                continue
            if r.r.get(ev[0], 0) < ev[1]:
                r.r[ev[0]] = ev[1]
        for w in writes:
            w.w = ev
            w.r = {}

    def stage(self, n):
        if n >= self.stop:
            self.dead = True

    def op(self, eng, fn, reads=(), writes=()):
        if self.dead:
            return None
        self._deps(eng, reads, writes)
        self.ecnt[eng] += 1
        ev = ("E_" + eng, self.ecnt[eng])
        self.streams[eng].append(("op", fn, ev[0], 1))
        self._record(ev, reads, writes)
        self.n_ops += 1
        return ev

    def dma(self, eng, dsem, out, in_, reads=(), writes=(), **kw):
        if self.dead:
            return None
        self._deps(eng, reads, writes)
        dsem.count += 16
        ev = (dsem.key, dsem.count)

        def fn(e, out=out, in_=in_, kw=kw):
            return e.dma_start(out=out, in_=in_, **kw)
        self.streams[eng].append(("op", fn, ev[0], 16))
        self._record(ev, reads, writes)
        self.n_ops += 1
        return ev

    def finish(self):
        for d in self.dsems:
            if d.count:
                self._wait("sp", (d.key, d.count))
        for e in ENGS:
            if e != "sp" and self.ecnt[e]:
                self._wait("sp", ("E_" + e, self.ecnt[e]))

    def _replay(self, eng, e):
        for item in self.streams[eng]:
            if item[0] == "wait":
                e.wait_ge(self.sems[item[1]], item[2])
            else:
                _, fn, key, amt = item
                fn(e).then_inc(self.sems[key], amt)

    def emit(self):
        self.finish()
        with self.nc.Block() as block:
            @block.tensor
            def _(e):
                self._replay("pe", e)

            @block.scalar
            def _(e):
                self._replay("act", e)

            @block.vector
            def _(e):
                self._replay("dve", e)

            @block.gpsimd
            def _(e):
                self._replay("pool", e)

            @block.sync
            def _(e):
                self._replay("sp", e)


class Buf:
    def __init__(self, S, t, name, dma=False):
        self.t = t
        self.name_ = name
        self.res = Res(name)
        self.ds = S.dsem(name) if dma else None


NPS = SEQ // TT + 1
NSS = 3
PAIRS = [[0, 1], [2, 3], [4, 5], [6, 7]]


def build_program(n_psteps=NPS, n_ssteps=NSS, exchange=True):
    nc = bass.Bass("TRN2", target_bir_lowering=False)
    S = Sched(nc)

    def din(name, shape, dt=F32):
        return nc.dram_tensor(name, list(shape), dt, kind="ExternalInput").ap()

    def dout(name, shape, dt=F32):
        return nc.dram_tensor(name, list(shape), dt, kind="ExternalOutput").ap()

    def dscr(name, shape, dt=BF16):
        return nc.dram_tensor(name, list(shape), dt).ap()

    NROW = NPS * TT
    xp = din("xp", [NROW, D])
    xs = din("xs", [NSS, DEC, D])
    ck = din("ck", [NSS, PAST, D])
    cv = din("cv", [NSS, PAST, D])
    sc = din("sc", [NSS, 2, D])
    w_qkv = din("w_qkv", [D, 3 * D])
    w_ao = din("w_ao", [D, D])
    w_ci = din("w_ci", [D, 3 * D])
    w_cw = din("w_cw", [3, D])
    w_co = din("w_co", [D, D])
    w_up = din("w_up", [2, D, 4 * D])
    w_dn = din("w_dn", [2, 4 * D, D])
    lam4 = din("lam4", [4, 64])
    gsub = din("gsub", [128, 1])
    ln_g = din("ln_g", [4, D])
    ln_b = din("ln_b", [4, D])
    linit_d = din("linit", [128, 2])
    sel_d = din("sel", [128, 1])
    hm_d = din("hm", [128, NPS])
    ones0_d = din("ones0", [128, 128])
    ident_d = din("ident", [128, 128])
    bias_d = din("biastab", [128, NH * 64])
    mtab_d = din("mtab", [128, 4 * TT])
    qrel_d = din("qrel", [128, TT])
    ftab_d = din("ftab", [128, NH * 4])

    yp = dout("yp", [NROW, D])
    ys = dout("ys", [NSS, DEC, D])
    kp = dout("kp", [NROW, D])
    vp = dout("vp", [NROW, D])
    cpo = dout("cpo", [2, 2, D])
    ks = dout("ks", [NSS, DEC, D])
    vs = dout("vs", [NSS, DEC, D])
    cs = dout("cs", [NSS, 2, D])

    wscr = {}
    KT_p = dscr("KT_p", [NH, 128, NROW])
    V_p = dscr("V_p", [NH, NROW, 128])
    NKS = PAST + 128
    KT_s = [dscr(f"KT_s{j}", [NH, 128, NKS]) for j in range(NSS)]
    V_s = [dscr(f"V_s{j}", [NH, NKS, 128]) for j in range(NSS)]
    send = dscr("send", [TT, D], F32)
    recv = dscr("recv", [2 * TT, D], F32)
    send_res, recv_res = Res("send"), Res("recv")
    scr_res = {}

    def sres(key):
        if key not in scr_res:
            scr_res[key] = Res("scr" + str(key))
        return scr_res[key]

    A = nc.alloc_sbuf_tensor

    def sb(name, shape, dt=F32, dma=False):
        return Buf(S, A("sb_" + name, list(shape), dt), name, dma)

    x_tm = sb("x_tm", [128, 4, D], F32, dma=True)
    xres = [Res(f"x_tm{s}") for s in range(4)]
    xT = sb("xT", [128, 8, TT], BF16)
    xTres = [Res(f"xT{k}") for k in range(8)]
    ring = [sb(f"wring{i}", [128, 8, 1024], BF16, dma=True) for i in range(NSLOT_W)]
    QT = [sb(f"QT{c}", [128, NH, TT], BF16) for c in range(2)]
    QTres = [Res(f"QTh{h}") for h in range(NH)]
    KTc = [sb(f"KTc{i}", [128, KVC], BF16, dma=True) for i in range(NSLOT_KV)]
    Vc = [sb(f"Vc{i}", [128, KVC // 128, VW], BF16, dma=True) for i in range(NSLOT_KV)]
    Pt = [sb(f"P{i}", [128, TT], BF16) for i in range(4)]
    OnT = sb("OnT", [128, NH, TT], BF16)
    OnTres = [Res(f"OnT{h}") for h in range(NH)]
    Ap = sb("Ap", [128, 8 * VW], F32)
    At = sb("At", [128, 8 * VW], F32)
    Onf2 = [sb(f"Onf{i}", [128, 4, 128], F32) for i in range(2)]
    Otm = sb("Otm", [128, 4, 128], F32)
    rr = sb("rr", [128, 8, 1], F32)
    ss4 = sb("ss4", [128, 4, 1], F32)
    glb = sb("glb", [128, 128], F32, dma=True)
    ftab = sb("ftab", [128, NH * 4], F32, dma=True)
    tmpf = [sb(f"tmpf{i}", [128, TT], F32) for i in range(2)]
    dtmp = tmpf
    rbuf = tmpf
    zc = tmpf
    ktb = sb("ktb", [128, NH, TT], BF16, dma=True)
    hT = [OnT, ktb]
    biast = sb("biast", [128, NH * 64], F32, dma=True)
    mtab = sb("mtab", [128, 4 * TT], F32, dma=True)
    qrel = sb("qrel", [128, TT], F32, dma=True)
    ident = sb("ident", [128, 128], F32, dma=True)
    ones = sb("ones", [128, 128], BF16)
    ones0f = sb("ones0f", [128, 128], F32, dma=True)
    ones0 = sb("ones0", [128, 128], BF16)
    epst = sb("epst", [128, 1], F32)
    gbt = [sb(f"gb{i}", [128, 2, D], F32, dma=True) for i in range(2)]
    kst = [sb(f"kst{i}", [128, 512], F32, dma=True) for i in range(2)]
    vst = [sb(f"vst{i}", [128, 512], F32, dma=True) for i in range(2)]
    vbb = [sb(f"vbb{i}", [128, 512], BF16, dma=True) for i in range(2)]
    uext = sb("uext", [128, 8, TT + 4], F32, dma=True)
    ures = [Res(f"uext{k}") for k in range(8)]
    halo = sb("halo", [128, 8, 2], F32, dma=True)
    cwt = sb("cwt", [128, 3, 8], F32, dma=True)
    lamt = sb("lamt", [128, 4, 64], F32, dma=True)
    lamw = sb("lamw", [128, 2, 64], F32)
    lams = sb("lams", [128, 2], F32)
    neglam = sb("neglam", [128, 1], F32)
    glt = sb("glt", [128, 1], F32, dma=True)
    linit = sb("linit", [128, 2], F32, dma=True)
    selt = sb("selt", [128, 1], F32, dma=True)
    hmt = sb("hmt", [128, NPS], F32, dma=True)
    stt = sb("stt", [128, 4, 12], F32)
    mvt = sb("mvt", [128, 4, 2], F32)
    rs4 = sb("rs4", [128, 4, 1], F32)
    nm4 = sb("nm4", [128, 4, 1], F32)
    print("sbuf bytes remaining:", nc.sbuf_bytes_remaining)

    class _View:
        pass
    cache_ld = []
    for i in range(2):
        v = _View()
        v.t = x_tm.t[:, i, :]
        v.res = xres[i]
        v.ds = S.dsem(f"cld{i}")
        cache_ld.append(v)
    rview = uext.t[:].rearrange("p a b -> p (a b)")[:, 0:4 * D].rearrange("p (s d) -> p s d", s=4)

    banks = [Buf(S, nc.alloc_psum_tensor(f"bank{i}", [128, 512], F32), f"bank{i}") for i in range(8)]
    bank_rr = [0]

    def next_bank():
        b = banks[bank_rr[0] % 8]
        bank_rr[0] += 1
        return b

    S.dma("sp", ident.ds, ident.t[:], ident_d, writes=[ident.res])
    S.dma("sp", biast.ds, biast.t[:], bias_d, writes=[biast.res])
    S.dma("sp", mtab.ds, mtab.t[:], mtab_d, writes=[mtab.res])
    S.dma("sp", qrel.ds, qrel.t[:], qrel_d, writes=[qrel.res])
    S.dma("sp", ones0f.ds, ones0f.t[:], ones0_d, writes=[ones0f.res])
    S.dma("sp", linit.ds, linit.t[:], linit_d, writes=[linit.res])
    S.dma("sp", selt.ds, selt.t[:], sel_d, writes=[selt.res])
    S.dma("sp", hmt.ds, hmt.t[:], hm_d, writes=[hmt.res])
    S.dma("sp", glt.ds, glt.t[:], gsub, writes=[glt.res])
    S.dma("sp", ftab.ds, ftab.t[:], ftab_d, writes=[ftab.res])
    S.dma("sp", glb.ds, glb.t[:], gsub.rearrange("e o -> (e o)").partition_broadcast(128), writes=[glb.res])
    for i_ in range(NSLOT_KV):
        S.op("pool", lambda e, i_=i_: e.memset(Vc[i_].t[:, :, 128:129], 1.0), writes=[Vc[i_].res])
    S.op("dve", lambda e: e.memset(ones.t[:], 1.0), writes=[ones.res])
    S.op("dve", lambda e: e.memset(epst.t[:], LN_EPS), writes=[epst.res])
    S.op("dve", lambda e: e.tensor_copy(ones0.t[:], ones0f.t[:]), reads=[ones0f.res], writes=[ones0.res])
    S.op("pool", lambda e: e.memset(halo.t[:], 0.0), writes=[halo.res])
    for c in range(2):
        S.op("pool", lambda e, c=c: e.memset(QT[c].t[:], 0.0), writes=QTres)
    for j_ in range(3):
        S.dma("sp", cwt.ds, cwt.t[:, j_, :], w_cw[j_].rearrange("(dc p) -> p dc", p=128),
              writes=[cwt.res], allow_slow_non_contiguous=True)
    S.dma("sp", lamt.ds, lamt.t[:].rearrange("p b c -> p (b c)"),
          lam4.rearrange("b c -> (b c)").partition_broadcast(128), writes=[lamt.res])
    S.op("dve", lambda e: e.tensor_tensor(out=lamw.t[:], in0=lamt.t[:, 0:4:2, :], in1=lamt.t[:, 1:4:2, :], op=ALU.mult),
         reads=[lamt.res], writes=[lamw.res])
    S.op("dve", lambda e: e.tensor_reduce(out=lams.t[:], in_=lamw.t[:], axis=mybir.AxisListType.X, op=ALU.add),
         reads=[lamw.res], writes=[lams.res])
    S.op("act", lambda e: e.activation(out=lams.t[:], in_=lams.t[:], func=AF.Exp), reads=[lams.res], writes=[lams.res])
    S.op("dve", lambda e: e.scalar_tensor_tensor(out=neglam.t[:], in0=lams.t[:, 1:2], scalar=linit.t[:, 0:1],
                                                 in1=lams.t[:, 0:1], op0=ALU.add, op1=ALU.subtract),
         reads=[lams.res, linit.res], writes=[neglam.res])
    S.op("dve", lambda e: e.tensor_scalar(out=glt.t[:], in0=glt.t[:], scalar1=linit.t[:, 1:2], scalar2=None, op0=ALU.mult),
         reads=[glt.res, linit.res], writes=[glt.res])
    S.op("dve", lambda e: e.tensor_scalar(out=glb.t[:], in0=glb.t[:], scalar1=linit.t[:, 1:2], scalar2=None, op0=ALU.mult),
         reads=[glb.res, linit.res], writes=[glb.res])
    for r in (ident, biast, mtab, qrel, ones, ones0, epst, cwt, selt, hmt, neglam, ftab, glb):
        r.res.const = True

    layer_chunks = []
    wres = {}
    cast_jobs = []
    cast_state = {'n': 0}

    def emit_casts(upto):
        while cast_state['n'] < min(upto, len(cast_jobs)):
            ds_, dst_, src_, res_ = cast_jobs[cast_state['n']]
            S.dma('pool', ds_, dst_, src_, writes=[res_])
            cast_state['n'] += 1

    wds = S.dsem("wcast")
    for i in range(2):
        lst = []
        if i == 0:
            srcs = [(w_qkv[:, 0:D], "q"), (w_qkv[:, D:2 * D], "k"), (w_qkv[:, 2 * D:3 * D], "v"), (w_ao, "o")]
        else:
            srcs = [(w_ci[:, 2 * D:3 * D], "h"), (w_ci[:, D:2 * D], "c"), (w_ci[:, 0:D], "b"), (w_co, "o")]
        up = lambda j: (w_up[i, :, j * D:(j + 1) * D], f"u{j}")
        dn = lambda j: (w_dn[i, j * D:(j + 1) * D, :], f"d{j}")
        srcs += [up(0), up(1), dn(0), up(2), dn(1), up(3), dn(2), dn(3)]
        for src, nm in srcs:
            key = (i, nm)
            wscr[key] = dscr(f"w_{i}_{nm}", [128, 8, 1024])
            wres[key] = Res(f"w_{i}_{nm}")
            cast_jobs.append((S.dsem(f"wc_{i}_{nm}"), wscr[key], src.rearrange("(kc p) n -> p kc n", p=128), wres[key]))
            lst.append(key)
        layer_chunks.append(lst)

    steps = [("p", t) for t in range(n_psteps)] + [("s", j) for j in range(n_ssteps)]
    wseq = []
    for _ in steps:
        for i in range(2):
            wseq.extend(layer_chunks[i])
    wstate = {"loaded": 0, "use": 0}

    emit_casts(CAST_AHEAD)

    def wget():
        k = wstate["use"]
        wstate["use"] += 1
        emit_casts(k + 1 + CAST_AHEAD)
        while wstate["loaded"] < len(wseq) and wstate["loaded"] < k + NSLOT_W:
            j = wstate["loaded"]
            slot = ring[j % NSLOT_W]
            S.dma("sp", slot.ds, slot.t[:], wscr[wseq[j]], reads=[wres[wseq[j]]], writes=[slot.res])
            wstate["loaded"] += 1
        return ring[k % NSLOT_W]

    pro_blocks = []
    cv_jobs = []
    cv_state = {'n': 0}

    def emit_cv(count):
        while count > 0 and cv_state['n'] < len(cv_jobs):
            dst_, src_, res_ = cv_jobs[cv_state['n']]
            S.dma('pool', cvds, dst_, src_, writes=[res_])
            cv_state['n'] += 1
            count -= 1
        if cv_state['n'] == len(cv_jobs):
            for dst_, src_, res_ in cv_jobs:
                res_.w = (cvds.key, cvds.count)

    if n_ssteps:
        cvds = S.dsem("cvcast")
        for j in range(n_ssteps):
            for q4 in range(4):
                cv_jobs.append((V_s[j][:, q4 * 1024:(q4 + 1) * 1024, :].rearrange("h k e -> k h e"),
                                cv[j, q4 * 1024:(q4 + 1) * 1024, :].rearrange("k (h e) -> k h e", h=NH),
                                sres(("V", "s", j, "hist"))))
        for j in range(n_ssteps):
            for kb in range(PAST // 128):
                pro_blocks.append((j, kb))
    class _V2:
        def __init__(self, t, res):
            self.t, self.res = t, res
    pl_bufs = [(_V2(Ap.t[:, 0:512], Ap.res), _V2(Ap.t[:, 512:1024], Ap.res)),
               (_V2(At.t[:, 0:512], At.res), _V2(At.t[:, 512:1024], At.res))]
    pl_ds = [(S.dsem("pl00"), S.dsem("pl01")), (S.dsem("pl10"), S.dsem("pl11"))]
    kts = [sb(f"kts{i}", [128, NH, 128], BF16, dma=True) for i in range(2)]
    pro_state = {"n": 0, "loaded": []}

    def pro_prefetch():
        while len(pro_state["loaded"]) < 2 and pro_state["n"] < len(pro_blocks):
            j, kb = pro_blocks[pro_state["n"]]
            par = pro_state["n"] % 2
            pro_state["n"] += 1
            for half in range(2):
                buf = pl_bufs[par][half]
                S.dma("sp", pl_ds[par][half], buf.t[:, :], ck[j, kb * 128:(kb + 1) * 128, half * 512:(half + 1) * 512],
                      writes=[buf.res])
            pro_state["loaded"].append((j, kb, par))

    def pro_consume():
        for (j, kb, par) in pro_state["loaded"]:
            kt = kts[par]
            for half in range(2):
                buf = pl_bufs[par][half]
                b = next_bank()

                def tr(e, buf=buf, b=b):
                    ins = None
                    for hh in range(4):
                        ins = e.transpose(b.t[:, hh * 128:(hh + 1) * 128], buf.t[:, hh * 128:(hh + 1) * 128], ident.t[:])
                    return ins
                S.op("pe", tr, reads=[buf.res, ident.res], writes=[b.res])
                if half == 0:
                    S.op("act", lambda e, b=b, half=half, kt=kt: e.activation(
                        out=kt.t[:, half * 4:(half + 1) * 4, :],
                        in_=b.t[:].rearrange("p (h k) -> p h k", h=4), func=AF.Copy),
                        reads=[b.res], writes=[kt.res])
                else:
                    S.op("dve", lambda e, b=b, half=half, kt=kt: e.tensor_copy(
                        kt.t[:, half * 4:(half + 1) * 4, :],
                        b.t[:].rearrange("p (h k) -> p h k", h=4)),
                        reads=[b.res], writes=[kt.res])
            S.dma("sp", kt.ds, KT_s[j][:, :, kb * 128:(kb + 1) * 128].rearrange("h p k -> p h k"),
                  kt.t[:, :, :], reads=[kt.res], writes=[sres(("KT", "s", j, "hist"))])
        pro_state["loaded"] = []

    def pro_flush():
        while pro_state["n"] < len(pro_blocks) or pro_state["loaded"]:
            pro_prefetch()
            pro_consume()

    def make_xT(T, rows, nsub):
        for kc in range(8):
            b = next_bank()

            def tr(e, b=b, kc=kc):
                ins = None
                for s in range(nsub):
                    ins = e.transpose(b.t[:, s * 128:s * 128 + rows], x_tm.t[0:rows, s, kc * 128:(kc + 1) * 128],
                                      ident.t[0:rows, 0:rows])
                return ins
            S.op("pe", tr, reads=xres[:nsub] + [ident.res], writes=[b.res])
            if kc % 2 == 0:
                S.op("act", lambda e, b=b, kc=kc: e.activation(out=xT.t[:, kc, 0:T], in_=b.t[:, 0:T], func=AF.Copy),
                     reads=[b.res], writes=[xTres[kc]])
            else:
                S.op("dve", lambda e, b=b, kc=kc: e.tensor_copy(xT.t[:, kc, 0:T], b.t[:, 0:T]),
                     reads=[b.res], writes=[xTres[kc]])

    gb_state = {"n": 0}

    def layer_norm(idx, T, rows, nsub):
        gb = gbt[gb_state["n"] % 2]
        gb_state["n"] += 1
        S.dma("sp", gb.ds, gb.t[:, 0, :], ln_g[idx].partition_broadcast(128), writes=[gb.res])
        S.dma("sp", gb.ds, gb.t[:, 1, :], ln_b[idx].partition_broadcast(128), writes=[gb.res])
        for s in range(nsub):
            for hf in range(2):
                S.op("dve", lambda e, s=s, hf=hf: e.bn_stats(out=stt.t[0:rows, s, hf * 6:(hf + 1) * 6],
                                                            in_=x_tm.t[0:rows, s, hf * 512:(hf + 1) * 512]),
                     reads=[xres[s]], writes=[stt.res])
        for s in range(nsub):
            S.op("dve", lambda e, s=s: e.bn_aggr(out=mvt.t[0:rows, s, :], in_=stt.t[0:rows, s, :]),
                 reads=[stt.res], writes=[mvt.res])
        S.op("act", lambda e: e.activation(out=rs4.t[0:rows, 0:nsub, :], in_=mvt.t[0:rows, 0:nsub, 1:2], func=AF.Ln,
                                           bias=epst.t[0:rows, 0:1]),
             reads=[mvt.res, epst.res], writes=[rs4.res])
        S.op("act", lambda e: e.activation(out=rs4.t[0:rows, 0:nsub, :], in_=rs4.t[0:rows, 0:nsub, :], func=AF.Exp, scale=-0.5),
             reads=[rs4.res], writes=[rs4.res])
        S.op("dve", lambda e: e.scalar_tensor_tensor(out=nm4.t[0:rows, 0:nsub, :], in0=mvt.t[0:rows, 0:nsub, 0:1], scalar=-1.0,
                                                     in1=rs4.t[0:rows, 0:nsub, :], op0=ALU.mult, op1=ALU.mult),
             reads=[mvt.res, rs4.res], writes=[nm4.res])
        for s in range(nsub):
            S.op("act", lambda e, s=s: e.activation(out=x_tm.t[0:rows, s, :], in_=x_tm.t[0:rows, s, :], func=AF.Identity,
                                                    scale=rs4.t[0:rows, s, :], bias=nm4.t[0:rows, s, :]),
                 reads=[xres[s], rs4.res, nm4.res], writes=[xres[s]])
        for s in range(nsub):
            S.op("dve", lambda e, s=s, gb=gb: e.tensor_tensor(out=x_tm.t[0:rows, s, :], in0=x_tm.t[0:rows, s, :],
                                                             in1=gb.t[0:rows, 0, :], op=ALU.mult),
                 reads=[xres[s], gb.res], writes=[xres[s]])
            eng = "dve" if s % 2 == 0 else "pool"
            for hf in range(2):
                S.op(eng, lambda e, s=s, gb=gb, hf=hf: e.tensor_tensor(
                    out=x_tm.t[0:rows, s, hf * 512:(hf + 1) * 512], in0=x_tm.t[0:rows, s, hf * 512:(hf + 1) * 512],
                    in1=gb.t[0:rows, 1, hf * 512:(hf + 1) * 512], op=ALU.add),
                    reads=[xres[s], gb.res], writes=[xres[s]])

    def out_proj_residual(srcT, srcres, W, T, rows, nsub, first=True):
        for s in range(nsub):
            for n in range(2):
                b = next_bank()

                def mm(e, b=b, s=s, n=n):
                    ins = None
                    for k in range(8):
                        ins = e.matmul(b.t[0:rows, :], lhsT=srcT.t[:, k, s * 128:s * 128 + rows],
                                       rhs=W.t[:, k, n * 512:(n + 1) * 512], start=(k == 0), stop=(k == 7))
                    return ins
                S.op("pe", mm, reads=list(srcres) + [W.res], writes=[b.res])
                if first:
                    S.op("dve", lambda e, b=b, s=s, n=n: e.scalar_tensor_tensor(
                        out=x_tm.t[0:rows, s, n * 512:(n + 1) * 512], in0=x_tm.t[0:rows, s, n * 512:(n + 1) * 512],
                        scalar=ALPHA, in1=b.t[0:rows, :], op0=ALU.mult, op1=ALU.add),
                        reads=[b.res, xres[s]], writes=[xres[s]])
                else:
                    S.op("dve", lambda e, b=b, s=s, n=n: e.tensor_tensor(
                        out=x_tm.t[0:rows, s, n * 512:(n + 1) * 512], in0=x_tm.t[0:rows, s, n * 512:(n + 1) * 512],
                        in1=b.t[0:rows, :], op=ALU.add),
                        reads=[b.res, xres[s]], writes=[xres[s]])

    evac_rr = [0]

    def evac_copy(dst_ap, b, T, writes, src_ap=None):
        src = b.t[:, 0:T] if src_ap is None else src_ap
        evac_rr[0] += 1
        if evac_rr[0] % 2 == 0:
            S.op("act", lambda e: e.activation(out=dst_ap, in_=src, func=AF.Copy), reads=[b.res], writes=writes)
        else:
            S.op("dve", lambda e: e.tensor_copy(dst_ap, src), reads=[b.res], writes=writes)

    st_rr = [0]

    def proj_fm(W, T, dst_fn):
        xr = list(xTres)
        for h in range(NH):
            b = next_bank()

            def mm(e, b=b, h=h):
                ins = None
                for k in range(8):
                    ins = e.matmul(b.t[:, 0:T], lhsT=W.t[:, k, h * 128:(h + 1) * 128], rhs=xT.t[:, k, 0:T],
                                   start=(k == 0), stop=(k == 7))
                return ins
            S.op("pe", mm, reads=xr + [W.res], writes=[b.res])
            dst_fn(h, b)

    def attn_layer(step):
        kind, tidx = step
        if kind == "p":
            T, rows, nsub = TT, 128, 4
            t0 = tidx * TT
            nhist = t0
            KTd, Vd = KT_p, V_p
            kout = kp[t0:t0 + T, :]
            vout = vp[t0:t0 + T, :]
            own_key = ("p", tidx)
        else:
            T, rows, nsub = DEC, DEC, 1
            t0 = PAST
            nhist = PAST
            KTd, Vd = KT_s[tidx], V_s[tidx]
            kout = ks[tidx]
            vout = vs[tidx]
            own_key = ("s", tidx)
        xr = list(xTres)
        Wq = wget()

        def q_dst(h, b):
            S.op("act", lambda e: e.activation(out=QT[0].t[0:64, h, 0:T], in_=b.t[0:64, 0:T], func=AF.Copy),
                 reads=[b.res], writes=[QTres[h]])
            S.op("dve", lambda e: e.tensor_copy(QT[1].t[64:128, h, 0:T], b.t[64:128, 0:T]),
                 reads=[b.res], writes=[QTres[h]])
        proj_fm(Wq, T, q_dst)
        Wk = wget()
        proj_fm(Wk, T, lambda h, b: evac_copy(ktb.t[:, h, 0:T], b, T, [ktb.res]))
        ktres = sres(("KT",) + own_key)
        S.dma("sp", ktb.ds, KTd[:, :, t0:t0 + T].rearrange("h p k -> p h k"), ktb.t[:, :, 0:T],
              reads=[ktb.res], writes=[ktres])
        for s in range(nsub):
            for n in range(2):
                b = next_bank()

                def mm(e, b=b, s=s, n=n, W=Wk):
                    ins = None
                    for k in range(8):
                        ins = e.matmul(b.t[0:rows, :], lhsT=xT.t[:, k, s * 128:s * 128 + rows],
                                       rhs=W.t[:, k, n * 512:(n + 1) * 512], start=(k == 0), stop=(k == 7))
                    return ins
                S.op("pe", mm, reads=xr + [Wk.res], writes=[b.res])
                st = kst[st_rr[0] % 2]
                st_rr[0] += 1
                evac_copy(st.t[0:rows, :], b, 512, [st.res], src_ap=b.t[0:rows, :])
                S.dma("sp", st.ds, kout[s * 128:s * 128 + rows, n * 512:(n + 1) * 512], st.t[0:rows, :], reads=[st.res])
        Wv = wget()
        vres = sres(("V",) + own_key)
        for s in range(nsub):
            for n in range(2):
                b = next_bank()

                def mm(e, b=b, s=s, n=n, W=Wv):
                    ins = None
                    for k in range(8):
                        ins = e.matmul(b.t[0:rows, :], lhsT=xT.t[:, k, s * 128:s * 128 + rows],
                                       rhs=W.t[:, k, n * 512:(n + 1) * 512], start=(k == 0), stop=(k == 7))
                    return ins
                S.op("pe", mm, reads=xr + [Wv.res], writes=[b.res])
                st = vst[st_rr[0] % 2]
                vb = vbb[st_rr[0] % 2]
                st_rr[0] += 1
                S.op("act", lambda e, b=b, st=st: e.activation(out=st.t[0:rows, :], in_=b.t[0:rows, :], func=AF.Copy),
                     reads=[b.res], writes=[st.res])
                S.op("pool", lambda e, st=st, vb=vb: e.tensor_copy(vb.t[0:rows, :], st.t[0:rows, :]),
                     reads=[st.res], writes=[vb.res])
                S.dma("sp", st.ds, vout[s * 128:s * 128 + rows, n * 512:(n + 1) * 512], st.t[0:rows, :], reads=[st.res])
                S.dma("sp", vb.ds, Vd[n * 4:(n + 1) * 4, t0 + s * 128:t0 + s * 128 + rows, :].rearrange("h k e -> k h e"),
                      vb.t[0:rows, :].rearrange("k (h e) -> k h e", h=4), reads=[vb.res], writes=[vres])
        npast = nhist // 128
        keep = []
        for h in range(NH):
            kh = 0
            for r in range(1, npast + 1):
                if SLOPES[h] * (1 + (r - 1) * 128) <= ALIBI_THR:
                    kh = r
            keep.append(kh)
        loads = []
        head_chunks = []
        for h in range(NH):
            first_kb = npast - keep[h]
            chs = []
            if keep[h]:
                for c in range(first_kb // (KVC // 128), (npast + KVC // 128 - 1) // (KVC // 128)):
                    kb0 = max(c * (KVC // 128), first_kb)
                    kb1 = min((c + 1) * (KVC // 128), npast)
                    loads.append((h, "hist", kb0 * 128, (kb1 - kb0) * 128))
                    chs.append((kb0, kb1))
            head_chunks.append(chs)
            loads.append((h, "diag", t0, T))
        lstate = {"n": 0}

        def hist_res(kindKV, k0, n):
            if kind == "p":
                return [sres((kindKV, "p", tt)) for tt in range(k0 // TT, (k0 + n + TT - 1) // TT)]
            return [sres((kindKV, "s", tidx, "hist"))]

        def issue_load(idx):
            h, lk, k0, n = loads[idx]
            slot = idx % NSLOT_KV
            ktc, vc = KTc[slot], Vc[slot]
            if lk == "hist":
                rk, rv = hist_res("KT", k0, n), hist_res("V", k0, n)
            else:
                rk, rv = [ktres], [vres]
            S.dma("sp", ktc.ds, ktc.t[:, 0:n], KTd[h, :, k0:k0 + n], reads=rk, writes=[ktc.res])
            if n >= 128:
                S.dma("sp", vc.ds, vc.t[:, 0:n // 128, 0:128], Vd[h, k0:k0 + n, :].rearrange("(kb p) e -> p kb e", p=128),
                      reads=rv, writes=[vc.res])
            else:
                S.dma("sp", vc.ds, vc.t[0:n, 0, 0:128], Vd[h, k0:k0 + n, :], reads=rv, writes=[vc.res])

        def get_load(idx):
            while lstate["n"] < len(loads) and lstate["n"] < idx + NSLOT_KV - 1:
                issue_load(lstate["n"])
                lstate["n"] += 1
            slot = idx % NSLOT_KV
            return KTc[slot], Vc[slot]

        Sb = banks[0:4]

        blocks = []
        lidx = 0
        for h in range(NH):
            done = 0
            for (kb0, kb1) in head_chunks[h]:
                for kabs in range(kb0, kb1):
                    blocks.append(dict(h=h, typ="past", lidx=lidx, kb=kabs - kb0, kabs=kabs, first=(done == 0),
                                       last=(done == keep[h] - 1)))
                    done += 1
                lidx += 1
            ndb = (T + 127) // 128
            for j in range(ndb):
                blocks.append(dict(h=h, typ="diag", lidx=lidx, j=j, first=(j == 0), last=(j == ndb - 1), haspast=bool(keep[h])))
            lidx += 1

        def emit_qk(i, bl):
            ktc, vc = get_load(bl["lidx"])
            bl["ktc"], bl["vc"] = ktc, vc
            par = i % 2
            sb0, sb1 = Sb[par * 2], Sb[par * 2 + 1]
            bl["sb"] = (sb0, sb1)
            bl["p"] = (Pt[par * 2], Pt[par * 2 + 1])
            h = bl["h"]
            if bl["typ"] == "past":
                kb = bl["kb"]

                def qk(e):
                    e.matmul(sb0.t[:, 0:T], lhsT=ktc.t[:, kb * 128:(kb + 1) * 128], rhs=QT[0].t[:, h, 0:T],
                             start=True, stop=True)
                    return e.matmul(sb1.t[:, 0:T], lhsT=ktc.t[:, kb * 128:(kb + 1) * 128], rhs=QT[1].t[:, h, 0:T],
                                    start=True, stop=True)
            else:
                j = bl["j"]
                nk = min(128, T - j * 128)
                q0 = j * 128

                def qk(e):
                    e.matmul(sb0.t[0:nk, q0:T], lhsT=ktc.t[:, j * 128:j * 128 + nk], rhs=QT[0].t[:, h, q0:T],
                             start=True, stop=True)
                    return e.matmul(sb1.t[0:nk, q0:T], lhsT=ktc.t[:, j * 128:j * 128 + nk], rhs=QT[1].t[:, h, q0:T],
                                    start=True, stop=True)
            S.op("pe", qk, reads=[ktc.res, QTres[h]], writes=[sb0.res, sb1.res])

        def emit_exp(bl):
            h = bl["h"]
            slope = SLOPES[h]
            if bl["typ"] == "past":
                rel = bl["kabs"] - npast + 64
                for (sbx, px) in zip(bl["sb"], bl["p"]):
                    S.op("act", lambda e, sbx=sbx, px=px: e.activation(
                        out=px.t[:, 0:T], in_=sbx.t[:, 0:T], func=AF.Exp,
                        bias=biast.t[:, h * 64 + rel:h * 64 + rel + 1], scale=0.125),
                        reads=[sbx.res, biast.res], writes=[px.res])
            else:
                j = bl["j"]
                nk = min(128, T - j * 128)
                q0 = j * 128
                for ci, (sbx, px) in enumerate(zip(bl["sb"], bl["p"])):
                    dt_ = dtmp[ci]
                    S.op("dve", lambda e, sbx=sbx, dt_=dt_: e.scalar_tensor_tensor(
                        out=dt_.t[0:nk, q0:T], in0=mtab.t[0:nk, j * TT + q0:j * TT + T], scalar=slope,
                        in1=sbx.t[0:nk, q0:T], op0=ALU.mult, op1=ALU.add),
                        reads=[sbx.res, mtab.res], writes=[dt_.res])
                    S.op("act", lambda e, px=px, dt_=dt_: e.activation(
                        out=px.t[0:nk, q0:T], in_=dt_.t[0:nk, q0:T], func=AF.Exp, scale=0.125),
                        reads=[dt_.res], writes=[px.res])

        nqs = nsub
        OB = banks[4:7]
        TB = banks[7]

        def otile(t):
            return OB[t // 3], (t % 3) * 160

        def apv(buf, t):
            return buf.t[0:rows, t * VW:(t + 1) * VW]

        bank_started = {}

        def emit_pv(bl):
            vc = bl["vc"]
            pp = bl["p"]
            first, last = bl["first"], bl["last"]
            if bl["typ"] == "past":
                kb = bl["kb"]
                nk, qs0 = 128, 0
                if kind == "p" and bl["kabs"] < TT // 128:
                    pass
            else:
                kb = bl["j"]
                nk = min(128, T - kb * 128)
                qs0 = kb
            if first:
                bank_started.clear()
            use0 = (bl["typ"] == "past" and kind == "p" and bl["kabs"] < TT // 128)
            plan = []
            for c in range(2):
                for qs in range(qs0, nqs):
                    t = c * 4 + qs
                    bk, off = otile(t)
                    st = bk.name_ not in bank_started
                    bank_started[bk.name_] = True
                    plan.append((c, qs, bk, off, st))

            def pv(e):
                ins = None
                for (c, qs, bk, off, st) in plan:
                    q0 = qs * 128
                    qn = min(128, T - q0)
                    if use0:
                        ins = e.matmul(bk.t[0:qn, off:off + 128], lhsT=pp[c].t[0:nk, q0:q0 + qn], rhs=vc.t[0:nk, kb, 0:128],
                                       start=st, stop=last, skip_group_check=True)
                        ins = e.matmul(bk.t[0:qn, off + 128:off + VW], lhsT=pp[c].t[0:nk, q0:q0 + qn], rhs=ones0.t[0:nk, 0:1],
                                       start=False, stop=last, skip_group_check=True)
                    else:
                        ins = e.matmul(bk.t[0:qn, off:off + VW], lhsT=pp[c].t[0:nk, q0:q0 + qn], rhs=vc.t[0:nk, kb, :],
                                       start=st, stop=last, skip_group_check=True)
                return ins
            S.op("pe", pv, reads=[vc.res, pp[0].res, pp[1].res, ones0.res], writes=[bk.res for bk in OB])

        def emit_past_evac(h):
            for t in range(8):
                c, qs = t // 4, t % 4
                if qs >= nqs:
                    continue
                bk, off = otile(t)
                eng = "dve" if (t // 3) != 1 else "act"
                fcol = ftab.t[0:rows, h * 4 + qs:h * 4 + qs + 1]
                if eng == "dve":
                    S.op("dve", lambda e, t=t, bk=bk, off=off, fcol=fcol: e.tensor_scalar(
                        out=apv(Ap, t), in0=bk.t[0:rows, off:off + VW], scalar1=fcol, scalar2=None, op0=ALU.mult),
                        reads=[bk.res, ftab.res], writes=[Ap.res])
                else:
                    S.op("act", lambda e, t=t, bk=bk, off=off, fcol=fcol: e.activation(
                        out=apv(Ap, t), in_=bk.t[0:rows, off:off + VW], func=AF.Copy, scale=fcol),
                        reads=[bk.res, ftab.res], writes=[Ap.res])

        def emit_finalize(h, haspast):
            Onf = Onf2[h % 2]
            for t in range(8):
                c, qs = t // 4, t % 4
                if qs >= nqs:
                    continue
                bk, off = otile(t)
                if haspast:
                    S.op("dve", lambda e, t=t, bk=bk, off=off: e.tensor_tensor(
                        out=apv(At, t), in0=bk.t[0:rows, off:off + VW], in1=apv(Ap, t), op=ALU.add),
                        reads=[bk.res, Ap.res], writes=[At.res])
                elif (t // 3) != 1:
                    S.op("dve", lambda e, t=t, bk=bk, off=off: e.tensor_copy(apv(At, t), bk.t[0:rows, off:off + VW]),
                         reads=[bk.res], writes=[At.res])
                else:
                    S.op("act", lambda e, t=t, bk=bk, off=off: e.activation(out=apv(At, t), in_=bk.t[0:rows, off:off + VW],
                                                                           func=AF.Copy),
                         reads=[bk.res], writes=[At.res])
            Atv = At.t[0:rows, :].rearrange("p (t w) -> p t w", w=VW)
            S.op("dve", lambda e: e.reciprocal(rr.t[0:rows, :, :], Atv[:, :, 128:129]), reads=[At.res], writes=[rr.res])
            S.op("dve", lambda e: e.tensor_scalar(out=rr.t[0:rows, 4:8, :], in0=rr.t[0:rows, 4:8, :],
                                                  scalar1=neglam.t[0:rows, 0:1], scalar2=None, op0=ALU.mult),
                 reads=[rr.res, neglam.res], writes=[rr.res])
            for qs in range(nqs):
                S.op("act", lambda e, qs=qs: e.activation(out=Otm.t[0:rows, qs, :], in_=Atv[:, qs, 0:128], func=AF.Copy,
                                                         scale=rr.t[0:rows, qs, :]),
                     reads=[At.res, rr.res], writes=[Otm.res])
            for qs in range(nqs):
                S.op("dve", lambda e, qs=qs: e.scalar_tensor_tensor(out=Otm.t[0:rows, qs, :], in0=Atv[:, 4 + qs, 0:128],
                                                                   scalar=rr.t[0:rows, 4 + qs, :], in1=Otm.t[0:rows, qs, :],
                                                                   op0=ALU.mult, op1=ALU.add),
                     reads=[At.res, rr.res, Otm.res], writes=[Otm.res])
            for qs in range(nqs):
                S.op("act", lambda e, qs=qs: e.activation(out=Onf.t[0:rows, qs, :], in_=Otm.t[0:rows, qs, :], func=AF.Square,
                                                         accum_out=ss4.t[0:rows, qs, :]),
                     reads=[Otm.res], writes=[Onf.res, ss4.res])
            S.op("act", lambda e: e.activation(out=ss4.t[0:rows, 0:nqs, :], in_=ss4.t[0:rows, 0:nqs, :], func=AF.Ln,
                                               scale=1.0 / 128.0, bias=epst.t[0:rows, 0:1]),
                 reads=[ss4.res, epst.res], writes=[ss4.res])
            S.op("act", lambda e: e.activation(out=ss4.t[0:rows, 0:nqs, :], in_=ss4.t[0:rows, 0:nqs, :], func=AF.Exp, scale=-0.5),
                 reads=[ss4.res], writes=[ss4.res])
            for qs in range(nqs):
                S.op("dve", lambda e, qs=qs: e.scalar_tensor_tensor(out=Onf.t[0:rows, qs, :], in0=Otm.t[0:rows, qs, :],
                                                                   scalar=ss4.t[0:rows, qs, :], in1=glb.t[0:rows, :],
                                                                   op0=ALU.mult, op1=ALU.mult),
                     reads=[Otm.res, ss4.res, glb.res, Onf.res], writes=[Onf.res])


        def emit_head_transpose(h):
            Onf = Onf2[h % 2]
            def tr(e):
                ins = None
                for qs in range(nqs):
                    ins = e.transpose(TB.t[:, qs * 128:qs * 128 + rows], Onf.t[0:rows, qs, :], ident.t[0:rows, 0:rows])
                return ins
            S.op("pe", tr, reads=[Onf.res, ident.res], writes=[TB.res])
            if h % 2 == 0:
                S.op("act", lambda e: e.activation(out=OnT.t[:, h, 0:T], in_=TB.t[:, 0:T], func=AF.Copy),
                     reads=[TB.res], writes=[OnTres[h]])
            else:
                S.op("dve", lambda e: e.tensor_copy(OnT.t[:, h, 0:T], TB.t[:, 0:T]), reads=[TB.res], writes=[OnTres[h]])

        nb = len(blocks)
        pending = []
        emit_qk(0, blocks[0])
        for i, bl in enumerate(blocks):
            if i + 1 < nb:
                emit_qk(i + 1, blocks[i + 1])
            emit_exp(bl)
            emit_pv(bl)
            if bl["typ"] == "past" and bl["last"]:
                emit_past_evac(bl["h"])
            if bl["typ"] == "diag" and bl["last"]:
                while pending:
                    emit_head_transpose(pending.pop(0)[0])
                emit_finalize(bl["h"], bl["haspast"])
                pending.append((bl["h"], i))
        while pending:
            emit_head_transpose(pending.pop(0)[0])
        S.stage(6)
        Wo = wget()
        out_proj_residual(OnT, OnTres, Wo, T, rows, nsub, first=True)
        S.stage(7)

    def conv_layer(step):
        kind, tidx = step
        if kind == "p":
            T, rows, nsub = TT, 128, 4
        else:
            T, rows, nsub = DEC, DEC, 1
        hl = halo
        if kind == "s":
            for j_ in range(2):
                S.dma("sp", hl.ds, hl.t[:, :, j_], sc[tidx, j_].rearrange("(dc p) -> p dc", p=128), writes=[hl.res],
                      allow_slow_non_contiguous=True)
            S.op("pool", lambda e: e.tensor_copy(uext.t[:, :, 0:2], hl.t[:]), reads=[hl.res], writes=ures)
        else:
            S.op("pool", lambda e: e.tensor_scalar(out=uext.t[:, :, 0:2], in0=hl.t[:], scalar1=hmt.t[:, tidx:tidx + 1],
                                                   scalar2=None, op0=ALU.mult),
                 reads=[hl.res, hmt.res], writes=ures)
        Wh = wget()
        proj_fm(Wh, T, lambda dc, b: S.op("act", lambda e: e.activation(out=uext.t[:, dc, 2:2 + T], in_=b.t[:, 0:T],
                                                                         func=AF.Copy),
                                          reads=[b.res], writes=[ures[dc]]))
        Wc = wget()
        proj_fm(Wc, T, lambda dc, b: S.op("dve", lambda e: e.tensor_tensor(out=uext.t[:, dc, 2:2 + T], in0=b.t[:, 0:T],
                                                                            in1=uext.t[:, dc, 2:2 + T], op=ALU.mult),
                                          reads=[b.res, ures[dc]], writes=[ures[dc]]))
        S.op("pool", lambda e: e.tensor_copy(hl.t[:], uext.t[:, :, T:T + 2]), reads=ures, writes=[hl.res])
        dsts = []
        if kind == "p" and tidx == SEQ // TT - 1:
            dsts.append(cpo[0])
        if kind == "p" and tidx == SEQ // TT:
            dsts.append(cpo[1])
        if kind == "s":
            dsts.append(cs[tidx])
        for dst in dsts:
            for j_ in range(2):
                S.dma("sp", hl.ds, dst[j_].rearrange("(dc p) -> p dc", p=128), hl.t[:, :, j_], reads=[hl.res],
                      allow_slow_non_contiguous=True)
        Wb = wget()

        def b_dst(dc, b):
            z = zc[dc % 2]
            S.op("act", lambda e: e.activation(out=z.t[:, 0:T], in_=uext.t[:, dc, 0:T], func=AF.Copy,
                                               scale=cwt.t[:, 0, dc:dc + 1]),
                 reads=[ures[dc], cwt.res], writes=[z.res])
            for jj in (1, 2):
                S.op("dve", lambda e, jj=jj: e.scalar_tensor_tensor(
                    out=z.t[:, 0:T], in0=uext.t[:, dc, jj:jj + T], scalar=cwt.t[:, jj, dc:dc + 1],
                    in1=z.t[:, 0:T], op0=ALU.mult, op1=ALU.add),
                    reads=[ures[dc], cwt.res, z.res], writes=[z.res])
            S.op("dve", lambda e: e.tensor_tensor(out=OnT.t[:, dc, 0:T], in0=b.t[:, 0:T], in1=z.t[:, 0:T], op=ALU.mult),
                 reads=[b.res, z.res], writes=[OnTres[dc]])
        proj_fm(Wb, T, b_dst)
        Wo = wget()
        out_proj_residual(OnT, OnTres, Wo, T, rows, nsub, first=True)

    def mlp_layer(step):
        kind, tidx = step
        if kind == "p":
            T, rows, nsub = TT, 128, 4
        else:
            T, rows, nsub = DEC, DEC, 1

        def up(j):
            Wu = wget()
            hb = hT[j % 2]

            def dst(fc, b):
                rb = rbuf[fc % 2]
                hres = OnTres[fc] if hb is OnT else ktb.res
                S.op("act", lambda e: e.activation(out=rb.t[:, 0:T], in_=b.t[:, 0:T], func=AF.Relu),
                     reads=[b.res], writes=[rb.res])
                S.op("pool", lambda e: e.tensor_tensor(out=hb.t[:, fc, 0:T], in0=rb.t[:, 0:T], in1=rb.t[:, 0:T],
                                                       op=ALU.mult),
                     reads=[rb.res], writes=[hres])
            proj_fm(Wu, T, dst)

        def down(j):
            Wd = wget()
            hb = hT[j % 2]
            out_proj_residual(hb, OnTres if hb is OnT else [ktb.res], Wd, T, rows, nsub, first=(j == 0))
        up(0); up(1); down(0); up(2); down(1); up(3); down(2); down(3)

    xds = x_tm.ds
    ccds = S.dsem("cc")
    for si, step in enumerate(steps):
        kind, tidx = step
        if kind == "p":
            T, rows, nsub = TT, 128, 4
            src = xp[tidx * TT:(tidx + 1) * TT, :].rearrange("(s p) d -> p s d", p=128)
            S.dma("sp", xds, x_tm.t[:, :, :], src, writes=xres)
        else:
            T, rows, nsub = DEC, DEC, 1
            S.dma("sp", xds, x_tm.t[0:DEC, 0, :], xs[tidx], writes=xres)
        if exchange and si > 0:
            if kind == "p":
                S.dma("sp", uext.ds, rview, recv[0:TT, :].rearrange("(s p) d -> p s d", p=128), reads=[recv_res], writes=ures)
            else:
                S.dma("sp", uext.ds, rview[0:DEC, 0, :], recv[0:DEC, :], reads=[recv_res], writes=ures)
            for s in range(nsub):
                S.op("dve", lambda e, s=s, rows=rows: e.scalar_tensor_tensor(
                    out=x_tm.t[0:rows, s, :], in0=rview[0:rows, s, :], scalar=selt.t[0:rows, 0:1],
                    in1=x_tm.t[0:rows, s, :], op0=ALU.mult, op1=ALU.add),
                    reads=ures + [xres[s], selt.res], writes=[xres[s]])
        make_xT(T, rows, nsub)
        S.stage(3)
        pro = (kind == "p")
        attn_layer(step)
        if pro:
            pro_prefetch()
        layer_norm(0, T, rows, nsub)
        make_xT(T, rows, nsub)
        S.stage(8)
        mlp_layer(step)
        S.stage(9)
        if pro:
            pro_consume()
            pro_prefetch()
        layer_norm(1, T, rows, nsub)
        make_xT(T, rows, nsub)
        conv_layer(step)
        if pro:
            pro_consume()
            pro_prefetch()
        layer_norm(2, T, rows, nsub)
        make_xT(T, rows, nsub)
        mlp_layer(step)
        if pro:
            pro_consume()
        layer_norm(3, T, rows, nsub)
        if kind == "p":
            S.dma("sp", xds, yp[tidx * TT:(tidx + 1) * TT, :].rearrange("(s p) d -> p s d", p=128), x_tm.t[:, :, :],
                  reads=xres)
        else:
            S.dma("sp", xds, ys[tidx], x_tm.t[0:DEC, 0, :], reads=xres)
        if exchange and si < len(steps) - 1:
            if kind == "p":
                S.dma("sp", xds, send.rearrange("(s p) d -> p s d", p=128), x_tm.t[:, :, :], reads=xres, writes=[send_res])
            else:
                S.dma("sp", xds, send[0:DEC, :], x_tm.t[0:DEC, 0, :], reads=xres, writes=[send_res])
            S._deps("pool", [send_res], [recv_res])
            ccds.count += 1
            ev = (ccds.key, ccds.count)
            S.streams["pool"].append(("op", lambda e: e.collective_compute(
                "AllGather", ALU.bypass, replica_groups=PAIRS, ins=[send], outs=[recv]), ev[0], 1))
            S._record(ev, [send_res], [recv_res])
        if kind == "p":
            emit_cv(1)
            if si + 1 < len(steps) and steps[si + 1][0] == "s":
                pro_flush()
                emit_cv(10 ** 6)
    S.emit()
    return nc, S


def _tables():
    biast = np.zeros((128, NH * 64), np.float32)
    p = np.arange(128, dtype=np.float64)
    for h in range(NH):
        for r in range(64):
            rel = r - 64
            biast[:, h * 64 + r] = (SLOPES[h] * (rel * 128 + p)).astype(np.float32)
    mt = np.zeros((128, 4 * TT), np.float32)
    q = np.arange(TT)
    for j in range(4):
        k = j * 128 + np.arange(128)
        vis = (k[:, None] // 64) <= (q[None, :] // 64)
        m = -np.abs(q[None, :] - k[:, None]).astype(np.float64)
        mt[:, j * TT:(j + 1) * TT] = (8.0 * np.where(vis, m, -1.0e5)).astype(np.float32)
    qrel = np.broadcast_to(np.arange(TT, dtype=np.float32)[None, :], (128, TT)).copy()
    return biast, mt, qrel


def _ftab():
    ft = np.zeros((128, NH * 4), np.float32)
    p = np.arange(128, dtype=np.float64)
    for h in range(NH):
        for qs in range(4):
            ft[:, h * 4 + qs] = np.exp(-SLOPES[h] * (qs * 128 + p)).astype(np.float32)
    return ft


_CACHE = {}


def make_in_maps(x_prompt, x_sample, cache_k, cache_v, state_conv, w_qkv, lambda_q1, lambda_k1,
                 lambda_q2, lambda_k2, g_subln, w_attn_out, w_conv_in, w_conv, w_conv_out,
                 w_up, w_down, ln_g, ln_b):
    f = lambda a: np.ascontiguousarray(np.asarray(a, dtype=np.float32))
    biast, mt, qrel = _tables()
    xp = np.asarray(x_prompt); xs = np.asarray(x_sample)
    ck = np.asarray(cache_k).reshape(2, 8, PAST, D); cv = np.asarray(cache_v).reshape(2, 8, PAST, D)
    sc = np.asarray(state_conv)
    lng = np.asarray(ln_g); lnb = np.asarray(ln_b)
    lam = np.stack([np.asarray(lambda_q1), np.asarray(lambda_k1), np.asarray(lambda_q2), np.asarray(lambda_k2)], axis=1)
    in_maps = []
    for c in range(8):
        p, r = c // 2, c % 2
        m = {"ident": np.eye(128, dtype=np.float32), "biastab": biast, "mtab": mt, "qrel": qrel, "ftab": _ftab()}
        xpc = np.zeros((NPS * TT, D), np.float32)
        xsc = np.zeros((NSS, DEC, D), np.float32)
        ckc = np.zeros((NSS, PAST, D), np.float32)
        cvc = np.zeros((NSS, PAST, D), np.float32)
        scc = np.zeros((NSS, 2, D), np.float32)
        if r == 0:
            xpc[:SEQ] = xp[p]
            xsc[0:2] = xs[2 * p:2 * p + 2]
        for j in range(2):
            ckc[j + r] = ck[r, 2 * p + j]
            cvc[j + r] = cv[r, 2 * p + j]
            scc[j + r] = sc[r, 2 * p + j]
        m.update(xp=xpc, xs=xsc, ck=ckc, cv=cvc, sc=scc)
        m["w_qkv"] = f(np.asarray(w_qkv)[r]); m["w_ao"] = f(np.asarray(w_attn_out)[r])
        m["w_ci"] = f(np.asarray(w_conv_in)[r]); m["w_cw"] = f(np.asarray(w_conv)[r]); m["w_co"] = f(np.asarray(w_conv_out)[r])
        m["w_up"] = f(np.asarray(w_up)[2 * r:2 * r + 2]); m["w_dn"] = f(np.asarray(w_down)[2 * r:2 * r + 2])
        m["lam4"] = f(lam[r]); m["gsub"] = f(np.asarray(g_subln)[r].reshape(128, 1))
        m["ln_g"] = f(lng[2 * r:2 * r + 2].reshape(4, D)); m["ln_b"] = f(lnb[2 * r:2 * r + 2].reshape(4, D))
        li0 = lam_init_of(2 * r)
        m["linit"] = np.tile(np.array([[-li0, 1.0 - li0]], np.float32), (128, 1))
        m["sel"] = np.full((128, 1), float(r), np.float32)
        hm = np.ones((128, NPS), np.float32)
        hm[:, 0] = 0.0
        if r == 1:
            hm[:, 1] = 0.0
        m["hm"] = hm
        m["ones0"] = np.full((128, 128), 1.0 - r, np.float32)
        in_maps.append(m)
    return in_maps


def assemble(res):
    yp = np.zeros((4, SEQ, D), np.float32); ys = np.zeros((8, DEC, D), np.float32)
    kp = np.zeros((2, 4, SEQ, D), np.float32); vp = np.zeros((2, 4, SEQ, D), np.float32)
    cp = np.zeros((2, 4, 2, D), np.float32)
    ks = np.zeros((2, 8, DEC, D), np.float32); vs = np.zeros((2, 8, DEC, D), np.float32)
    cs = np.zeros((2, 8, 2, D), np.float32)
    for p in range(4):
        a, b = res[2 * p], res[2 * p + 1]
        yp[p] = b["yp"][TT:TT + SEQ]
        kp[0, p] = a["kp"][0:SEQ]; kp[1, p] = b["kp"][TT:TT + SEQ]
        vp[0, p] = a["vp"][0:SEQ]; vp[1, p] = b["vp"][TT:TT + SEQ]
        cp[0, p] = a["cpo"][0]; cp[1, p] = b["cpo"][1]
        for j in range(2):
            ys[2 * p + j] = b["ys"][1 + j]
            ks[0, 2 * p + j] = a["ks"][j]; ks[1, 2 * p + j] = b["ks"][1 + j]
            vs[0, 2 * p + j] = a["vs"][j]; vs[1, 2 * p + j] = b["vs"][1 + j]
            cs[0, 2 * p + j] = a["cs"][j]; cs[1, 2 * p + j] = b["cs"][1 + j]
    return (yp, ys, kp.reshape(2, 4, SEQ, NH, 2, 64), vp.reshape(2, 4, SEQ, NH, 128), cp,
            ks.reshape(2, 8, DEC, NH, 2, 64), vs.reshape(2, 8, DEC, NH, 128), cs)


def kernel(**inputs):
    if "nc" not in _CACHE:
        _CACHE["nc"] = build_program()[0]
    nc = _CACHE["nc"]
    in_maps = make_in_maps(**inputs)
    res = run_bass_kernel_spmd(nc, in_maps, core_ids=list(range(8))).results
    return assemble(res)
```

```python
import math
import os
import numpy as np
import concourse.bass as bass
import concourse.mybir as mybir
from concourse.bass_utils import run_bass_kernel_spmd

F32 = mybir.dt.float32
BF16 = mybir.dt.bfloat16
ALU = mybir.AluOpType
AF = mybir.ActivationFunctionType

D = 1024
SEQ = 8192
NH = 8
DEPTH = 4
PAST = 4096
DEC = 16
TT = 512
ALPHA = (2.0 * DEPTH) ** 0.25
LN_EPS = 1e-5
SLOPES = [2.0 ** (-8.0 * (h + 1) / NH) for h in range(NH)]
NSLOT_W = 3
NSLOT_KV = 4
VW = 129
CAST_AHEAD = 5
ALIBI_THR = 125.0
KVC = 1024


def lam_init_of(i):
    return 0.8 - 0.6 * math.exp(-0.3 * i)


ENGS = ("pe", "act", "dve", "pool", "sp")


class Res:
    __slots__ = ("name", "w", "r", "const")

    def __init__(self, name):
        self.name = name
        self.w = None
        self.r = {}
        self.const = False


class DSem:
    __slots__ = ("key", "count")

    def __init__(self, key):
        self.key = key
        self.count = 0


class Sched:
    def __init__(self, nc):
        self.nc = nc
        self.streams = {e: [] for e in ENGS}
        self.sems = {}
        self.ecnt = {e: 0 for e in ENGS}
        self.known = {e: {} for e in ENGS}
        self.selfsync = {"pe": False, "act": True, "dve": True, "pool": True, "sp": True}
        for e in ENGS:
            self._sem("E_" + e)
        self.dsems = []
        self.n_ops = 0
        self.dead = False
        self.stop = int(os.environ.get('K_STOP', '99'))

    def _sem(self, key):
        if key not in self.sems:
            self.sems[key] = self.nc.alloc_semaphore(name=key)
        return key

    def dsem(self, name):
        d = DSem(self._sem("D_" + name))
        self.dsems.append(d)
        return d

    def _wait(self, eng, ev):
        if ev is None:
            return
        key, val = ev
        if key == "E_" + eng and not self.selfsync[eng]:
            return
        k = self.known[eng]
        if k.get(key, 0) >= val:
            return
        k[key] = val
        self.streams[eng].append(("wait", key, val))

    def _deps(self, eng, reads, writes):
        for r in reads:
            self._wait(eng, r.w)
        for w in writes:
            self._wait(eng, w.w)
            for key, val in w.r.items():
                self._wait(eng, (key, val))

    def _record(self, ev, reads, writes):
        for r in reads:
            if r.const:
                continue
            if r.r.get(ev[0], 0) < ev[1]:
                r.r[ev[0]] = ev[1]
        for w in writes:
            w.w = ev
            w.r = {}

    def stage(self, n):
        if n >= self.stop:
            self.dead = True

    def op(self, eng, fn, reads=(), writes=()):
        if self.dead:
            return None
        self._deps(eng, reads, writes)
        self.ecnt[eng] += 1
        ev = ("E_" + eng, self.ecnt[eng])
        self.streams[eng].append(("op", fn, ev[0], 1))
        self._record(ev, reads, writes)
        self.n_ops += 1
        return ev

    def dma(self, eng, dsem, out, in_, reads=(), writes=(), **kw):
        if self.dead:
            return None
        self._deps(eng, reads, writes)
        dsem.count += 16
        ev = (dsem.key, dsem.count)

        def fn(e, out=out, in_=in_, kw=kw):
            return e.dma_start(out=out, in_=in_, **kw)
        self.streams[eng].append(("op", fn, ev[0], 16))
        self._record(ev, reads, writes)
        self.n_ops += 1
        return ev

    def finish(self):
        for d in self.dsems:
            if d.count:
                self._wait("sp", (d.key, d.count))
        for e in ENGS:
            if e != "sp" and self.ecnt[e]:
                self._wait("sp", ("E_" + e, self.ecnt[e]))

    def _replay(self, eng, e):
        for item in self.streams[eng]:
            if item[0] == "wait":
                e.wait_ge(self.sems[item[1]], item[2])
            else:
                _, fn, key, amt = item
                fn(e).then_inc(self.sems[key], amt)

    def emit(self):
        self.finish()
        with self.nc.Block() as block:
            @block.tensor
            def _(e):
                self._replay("pe", e)

            @block.scalar
            def _(e):
                self._replay("act", e)

            @block.vector
            def _(e):
                self._replay("dve", e)

            @block.gpsimd
            def _(e):
                self._replay("pool", e)

            @block.sync
            def _(e):
                self._replay("sp", e)


class Buf:
    def __init__(self, S, t, name, dma=False):
        self.t = t
        self.name_ = name
        self.res = Res(name)
        self.ds = S.dsem(name) if dma else None


NPS = SEQ // TT + 1
NSS = 3
PAIRS = [[0, 1], [2, 3], [4, 5], [6, 7]]


def build_program(n_psteps=NPS, n_ssteps=NSS, exchange=True):
    nc = bass.Bass("TRN2", target_bir_lowering=False)
    S = Sched(nc)

    def din(name, shape, dt=F32):
        return nc.dram_tensor(name, list(shape), dt, kind="ExternalInput").ap()

    def dout(name, shape, dt=F32):
        return nc.dram_tensor(name, list(shape), dt, kind="ExternalOutput").ap()

    def dscr(name, shape, dt=BF16):
        return nc.dram_tensor(name, list(shape), dt).ap()

    NROW = NPS * TT
    xp = din("xp", [NROW, D])
    xs = din("xs", [NSS, DEC, D])
    ck = din("ck", [NSS, PAST, D])
    cv = din("cv", [NSS, PAST, D])
    sc = din("sc", [NSS, 2, D])
    w_qkv = din("w_qkv", [D, 3 * D])
    w_ao = din("w_ao", [D, D])
    w_ci = din("w_ci", [D, 3 * D])
    w_cw = din("w_cw", [3, D])
    w_co = din("w_co", [D, D])
    w_up = din("w_up", [2, D, 4 * D])
    w_dn = din("w_dn", [2, 4 * D, D])
    lam4 = din("lam4", [4, 64])
    gsub = din("gsub", [128, 1])
    ln_g = din("ln_g", [4, D])
    ln_b = din("ln_b", [4, D])
    linit_d = din("linit", [128, 2])
    sel_d = din("sel", [128, 1])
    hm_d = din("hm", [128, NPS])
    ones0_d = din("ones0", [128, 128])
    ident_d = din("ident", [128, 128])
    bias_d = din("biastab", [128, NH * 64])
    mtab_d = din("mtab", [128, 4 * TT])
    qrel_d = din("qrel", [128, TT])
    ftab_d = din("ftab", [128, NH * 4])

    yp = dout("yp", [NROW, D])
    ys = dout("ys", [NSS, DEC, D])
    kp = dout("kp", [NROW, D])
    vp = dout("vp", [NROW, D])
    cpo = dout("cpo", [2, 2, D])
    ks = dout("ks", [NSS, DEC, D])
    vs = dout("vs", [NSS, DEC, D])
    cs = dout("cs", [NSS, 2, D])

    wscr = {}
    KT_p = dscr("KT_p", [NH, 128, NROW])
    V_p = dscr("V_p", [NH, NROW, 128])
    NKS = PAST + 128
    KT_s = [dscr(f"KT_s{j}", [NH, 128, NKS]) for j in range(NSS)]
    V_s = [dscr(f"V_s{j}", [NH, NKS, 128]) for j in range(NSS)]
    send = dscr("send", [TT, D], F32)
    recv = dscr("recv", [2 * TT, D], F32)
    send_res, recv_res = Res("send"), Res("recv")
    scr_res = {}

    def sres(key):
        if key not in scr_res:
            scr_res[key] = Res("scr" + str(key))
        return scr_res[key]

    A = nc.alloc_sbuf_tensor

    def sb(name, shape, dt=F32, dma=False):
        return Buf(S, A("sb_" + name, list(shape), dt), name, dma)

    x_tm = sb("x_tm", [128, 4, D], F32, dma=True)
    xres = [Res(f"x_tm{s}") for s in range(4)]
    xT = sb("xT", [128, 8, TT], BF16)
    xTres = [Res(f"xT{k}") for k in range(8)]
    ring = [sb(f"wring{i}", [128, 8, 1024], BF16, dma=True) for i in range(NSLOT_W)]
    QT = [sb(f"QT{c}", [128, NH, TT], BF16) for c in range(2)]
    QTres = [Res(f"QTh{h}") for h in range(NH)]
    KTc = [sb(f"KTc{i}", [128, KVC], BF16, dma=True) for i in range(NSLOT_KV)]
    Vc = [sb(f"Vc{i}", [128, KVC // 128, VW], BF16, dma=True) for i in range(NSLOT_KV)]
    Pt = [sb(f"P{i}", [128, TT], BF16) for i in range(4)]
    OnT = sb("OnT", [128, NH, TT], BF16)
    OnTres = [Res(f"OnT{h}") for h in range(NH)]
    Ap = sb("Ap", [128, 8 * VW], F32)
    At = sb("At", [128, 8 * VW], F32)
    Onf2 = [sb(f"Onf{i}", [128, 4, 128], F32) for i in range(2)]
    Otm = sb("Otm", [128, 4, 128], F32)
    rr = sb("rr", [128, 8, 1], F32)
    ss4 = sb("ss4", [128, 4, 1], F32)
    glb = sb("glb", [128, 128], F32, dma=True)
    ftab = sb("ftab", [128, NH * 4], F32, dma=True)
    tmpf = [sb(f"tmpf{i}", [128, TT], F32) for i in range(2)]
    dtmp = tmpf
    rbuf = tmpf
    zc = tmpf
    ktb = sb("ktb", [128, NH, TT], BF16, dma=True)
    hT = [OnT, ktb]
    biast = sb("biast", [128, NH * 64], F32, dma=True)
    mtab = sb("mtab", [128, 4 * TT], F32, dma=True)
    qrel = sb("qrel", [128, TT], F32, dma=True)
    ident = sb("ident", [128, 128], F32, dma=True)
    ones = sb("ones", [128, 128], BF16)
    ones0f = sb("ones0f", [128, 128], F32, dma=True)
    ones0 = sb("ones0", [128, 128], BF16)
    epst = sb("epst", [128, 1], F32)
    gbt = [sb(f"gb{i}", [128, 2, D], F32, dma=True) for i in range(2)]
    kst = [sb(f"kst{i}", [128, 512], F32, dma=True) for i in range(2)]
    vst = [sb(f"vst{i}", [128, 512], F32, dma=True) for i in range(2)]
    vbb = [sb(f"vbb{i}", [128, 512], BF16, dma=True) for i in range(2)]
    uext = sb("uext", [128, 8, TT + 4], F32, dma=True)
    ures = [Res(f"uext{k}") for k in range(8)]
    halo = sb("halo", [128, 8, 2], F32, dma=True)
    cwt = sb("cwt", [128, 3, 8], F32, dma=True)
    lamt = sb("lamt", [128, 4, 64], F32, dma=True)
    lamw = sb("lamw", [128, 2, 64], F32)
    lams = sb("lams", [128, 2], F32)
    neglam = sb("neglam", [128, 1], F32)
    glt = sb("glt", [128, 1], F32, dma=True)
    linit = sb("linit", [128, 2], F32, dma=True)
    selt = sb("selt", [128, 1], F32, dma=True)
    hmt = sb("hmt", [128, NPS], F32, dma=True)
    stt = sb("stt", [128, 4, 12], F32)
    mvt = sb("mvt", [128, 4, 2], F32)
    rs4 = sb("rs4", [128, 4, 1], F32)
    nm4 = sb("nm4", [128, 4, 1], F32)
    print("sbuf bytes remaining:", nc.sbuf_bytes_remaining)

    class _View:
        pass
    cache_ld = []
    for i in range(2):
        v = _View()
        v.t = x_tm.t[:, i, :]
        v.res = xres[i]
        v.ds = S.dsem(f"cld{i}")
        cache_ld.append(v)
    rview = uext.t[:].rearrange("p a b -> p (a b)")[:, 0:4 * D].rearrange("p (s d) -> p s d", s=4)

    banks = [Buf(S, nc.alloc_psum_tensor(f"bank{i}", [128, 512], F32), f"bank{i}") for i in range(8)]
    bank_rr = [0]

    def next_bank():
        b = banks[bank_rr[0] % 8]
        bank_rr[0] += 1
        return b

    S.dma("sp", ident.ds, ident.t[:], ident_d, writes=[ident.res])
    S.dma("sp", biast.ds, biast.t[:], bias_d, writes=[biast.res])
    S.dma("sp", mtab.ds, mtab.t[:], mtab_d, writes=[mtab.res])
    S.dma("sp", qrel.ds, qrel.t[:], qrel_d, writes=[qrel.res])
    S.dma("sp", ones0f.ds, ones0f.t[:], ones0_d, writes=[ones0f.res])
    S.dma("sp", linit.ds, linit.t[:], linit_d, writes=[linit.res])
    S.dma("sp", selt.ds, selt.t[:], sel_d, writes=[selt.res])
    S.dma("sp", hmt.ds, hmt.t[:], hm_d, writes=[hmt.res])
    S.dma("sp", glt.ds, glt.t[:], gsub, writes=[glt.res])
    S.dma("sp", ftab.ds, ftab.t[:], ftab_d, writes=[ftab.res])
    S.dma("sp", glb.ds, glb.t[:], gsub.rearrange("e o -> (e o)").partition_broadcast(128), writes=[glb.res])
    for i_ in range(NSLOT_KV):
        S.op("pool", lambda e, i_=i_: e.memset(Vc[i_].t[:, :, 128:129], 1.0), writes=[Vc[i_].res])
    S.op("dve", lambda e: e.memset(ones.t[:], 1.0), writes=[ones.res])
    S.op("dve", lambda e: e.memset(epst.t[:], LN_EPS), writes=[epst.res])
    S.op("dve", lambda e: e.tensor_copy(ones0.t[:], ones0f.t[:]), reads=[ones0f.res], writes=[ones0.res])
    S.op("pool", lambda e: e.memset(halo.t[:], 0.0), writes=[halo.res])
    for c in range(2):
        S.op("pool", lambda e, c=c: e.memset(QT[c].t[:], 0.0), writes=QTres)
    for j_ in range(3):
        S.dma("sp", cwt.ds, cwt.t[:, j_, :], w_cw[j_].rearrange("(dc p) -> p dc", p=128),
              writes=[cwt.res], allow_slow_non_contiguous=True)
    S.dma("sp", lamt.ds, lamt.t[:].rearrange("p b c -> p (b c)"),
          lam4.rearrange("b c -> (b c)").partition_broadcast(128), writes=[lamt.res])
    S.op("dve", lambda e: e.tensor_tensor(out=lamw.t[:], in0=lamt.t[:, 0:4:2, :], in1=lamt.t[:, 1:4:2, :], op=ALU.mult),
         reads=[lamt.res], writes=[lamw.res])
    S.op("dve", lambda e: e.tensor_reduce(out=lams.t[:], in_=lamw.t[:], axis=mybir.AxisListType.X, op=ALU.add),
         reads=[lamw.res], writes=[lams.res])
    S.op("act", lambda e: e.activation(out=lams.t[:], in_=lams.t[:], func=AF.Exp), reads=[lams.res], writes=[lams.res])
    S.op("dve", lambda e: e.scalar_tensor_tensor(out=neglam.t[:], in0=lams.t[:, 1:2], scalar=linit.t[:, 0:1],
                                                 in1=lams.t[:, 0:1], op0=ALU.add, op1=ALU.subtract),
         reads=[lams.res, linit.res], writes=[neglam.res])
    S.op("dve", lambda e: e.tensor_scalar(out=glt.t[:], in0=glt.t[:], scalar1=linit.t[:, 1:2], scalar2=None, op0=ALU.mult),
         reads=[glt.res, linit.res], writes=[glt.res])
    S.op("dve", lambda e: e.tensor_scalar(out=glb.t[:], in0=glb.t[:], scalar1=linit.t[:, 1:2], scalar2=None, op0=ALU.mult),
         reads=[glb.res, linit.res], writes=[glb.res])
    for r in (ident, biast, mtab, qrel, ones, ones0, epst, cwt, selt, hmt, neglam, ftab, glb):
        r.res.const = True

    layer_chunks = []
    wres = {}
    cast_jobs = []
    cast_state = {'n': 0}

    def emit_casts(upto):
        while cast_state['n'] < min(upto, len(cast_jobs)):
            ds_, dst_, src_, res_ = cast_jobs[cast_state['n']]
            S.dma('pool', ds_, dst_, src_, writes=[res_])
            cast_state['n'] += 1

    wds = S.dsem("wcast")
    for i in range(2):
        lst = []
        if i == 0:
            srcs = [(w_qkv[:, 0:D], "q"), (w_qkv[:, D:2 * D], "k"), (w_qkv[:, 2 * D:3 * D], "v"), (w_ao, "o")]
        else:
            srcs = [(w_ci[:, 2 * D:3 * D], "h"), (w_ci[:, D:2 * D], "c"), (w_ci[:, 0:D], "b"), (w_co, "o")]
        up = lambda j: (w_up[i, :, j * D:(j + 1) * D], f"u{j}")
        dn = lambda j: (w_dn[i, j * D:(j + 1) * D, :], f"d{j}")
        srcs += [up(0), up(1), dn(0), up(2), dn(1), up(3), dn(2), dn(3)]
        for src, nm in srcs:
            key = (i, nm)
            wscr[key] = dscr(f"w_{i}_{nm}", [128, 8, 1024])
            wres[key] = Res(f"w_{i}_{nm}")
            cast_jobs.append((S.dsem(f"wc_{i}_{nm}"), wscr[key], src.rearrange("(kc p) n -> p kc n", p=128), wres[key]))
            lst.append(key)
        layer_chunks.append(lst)

    steps = [("p", t) for t in range(n_psteps)] + [("s", j) for j in range(n_ssteps)]
    wseq = []
    for _ in steps:
        for i in range(2):
            wseq.extend(layer_chunks[i])
    wstate = {"loaded": 0, "use": 0}

    emit_casts(CAST_AHEAD)

    def wget():
        k = wstate["use"]
        wstate["use"] += 1
        emit_casts(k + 1 + CAST_AHEAD)
        while wstate["loaded"] < len(wseq) and wstate["loaded"] < k + NSLOT_W:
            j = wstate["loaded"]
            slot = ring[j % NSLOT_W]
            S.dma("sp", slot.ds, slot.t[:], wscr[wseq[j]], reads=[wres[wseq[j]]], writes=[slot.res])
            wstate["loaded"] += 1
        return ring[k % NSLOT_W]

    pro_blocks = []
    cv_jobs = []
    cv_state = {'n': 0}

    def emit_cv(count):
        while count > 0 and cv_state['n'] < len(cv_jobs):
            dst_, src_, res_ = cv_jobs[cv_state['n']]
            S.dma('pool', cvds, dst_, src_, writes=[res_])
            cv_state['n'] += 1
            count -= 1
        if cv_state['n'] == len(cv_jobs):
            for dst_, src_, res_ in cv_jobs:
                res_.w = (cvds.key, cvds.count)

    if n_ssteps:
        cvds = S.dsem("cvcast")
        for j in range(n_ssteps):
            for q4 in range(4):
                cv_jobs.append((V_s[j][:, q4 * 1024:(q4 + 1) * 1024, :].rearrange("h k e -> k h e"),
                                cv[j, q4 * 1024:(q4 + 1) * 1024, :].rearrange("k (h e) -> k h e", h=NH),
                                sres(("V", "s", j, "hist"))))
        for j in range(n_ssteps):
            for kb in range(PAST // 128):
                pro_blocks.append((j, kb))
    class _V2:
        def __init__(self, t, res):
            self.t, self.res = t, res
    pl_bufs = [(_V2(Ap.t[:, 0:512], Ap.res), _V2(Ap.t[:, 512:1024], Ap.res)),
               (_V2(At.t[:, 0:512], At.res), _V2(At.t[:, 512:1024], At.res))]
    pl_ds = [(S.dsem("pl00"), S.dsem("pl01")), (S.dsem("pl10"), S.dsem("pl11"))]
    kts = [sb(f"kts{i}", [128, NH, 128], BF16, dma=True) for i in range(2)]
    pro_state = {"n": 0, "loaded": []}

    def pro_prefetch():
        while len(pro_state["loaded"]) < 2 and pro_state["n"] < len(pro_blocks):
            j, kb = pro_blocks[pro_state["n"]]
            par = pro_state["n"] % 2
            pro_state["n"] += 1
            for half in range(2):
                buf = pl_bufs[par][half]
                S.dma("sp", pl_ds[par][half], buf.t[:, :], ck[j, kb * 128:(kb + 1) * 128, half * 512:(half + 1) * 512],
                      writes=[buf.res])
            pro_state["loaded"].append((j, kb, par))

    def pro_consume():
        for (j, kb, par) in pro_state["loaded"]:
            kt = kts[par]
            for half in range(2):
                buf = pl_bufs[par][half]
                b = next_bank()

                def tr(e, buf=buf, b=b):
                    ins = None
                    for hh in range(4):
                        ins = e.transpose(b.t[:, hh * 128:(hh + 1) * 128], buf.t[:, hh * 128:(hh + 1) * 128], ident.t[:])
                    return ins
                S.op("pe", tr, reads=[buf.res, ident.res], writes=[b.res])
                if half == 0:
                    S.op("act", lambda e, b=b, half=half, kt=kt: e.activation(
                        out=kt.t[:, half * 4:(half + 1) * 4, :],
                        in_=b.t[:].rearrange("p (h k) -> p h k", h=4), func=AF.Copy),
                        reads=[b.res], writes=[kt.res])
                else:
                    S.op("dve", lambda e, b=b, half=half, kt=kt: e.tensor_copy(
                        kt.t[:, half * 4:(half + 1) * 4, :],
                        b.t[:].rearrange("p (h k) -> p h k", h=4)),
                        reads=[b.res], writes=[kt.res])
            S.dma("sp", kt.ds, KT_s[j][:, :, kb * 128:(kb + 1) * 128].rearrange("h p k -> p h k"),
                  kt.t[:, :, :], reads=[kt.res], writes=[sres(("KT", "s", j, "hist"))])
        pro_state["loaded"] = []

    def pro_flush():
        while pro_state["n"] < len(pro_blocks) or pro_state["loaded"]:
            pro_prefetch()
            pro_consume()

    def make_xT(T, rows, nsub):
        for kc in range(8):
            b = next_bank()

            def tr(e, b=b, kc=kc):
                ins = None
                for s in range(nsub):
                    ins = e.transpose(b.t[:, s * 128:s * 128 + rows], x_tm.t[0:rows, s, kc * 128:(kc + 1) * 128],
                                      ident.t[0:rows, 0:rows])
                return ins
            S.op("pe", tr, reads=xres[:nsub] + [ident.res], writes=[b.res])
            if kc % 2 == 0:
                S.op("act", lambda e, b=b, kc=kc: e.activation(out=xT.t[:, kc, 0:T], in_=b.t[:, 0:T], func=AF.Copy),
                     reads=[b.res], writes=[xTres[kc]])
            else:
                S.op("dve", lambda e, b=b, kc=kc: e.tensor_copy(xT.t[:, kc, 0:T], b.t[:, 0:T]),
                     reads=[b.res], writes=[xTres[kc]])

    gb_state = {"n": 0}

    def layer_norm(idx, T, rows, nsub):
        gb = gbt[gb_state["n"] % 2]
        gb_state["n"] += 1
        S.dma("sp", gb.ds, gb.t[:, 0, :], ln_g[idx].partition_broadcast(128), writes=[gb.res])
        S.dma("sp", gb.ds, gb.t[:, 1, :], ln_b[idx].partition_broadcast(128), writes=[gb.res])
        for s in range(nsub):
            for hf in range(2):
                S.op("dve", lambda e, s=s, hf=hf: e.bn_stats(out=stt.t[0:rows, s, hf * 6:(hf + 1) * 6],
                                                            in_=x_tm.t[0:rows, s, hf * 512:(hf + 1) * 512]),
                     reads=[xres[s]], writes=[stt.res])
        for s in range(nsub):
            S.op("dve", lambda e, s=s: e.bn_aggr(out=mvt.t[0:rows, s, :], in_=stt.t[0:rows, s, :]),
                 reads=[stt.res], writes=[mvt.res])
        S.op("act", lambda e: e.activation(out=rs4.t[0:rows, 0:nsub, :], in_=mvt.t[0:rows, 0:nsub, 1:2], func=AF.Ln,
                                           bias=epst.t[0:rows, 0:1]),
             reads=[mvt.res, epst.res], writes=[rs4.res])
        S.op("act", lambda e: e.activation(out=rs4.t[0:rows, 0:nsub, :], in_=rs4.t[0:rows, 0:nsub, :], func=AF.Exp, scale=-0.5),
             reads=[rs4.res], writes=[rs4.res])
        S.op("dve", lambda e: e.scalar_tensor_tensor(out=nm4.t[0:rows, 0:nsub, :], in0=mvt.t[0:rows, 0:nsub, 0:1], scalar=-1.0,
                                                     in1=rs4.t[0:rows, 0:nsub, :], op0=ALU.mult, op1=ALU.mult),
             reads=[mvt.res, rs4.res], writes=[nm4.res])
        for s in range(nsub):
            S.op("act", lambda e, s=s: e.activation(out=x_tm.t[0:rows, s, :], in_=x_tm.t[0:rows, s, :], func=AF.Identity,
                                                    scale=rs4.t[0:rows, s, :], bias=nm4.t[0:rows, s, :]),
                 reads=[xres[s], rs4.res, nm4.res], writes=[xres[s]])
        for s in range(nsub):
            S.op("dve", lambda e, s=s, gb=gb: e.tensor_tensor(out=x_tm.t[0:rows, s, :], in0=x_tm.t[0:rows, s, :],
                                                             in1=gb.t[0:rows, 0, :], op=ALU.mult),
                 reads=[xres[s], gb.res], writes=[xres[s]])
            eng = "dve" if s % 2 == 0 else "pool"
            for hf in range(2):
                S.op(eng, lambda e, s=s, gb=gb, hf=hf: e.tensor_tensor(
                    out=x_tm.t[0:rows, s, hf * 512:(hf + 1) * 512], in0=x_tm.t[0:rows, s, hf * 512:(hf + 1) * 512],
                    in1=gb.t[0:rows, 1, hf * 512:(hf + 1) * 512], op=ALU.add),
                    reads=[xres[s], gb.res], writes=[xres[s]])

    def out_proj_residual(srcT, srcres, W, T, rows, nsub, first=True):
        for s in range(nsub):
            for n in range(2):
                b = next_bank()

                def mm(e, b=b, s=s, n=n):
                    ins = None
                    for k in range(8):
                        ins = e.matmul(b.t[0:rows, :], lhsT=srcT.t[:, k, s * 128:s * 128 + rows],
                                       rhs=W.t[:, k, n * 512:(n + 1) * 512], start=(k == 0), stop=(k == 7))
                    return ins
                S.op("pe", mm, reads=list(srcres) + [W.res], writes=[b.res])
                if first:
                    S.op("dve", lambda e, b=b, s=s, n=n: e.scalar_tensor_tensor(
                        out=x_tm.t[0:rows, s, n * 512:(n + 1) * 512], in0=x_tm.t[0:rows, s, n * 512:(n + 1) * 512],
                        scalar=ALPHA, in1=b.t[0:rows, :], op0=ALU.mult, op1=ALU.add),
                        reads=[b.res, xres[s]], writes=[xres[s]])
                else:
                    S.op("dve", lambda e, b=b, s=s, n=n: e.tensor_tensor(
                        out=x_tm.t[0:rows, s, n * 512:(n + 1) * 512], in0=x_tm.t[0:rows, s, n * 512:(n + 1) * 512],
                        in1=b.t[0:rows, :], op=ALU.add),
                        reads=[b.res, xres[s]], writes=[xres[s]])

    evac_rr = [0]

    def evac_copy(dst_ap, b, T, writes, src_ap=None):
        src = b.t[:, 0:T] if src_ap is None else src_ap
        evac_rr[0] += 1
        if evac_rr[0] % 2 == 0:
            S.op("act", lambda e: e.activation(out=dst_ap, in_=src, func=AF.Copy), reads=[b.res], writes=writes)
        else:
            S.op("dve", lambda e: e.tensor_copy(dst_ap, src), reads=[b.res], writes=writes)

    st_rr = [0]

    def proj_fm(W, T, dst_fn):
        xr = list(xTres)
        for h in range(NH):
            b = next_bank()

            def mm(e, b=b, h=h):
                ins = None
                for k in range(8):
                    ins = e.matmul(b.t[:, 0:T], lhsT=W.t[:, k, h * 128:(h + 1) * 128], rhs=xT.t[:, k, 0:T],
                                   start=(k == 0), stop=(k == 7))
                return ins
            S.op("pe", mm, reads=xr + [W.res], writes=[b.res])
            dst_fn(h, b)

    def attn_layer(step):
        kind, tidx = step
        if kind == "p":
            T, rows, nsub = TT, 128, 4
            t0 = tidx * TT
            nhist = t0
            KTd, Vd = KT_p, V_p
            kout = kp[t0:t0 + T, :]
            vout = vp[t0:t0 + T, :]
            own_key = ("p", tidx)
        else:
            T, rows, nsub = DEC, DEC, 1
            t0 = PAST
            nhist = PAST
            KTd, Vd = KT_s[tidx], V_s[tidx]
            kout = ks[tidx]
            vout = vs[tidx]
            own_key = ("s", tidx)
        xr = list(xTres)
        Wq = wget()

        def q_dst(h, b):
            S.op("act", lambda e: e.activation(out=QT[0].t[0:64, h, 0:T], in_=b.t[0:64, 0:T], func=AF.Copy),
                 reads=[b.res], writes=[QTres[h]])
            S.op("dve", lambda e: e.tensor_copy(QT[1].t[64:128, h, 0:T], b.t[64:128, 0:T]),
                 reads=[b.res], writes=[QTres[h]])
        proj_fm(Wq, T, q_dst)
        Wk = wget()
        proj_fm(Wk, T, lambda h, b: evac_copy(ktb.t[:, h, 0:T], b, T, [ktb.res]))
        ktres = sres(("KT",) + own_key)
        S.dma("sp", ktb.ds, KTd[:, :, t0:t0 + T].rearrange("h p k -> p h k"), ktb.t[:, :, 0:T],
              reads=[ktb.res], writes=[ktres])
        for s in range(nsub):
            for n in range(2):
                b = next_bank()

                def mm(e, b=b, s=s, n=n, W=Wk):
                    ins = None
                    for k in range(8):
                        ins = e.matmul(b.t[0:rows, :], lhsT=xT.t[:, k, s * 128:s * 128 + rows],
                                       rhs=W.t[:, k, n * 512:(n + 1) * 512], start=(k == 0), stop=(k == 7))
                    return ins
                S.op("pe", mm, reads=xr + [Wk.res], writes=[b.res])
                st = kst[st_rr[0] % 2]
                st_rr[0] += 1
                evac_copy(st.t[0:rows, :], b, 512, [st.res], src_ap=b.t[0:rows, :])
                S.dma("sp", st.ds, kout[s * 128:s * 128 + rows, n * 512:(n + 1) * 512], st.t[0:rows, :], reads=[st.res])
        Wv = wget()
        vres = sres(("V",) + own_key)
        for s in range(nsub):
            for n in range(2):
                b = next_bank()

                def mm(e, b=b, s=s, n=n, W=Wv):
                    ins = None
                    for k in range(8):
                        ins = e.matmul(b.t[0:rows, :], lhsT=xT.t[:, k, s * 128:s * 128 + rows],
                                       rhs=W.t[:, k, n * 512:(n + 1) * 512], start=(k == 0), stop=(k == 7))
                    return ins
                S.op("pe", mm, reads=xr + [Wv.res], writes=[b.res])
                st = vst[st_rr[0] % 2]
                vb = vbb[st_rr[0] % 2]
                st_rr[0] += 1
                S.op("act", lambda e, b=b, st=st: e.activation(out=st.t[0:rows, :], in_=b.t[0:rows, :], func=AF.Copy),
                     reads=[b.res], writes=[st.res])
                S.op("pool", lambda e, st=st, vb=vb: e.tensor_copy(vb.t[0:rows, :], st.t[0:rows, :]),
                     reads=[st.res], writes=[vb.res])
                S.dma("sp", st.ds, vout[s * 128:s * 128 + rows, n * 512:(n + 1) * 512], st.t[0:rows, :], reads=[st.res])
                S.dma("sp", vb.ds, Vd[n * 4:(n + 1) * 4, t0 + s * 128:t0 + s * 128 + rows, :].rearrange("h k e -> k h e"),
                      vb.t[0:rows, :].rearrange("k (h e) -> k h e", h=4), reads=[vb.res], writes=[vres])
        npast = nhist // 128
        keep = []
        for h in range(NH):
            kh = 0
            for r in range(1, npast + 1):
                if SLOPES[h] * (1 + (r - 1) * 128) <= ALIBI_THR:
                    kh = r
            keep.append(kh)
        loads = []
        head_chunks = []
        for h in range(NH):
            first_kb = npast - keep[h]
            chs = []
            if keep[h]:
                for c in range(first_kb // (KVC // 128), (npast + KVC // 128 - 1) // (KVC // 128)):
                    kb0 = max(c * (KVC // 128), first_kb)
                    kb1 = min((c + 1) * (KVC // 128), npast)
                    loads.append((h, "hist", kb0 * 128, (kb1 - kb0) * 128))
                    chs.append((kb0, kb1))
            head_chunks.append(chs)
            loads.append((h, "diag", t0, T))
        lstate = {"n": 0}

        def hist_res(kindKV, k0, n):
            if kind == "p":
                return [sres((kindKV, "p", tt)) for tt in range(k0 // TT, (k0 + n + TT - 1) // TT)]
            return [sres((kindKV, "s", tidx, "hist"))]

        def issue_load(idx):
            h, lk, k0, n = loads[idx]
            slot = idx % NSLOT_KV
            ktc, vc = KTc[slot], Vc[slot]
            if lk == "hist":
                rk, rv = hist_res("KT", k0, n), hist_res("V", k0, n)
            else:
                rk, rv = [ktres], [vres]
            S.dma("sp", ktc.ds, ktc.t[:, 0:n], KTd[h, :, k0:k0 + n], reads=rk, writes=[ktc.res])
            if n >= 128:
                S.dma("sp", vc.ds, vc.t[:, 0:n // 128, 0:128], Vd[h, k0:k0 + n, :].rearrange("(kb p) e -> p kb e", p=128),
                      reads=rv, writes=[vc.res])
            else:
                S.dma("sp", vc.ds, vc.t[0:n, 0, 0:128], Vd[h, k0:k0 + n, :], reads=rv, writes=[vc.res])

        def get_load(idx):
            while lstate["n"] < len(loads) and lstate["n"] < idx + NSLOT_KV - 1:
                issue_load(lstate["n"])
                lstate["n"] += 1
            slot = idx % NSLOT_KV
            return KTc[slot], Vc[slot]

        Sb = banks[0:4]

        blocks = []
        lidx = 0
        for h in range(NH):
            done = 0
            for (kb0, kb1) in head_chunks[h]:
                for kabs in range(kb0, kb1):
                    blocks.append(dict(h=h, typ="past", lidx=lidx, kb=kabs - kb0, kabs=kabs, first=(done == 0),
                                       last=(done == keep[h] - 1)))
                    done += 1
                lidx += 1
            ndb = (T + 127) // 128
            for j in range(ndb):
                blocks.append(dict(h=h, typ="diag", lidx=lidx, j=j, first=(j == 0), last=(j == ndb - 1), haspast=bool(keep[h])))
            lidx += 1

        def emit_qk(i, bl):
            ktc, vc = get_load(bl["lidx"])
            bl["ktc"], bl["vc"] = ktc, vc
            par = i % 2
            sb0, sb1 = Sb[par * 2], Sb[par * 2 + 1]
            bl["sb"] = (sb0, sb1)
            bl["p"] = (Pt[par * 2], Pt[par * 2 + 1])
            h = bl["h"]
            if bl["typ"] == "past":
                kb = bl["kb"]

                def qk(e):
                    e.matmul(sb0.t[:, 0:T], lhsT=ktc.t[:, kb * 128:(kb + 1) * 128], rhs=QT[0].t[:, h, 0:T],
                             start=True, stop=True)
                    return e.matmul(sb1.t[:, 0:T], lhsT=ktc.t[:, kb * 128:(kb + 1) * 128], rhs=QT[1].t[:, h, 0:T],
                                    start=True, stop=True)
            else:
                j = bl["j"]
                nk = min(128, T - j * 128)
                q0 = j * 128

                def qk(e):
                    e.matmul(sb0.t[0:nk, q0:T], lhsT=ktc.t[:, j * 128:j * 128 + nk], rhs=QT[0].t[:, h, q0:T],
                             start=True, stop=True)
                    return e.matmul(sb1.t[0:nk, q0:T], lhsT=ktc.t[:, j * 128:j * 128 + nk], rhs=QT[1].t[:, h, q0:T],
                                    start=True, stop=True)
            S.op("pe", qk, reads=[ktc.res, QTres[h]], writes=[sb0.res, sb1.res])

        def emit_exp(bl):
            h = bl["h"]
            slope = SLOPES[h]
            if bl["typ"] == "past":
                rel = bl["kabs"] - npast + 64
                for (sbx, px) in zip(bl["sb"], bl["p"]):
                    S.op("act", lambda e, sbx=sbx, px=px: e.activation(
                        out=px.t[:, 0:T], in_=sbx.t[:, 0:T], func=AF.Exp,
                        bias=biast.t[:, h * 64 + rel:h * 64 + rel + 1], scale=0.125),
                        reads=[sbx.res, biast.res], writes=[px.res])
            else:
                j = bl["j"]
                nk = min(128, T - j * 128)
                q0 = j * 128
                for ci, (sbx, px) in enumerate(zip(bl["sb"], bl["p"])):
                    dt_ = dtmp[ci]
                    S.op("dve", lambda e, sbx=sbx, dt_=dt_: e.scalar_tensor_tensor(
                        out=dt_.t[0:nk, q0:T], in0=mtab.t[0:nk, j * TT + q0:j * TT + T], scalar=slope,
                        in1=sbx.t[0:nk, q0:T], op0=ALU.mult, op1=ALU.add),
                        reads=[sbx.res, mtab.res], writes=[dt_.res])
                    S.op("act", lambda e, px=px, dt_=dt_: e.activation(
                        out=px.t[0:nk, q0:T], in_=dt_.t[0:nk, q0:T], func=AF.Exp, scale=0.125),
                        reads=[dt_.res], writes=[px.res])

        nqs = nsub
        OB = banks[4:7]
        TB = banks[7]

        def otile(t):
            return OB[t // 3], (t % 3) * 160

        def apv(buf, t):
            return buf.t[0:rows, t * VW:(t + 1) * VW]

        bank_started = {}

        def emit_pv(bl):
            vc = bl["vc"]
            pp = bl["p"]
            first, last = bl["first"], bl["last"]
            if bl["typ"] == "past":
                kb = bl["kb"]
                nk, qs0 = 128, 0
                if kind == "p" and bl["kabs"] < TT // 128:
                    pass
            else:
                kb = bl["j"]
                nk = min(128, T - kb * 128)
                qs0 = kb
            if first:
                bank_started.clear()
            use0 = (bl["typ"] == "past" and kind == "p" and bl["kabs"] < TT // 128)
            plan = []
            for c in range(2):
                for qs in range(qs0, nqs):
                    t = c * 4 + qs
                    bk, off = otile(t)
                    st = bk.name_ not in bank_started
                    bank_started[bk.name_] = True
                    plan.append((c, qs, bk, off, st))

            def pv(e):
                ins = None
                for (c, qs, bk, off, st) in plan:
                    q0 = qs * 128
                    qn = min(128, T - q0)
                    if use0:
                        ins = e.matmul(bk.t[0:qn, off:off + 128], lhsT=pp[c].t[0:nk, q0:q0 + qn], rhs=vc.t[0:nk, kb, 0:128],
                                       start=st, stop=last, skip_group_check=True)
                        ins = e.matmul(bk.t[0:qn, off + 128:off + VW], lhsT=pp[c].t[0:nk, q0:q0 + qn], rhs=ones0.t[0:nk, 0:1],
                                       start=False, stop=last, skip_group_check=True)
                    else:
                        ins = e.matmul(bk.t[0:qn, off:off + VW], lhsT=pp[c].t[0:nk, q0:q0 + qn], rhs=vc.t[0:nk, kb, :],
                                       start=st, stop=last, skip_group_check=True)
                return ins
            S.op("pe", pv, reads=[vc.res, pp[0].res, pp[1].res, ones0.res], writes=[bk.res for bk in OB])

        def emit_past_evac(h):
            for t in range(8):
                c, qs = t // 4, t % 4
                if qs >= nqs:
                    continue
                bk, off = otile(t)
                eng = "dve" if (t // 3) != 1 else "act"
                fcol = ftab.t[0:rows, h * 4 + qs:h * 4 + qs + 1]
                if eng == "dve":
                    S.op("dve", lambda e, t=t, bk=bk, off=off, fcol=fcol: e.tensor_scalar(
                        out=apv(Ap, t), in0=bk.t[0:rows, off:off + VW], scalar1=fcol, scalar2=None, op0=ALU.mult),
                        reads=[bk.res, ftab.res], writes=[Ap.res])
                else:
                    S.op("act", lambda e, t=t, bk=bk, off=off, fcol=fcol: e.activation(
                        out=apv(Ap, t), in_=bk.t[0:rows, off:off + VW], func=AF.Copy, scale=fcol),
                        reads=[bk.res, ftab.res], writes=[Ap.res])

        def emit_finalize(h, haspast):
            Onf = Onf2[h % 2]
            for t in range(8):
                c, qs = t // 4, t % 4
                if qs >= nqs:
                    continue
                bk, off = otile(t)
                if haspast:
                    S.op("dve", lambda e, t=t, bk=bk, off=off: e.tensor_tensor(
                        out=apv(At, t), in0=bk.t[0:rows, off:off + VW], in1=apv(Ap, t), op=ALU.add),
                        reads=[bk.res, Ap.res], writes=[At.res])
                elif (t // 3) != 1:
                    S.op("dve", lambda e, t=t, bk=bk, off=off: e.tensor_copy(apv(At, t), bk.t[0:rows, off:off + VW]),
                         reads=[bk.res], writes=[At.res])
                else:
                    S.op("act", lambda e, t=t, bk=bk, off=off: e.activation(out=apv(At, t), in_=bk.t[0:rows, off:off + VW],
                                                                           func=AF.Copy),
                         reads=[bk.res], writes=[At.res])
            Atv = At.t[0:rows, :].rearrange("p (t w) -> p t w", w=VW)
            S.op("dve", lambda e: e.reciprocal(rr.t[0:rows, :, :], Atv[:, :, 128:129]), reads=[At.res], writes=[rr.res])
            S.op("dve", lambda e: e.tensor_scalar(out=rr.t[0:rows, 4:8, :], in0=rr.t[0:rows, 4:8, :],
                                                  scalar1=neglam.t[0:rows, 0:1], scalar2=None, op0=ALU.mult),
                 reads=[rr.res, neglam.res], writes=[rr.res])
            for qs in range(nqs):
                S.op("dve", lambda e, qs=qs: e.tensor_scalar(out=Otm.t[0:rows, qs, :], in0=Atv[:, qs, 0:128],
                                                            scalar1=rr.t[0:rows, qs, :], scalar2=None, op0=ALU.mult),
                     reads=[At.res, rr.res], writes=[Otm.res])
            for qs in range(nqs):
                S.op("dve", lambda e, qs=qs: e.scalar_tensor_tensor(out=Otm.t[0:rows, qs, :], in0=Atv[:, 4 + qs, 0:128],
                                                                   scalar=rr.t[0:rows, 4 + qs, :], in1=Otm.t[0:rows, qs, :],
                                                                   op0=ALU.mult, op1=ALU.add),
                     reads=[At.res, rr.res, Otm.res], writes=[Otm.res])
            for qs in range(nqs):
                S.op("dve", lambda e, qs=qs: e.scalar_tensor_tensor(out=Onf.t[0:rows, qs, :], in0=Otm.t[0:rows, qs, :],
                                                                   scalar=1.0, in1=Otm.t[0:rows, qs, :],
                                                                   op0=ALU.mult, op1=ALU.mult, accum_out=ss4.t[0:rows, qs, :]),
                     reads=[Otm.res], writes=[Onf.res, ss4.res])

        def emit_finalize_B(h):
            S.op("act", lambda e: e.activation(out=ss4.t[0:rows, 0:nqs, :], in_=ss4.t[0:rows, 0:nqs, :], func=AF.Ln,
                                               scale=1.0 / 128.0, bias=epst.t[0:rows, 0:1]),
                 reads=[ss4.res, epst.res], writes=[ss4.res])
            S.op("act", lambda e: e.activation(out=ss4.t[0:rows, 0:nqs, :], in_=ss4.t[0:rows, 0:nqs, :], func=AF.Exp, scale=-0.5),
                 reads=[ss4.res], writes=[ss4.res])

        def emit_finalize_C(h):
            Onf = Onf2[h % 2]
            for qs in range(nqs):
                S.op("dve", lambda e, qs=qs: e.scalar_tensor_tensor(out=Onf.t[0:rows, qs, :], in0=Otm.t[0:rows, qs, :],
                                                                   scalar=ss4.t[0:rows, qs, :], in1=glb.t[0:rows, :],
                                                                   op0=ALU.mult, op1=ALU.mult),
                     reads=[Otm.res, ss4.res, glb.res, Onf.res], writes=[Onf.res])

        def emit_head_transpose(h):
            Onf = Onf2[h % 2]
            def tr(e):
                ins = None
                for qs in range(nqs):
                    ins = e.transpose(TB.t[:, qs * 128:qs * 128 + rows], Onf.t[0:rows, qs, :], ident.t[0:rows, 0:rows])
                return ins
            S.op("pe", tr, reads=[Onf.res, ident.res], writes=[TB.res])
            if h % 2 == 0:
                S.op("act", lambda e: e.activation(out=OnT.t[:, h, 0:T], in_=TB.t[:, 0:T], func=AF.Copy),
                     reads=[TB.res], writes=[OnTres[h]])
            else:
                S.op("dve", lambda e: e.tensor_copy(OnT.t[:, h, 0:T], TB.t[:, 0:T]), reads=[TB.res], writes=[OnTres[h]])

        nb = len(blocks)
        head_end = {}
        for i, bl in enumerate(blocks):
            head_end[bl["h"]] = i
        deferred = []
        seqc = [0]

        def defer(idx, fn):
            deferred.append((idx, seqc[0], fn))
            seqc[0] += 1

        def run_deferred(i):
            deferred.sort(key=lambda x: (x[0], x[1]))
            while deferred and deferred[0][0] <= i:
                deferred.pop(0)[2]()

        emit_qk(0, blocks[0])
        for i, bl in enumerate(blocks):
            if i + 1 < nb:
                emit_qk(i + 1, blocks[i + 1])
            emit_exp(bl)
            emit_pv(bl)
            if bl["typ"] == "past" and bl["last"]:
                emit_past_evac(bl["h"])
            run_deferred(i)
            if bl["typ"] == "diag" and bl["last"]:
                h = bl["h"]
                run_deferred(10 ** 9)
                emit_finalize(h, bl["haspast"])
                defer(i + 3, lambda h=h: emit_finalize_B(h))
                defer(i + 5, lambda h=h: emit_finalize_C(h))
                nxt_end = head_end.get(h + 1, nb + 10)
                defer(max(nxt_end, i + 6), lambda h=h: emit_head_transpose(h))
        run_deferred(10 ** 9)
        S.stage(6)
        Wo = wget()
        out_proj_residual(OnT, OnTres, Wo, T, rows, nsub, first=True)
        S.stage(7)

    def conv_layer(step):
        kind, tidx = step
        if kind == "p":
            T, rows, nsub = TT, 128, 4
        else:
            T, rows, nsub = DEC, DEC, 1
        hl = halo
        if kind == "s":
            for j_ in range(2):
                S.dma("sp", hl.ds, hl.t[:, :, j_], sc[tidx, j_].rearrange("(dc p) -> p dc", p=128), writes=[hl.res],
                      allow_slow_non_contiguous=True)
            S.op("pool", lambda e: e.tensor_copy(uext.t[:, :, 0:2], hl.t[:]), reads=[hl.res], writes=ures)
        else:
            S.op("pool", lambda e: e.tensor_scalar(out=uext.t[:, :, 0:2], in0=hl.t[:], scalar1=hmt.t[:, tidx:tidx + 1],
                                                   scalar2=None, op0=ALU.mult),
                 reads=[hl.res, hmt.res], writes=ures)
        Wh = wget()
        proj_fm(Wh, T, lambda dc, b: S.op("act", lambda e: e.activation(out=uext.t[:, dc, 2:2 + T], in_=b.t[:, 0:T],
                                                                         func=AF.Copy),
                                          reads=[b.res], writes=[ures[dc]]))
        Wc = wget()
        proj_fm(Wc, T, lambda dc, b: S.op("dve", lambda e: e.tensor_tensor(out=uext.t[:, dc, 2:2 + T], in0=b.t[:, 0:T],
                                                                            in1=uext.t[:, dc, 2:2 + T], op=ALU.mult),
                                          reads=[b.res, ures[dc]], writes=[ures[dc]]))
        S.op("pool", lambda e: e.tensor_copy(hl.t[:], uext.t[:, :, T:T + 2]), reads=ures, writes=[hl.res])
        dsts = []
        if kind == "p" and tidx == SEQ // TT - 1:
            dsts.append(cpo[0])
        if kind == "p" and tidx == SEQ // TT:
            dsts.append(cpo[1])
        if kind == "s":
            dsts.append(cs[tidx])
        for dst in dsts:
            for j_ in range(2):
                S.dma("sp", hl.ds, dst[j_].rearrange("(dc p) -> p dc", p=128), hl.t[:, :, j_], reads=[hl.res],
                      allow_slow_non_contiguous=True)
        Wb = wget()

        def b_dst(dc, b):
            z = zc[dc % 2]
            S.op("act", lambda e: e.activation(out=z.t[:, 0:T], in_=uext.t[:, dc, 0:T], func=AF.Copy,
                                               scale=cwt.t[:, 0, dc:dc + 1]),
                 reads=[ures[dc], cwt.res], writes=[z.res])
            for jj in (1, 2):
                S.op("dve", lambda e, jj=jj: e.scalar_tensor_tensor(
                    out=z.t[:, 0:T], in0=uext.t[:, dc, jj:jj + T], scalar=cwt.t[:, jj, dc:dc + 1],
                    in1=z.t[:, 0:T], op0=ALU.mult, op1=ALU.add),
                    reads=[ures[dc], cwt.res, z.res], writes=[z.res])
            S.op("dve", lambda e: e.tensor_tensor(out=OnT.t[:, dc, 0:T], in0=b.t[:, 0:T], in1=z.t[:, 0:T], op=ALU.mult),
                 reads=[b.res, z.res], writes=[OnTres[dc]])
        proj_fm(Wb, T, b_dst)
        Wo = wget()
        out_proj_residual(OnT, OnTres, Wo, T, rows, nsub, first=True)

    def mlp_layer(step):
        kind, tidx = step
        if kind == "p":
            T, rows, nsub = TT, 128, 4
        else:
            T, rows, nsub = DEC, DEC, 1

        def up(j):
            Wu = wget()
            hb = hT[j % 2]

            def dst(fc, b):
                rb = rbuf[fc % 2]
                hres = OnTres[fc] if hb is OnT else ktb.res
                S.op("act", lambda e: e.activation(out=rb.t[:, 0:T], in_=b.t[:, 0:T], func=AF.Relu),
                     reads=[b.res], writes=[rb.res])
                S.op("pool", lambda e: e.tensor_tensor(out=hb.t[:, fc, 0:T], in0=rb.t[:, 0:T], in1=rb.t[:, 0:T],
                                                       op=ALU.mult),
                     reads=[rb.res], writes=[hres])
            proj_fm(Wu, T, dst)

        def down(j):
            Wd = wget()
            hb = hT[j % 2]
            out_proj_residual(hb, OnTres if hb is OnT else [ktb.res], Wd, T, rows, nsub, first=(j == 0))
        up(0); up(1); down(0); up(2); down(1); up(3); down(2); down(3)

    xds = x_tm.ds
    ccds = S.dsem("cc")
    for si, step in enumerate(steps):
        kind, tidx = step
        if kind == "p":
            T, rows, nsub = TT, 128, 4
            src = xp[tidx * TT:(tidx + 1) * TT, :].rearrange("(s p) d -> p s d", p=128)
            S.dma("sp", xds, x_tm.t[:, :, :], src, writes=xres)
        else:
            T, rows, nsub = DEC, DEC, 1
            S.dma("sp", xds, x_tm.t[0:DEC, 0, :], xs[tidx], writes=xres)
        if exchange and si > 0:
            if kind == "p":
                S.dma("sp", uext.ds, rview, recv[0:TT, :].rearrange("(s p) d -> p s d", p=128), reads=[recv_res], writes=ures)
            else:
                S.dma("sp", uext.ds, rview[0:DEC, 0, :], recv[0:DEC, :], reads=[recv_res], writes=ures)
            for s in range(nsub):
                S.op("dve", lambda e, s=s, rows=rows: e.scalar_tensor_tensor(
                    out=x_tm.t[0:rows, s, :], in0=rview[0:rows, s, :], scalar=selt.t[0:rows, 0:1],
                    in1=x_tm.t[0:rows, s, :], op0=ALU.mult, op1=ALU.add),
                    reads=ures + [xres[s], selt.res], writes=[xres[s]])
        make_xT(T, rows, nsub)
        S.stage(3)
        pro = (kind == "p")
        attn_layer(step)
        if pro:
            pro_prefetch()
        layer_norm(0, T, rows, nsub)
        make_xT(T, rows, nsub)
        S.stage(8)
        mlp_layer(step)
        S.stage(9)
        if pro:
            pro_consume()
            pro_prefetch()
        layer_norm(1, T, rows, nsub)
        make_xT(T, rows, nsub)
        conv_layer(step)
        if pro:
            pro_consume()
            pro_prefetch()
        layer_norm(2, T, rows, nsub)
        make_xT(T, rows, nsub)
        mlp_layer(step)
        if pro:
            pro_consume()
        layer_norm(3, T, rows, nsub)
        if kind == "p":
            S.dma("sp", xds, yp[tidx * TT:(tidx + 1) * TT, :].rearrange("(s p) d -> p s d", p=128), x_tm.t[:, :, :],
                  reads=xres)
        else:
            S.dma("sp", xds, ys[tidx], x_tm.t[0:DEC, 0, :], reads=xres)
        if exchange and si < len(steps) - 1:
            if kind == "p":
                S.dma("sp", xds, send.rearrange("(s p) d -> p s d", p=128), x_tm.t[:, :, :], reads=xres, writes=[send_res])
            else:
                S.dma("sp", xds, send[0:DEC, :], x_tm.t[0:DEC, 0, :], reads=xres, writes=[send_res])
            S._deps("pool", [send_res], [recv_res])
            ccds.count += 1
            ev = (ccds.key, ccds.count)
            S.streams["pool"].append(("op", lambda e: e.collective_compute(
                "AllGather", ALU.bypass, replica_groups=PAIRS, ins=[send], outs=[recv]), ev[0], 1))
            S._record(ev, [send_res], [recv_res])
        if kind == "p":
            emit_cv(1)
            if si + 1 < len(steps) and steps[si + 1][0] == "s":
                pro_flush()
                emit_cv(10 ** 6)
    S.emit()
    return nc, S


def _tables():
    biast = np.zeros((128, NH * 64), np.float32)
    p = np.arange(128, dtype=np.float64)
    for h in range(NH):
        for r in range(64):
            rel = r - 64
            biast[:, h * 64 + r] = (SLOPES[h] * (rel * 128 + p)).astype(np.float32)
    mt = np.zeros((128, 4 * TT), np.float32)
    q = np.arange(TT)
    for j in range(4):
        k = j * 128 + np.arange(128)
        vis = (k[:, None] // 64) <= (q[None, :] // 64)
        m = -np.abs(q[None, :] - k[:, None]).astype(np.float64)
        mt[:, j * TT:(j + 1) * TT] = (8.0 * np.where(vis, m, -1.0e5)).astype(np.float32)
    qrel = np.broadcast_to(np.arange(TT, dtype=np.float32)[None, :], (128, TT)).copy()
    return biast, mt, qrel


def _ftab():
    ft = np.zeros((128, NH * 4), np.float32)
    p = np.arange(128, dtype=np.float64)
    for h in range(NH):
        for qs in range(4):
            ft[:, h * 4 + qs] = np.exp(-SLOPES[h] * (qs * 128 + p)).astype(np.float32)
    return ft


_CACHE = {}


def make_in_maps(x_prompt, x_sample, cache_k, cache_v, state_conv, w_qkv, lambda_q1, lambda_k1,
                 lambda_q2, lambda_k2, g_subln, w_attn_out, w_conv_in, w_conv, w_conv_out,
                 w_up, w_down, ln_g, ln_b):
    f = lambda a: np.ascontiguousarray(np.asarray(a, dtype=np.float32))
    biast, mt, qrel = _tables()
    xp = np.asarray(x_prompt); xs = np.asarray(x_sample)
    ck = np.asarray(cache_k).reshape(2, 8, PAST, D); cv = np.asarray(cache_v).reshape(2, 8, PAST, D)
    sc = np.asarray(state_conv)
    lng = np.asarray(ln_g); lnb = np.asarray(ln_b)
    lam = np.stack([np.asarray(lambda_q1), np.asarray(lambda_k1), np.asarray(lambda_q2), np.asarray(lambda_k2)], axis=1)
    in_maps = []
    for c in range(8):
        p, r = c // 2, c % 2
        m = {"ident": np.eye(128, dtype=np.float32), "biastab": biast, "mtab": mt, "qrel": qrel, "ftab": _ftab()}
        xpc = np.zeros((NPS * TT, D), np.float32)
        xsc = np.zeros((NSS, DEC, D), np.float32)
        ckc = np.zeros((NSS, PAST, D), np.float32)
        cvc = np.zeros((NSS, PAST, D), np.float32)
        scc = np.zeros((NSS, 2, D), np.float32)
        if r == 0:
            xpc[:SEQ] = xp[p]
            xsc[0:2] = xs[2 * p:2 * p + 2]
        for j in range(2):
            ckc[j + r] = ck[r, 2 * p + j]
            cvc[j + r] = cv[r, 2 * p + j]
            scc[j + r] = sc[r, 2 * p + j]
        m.update(xp=xpc, xs=xsc, ck=ckc, cv=cvc, sc=scc)
        m["w_qkv"] = f(np.asarray(w_qkv)[r]); m["w_ao"] = f(np.asarray(w_attn_out)[r])
        m["w_ci"] = f(np.asarray(w_conv_in)[r]); m["w_cw"] = f(np.asarray(w_conv)[r]); m["w_co"] = f(np.asarray(w_conv_out)[r])
        m["w_up"] = f(np.asarray(w_up)[2 * r:2 * r + 2]); m["w_dn"] = f(np.asarray(w_down)[2 * r:2 * r + 2])
        m["lam4"] = f(lam[r]); m["gsub"] = f(np.asarray(g_subln)[r].reshape(128, 1))
        m["ln_g"] = f(lng[2 * r:2 * r + 2].reshape(4, D)); m["ln_b"] = f(lnb[2 * r:2 * r + 2].reshape(4, D))
        li0 = lam_init_of(2 * r)
        m["linit"] = np.tile(np.array([[-li0, 1.0 - li0]], np.float32), (128, 1))
        m["sel"] = np.full((128, 1), float(r), np.float32)
        hm = np.ones((128, NPS), np.float32)
        hm[:, 0] = 0.0
        if r == 1:
            hm[:, 1] = 0.0
        m["hm"] = hm
        m["ones0"] = np.full((128, 128), 1.0 - r, np.float32)
        in_maps.append(m)
    return in_maps


def assemble(res):
    yp = np.zeros((4, SEQ, D), np.float32); ys = np.zeros((8, DEC, D), np.float32)
    kp = np.zeros((2, 4, SEQ, D), np.float32); vp = np.zeros((2, 4, SEQ, D), np.float32)
    cp = np.zeros((2, 4, 2, D), np.float32)
    ks = np.zeros((2, 8, DEC, D), np.float32); vs = np.zeros((2, 8, DEC, D), np.float32)
    cs = np.zeros((2, 8, 2, D), np.float32)
    for p in range(4):
        a, b = res[2 * p], res[2 * p + 1]
        yp[p] = b["yp"][TT:TT + SEQ]
        kp[0, p] = a["kp"][0:SEQ]; kp[1, p] = b["kp"][TT:TT + SEQ]
        vp[0, p] = a["vp"][0:SEQ]; vp[1, p] = b["vp"][TT:TT + SEQ]
        cp[0, p] = a["cpo"][0]; cp[1, p] = b["cpo"][1]
        for j in range(2):
            ys[2 * p + j] = b["ys"][1 + j]
            ks[0, 2 * p + j] = a["ks"][j]; ks[1, 2 * p + j] = b["ks"][1 + j]
            vs[0, 2 * p + j] = a["vs"][j]; vs[1, 2 * p + j] = b["vs"][1 + j]
            cs[0, 2 * p + j] = a["cs"][j]; cs[1, 2 * p + j] = b["cs"][1 + j]
    return (yp, ys, kp.reshape(2, 4, SEQ, NH, 2, 64), vp.reshape(2, 4, SEQ, NH, 128), cp,
            ks.reshape(2, 8, DEC, NH, 2, 64), vs.reshape(2, 8, DEC, NH, 128), cs)


def kernel(**inputs):
    if "nc" not in _CACHE:
        _CACHE["nc"] = build_program()[0]
    nc = _CACHE["nc"]
    in_maps = make_in_maps(**inputs)
    res = run_bass_kernel_spmd(nc, in_maps, core_ids=list(range(8))).results
    return assemble(res)
```

```python
import math
import os
import numpy as np
import concourse.bass as bass
import concourse.mybir as mybir
from concourse.bass_utils import run_bass_kernel_spmd

F32 = mybir.dt.float32
BF16 = mybir.dt.bfloat16
ALU = mybir.AluOpType
AF = mybir.ActivationFunctionType

D = 1024
SEQ = 8192
NH = 8
DEPTH = 4
PAST = 4096
DEC = 16
TT = 512
ALPHA = (2.0 * DEPTH) ** 0.25
LN_EPS = 1e-5
SLOPES = [2.0 ** (-8.0 * (h + 1) / NH) for h in range(NH)]
NSLOT_W = 3
NSLOT_KV = 4
VW = 129
MTW = 1280
MTOFF = [0, 512, 896, 1152]
UNI_H = 2
CAST_AHEAD = 5
ALIBI_THR = 125.0
KVC = 1024


def lam_init_of(i):
    return 0.8 - 0.6 * math.exp(-0.3 * i)


ENGS = ("pe", "act", "dve", "pool", "sp")


class Res:
    __slots__ = ("name", "w", "r", "const")

    def __init__(self, name):
        self.name = name
        self.w = None
        self.r = {}
        self.const = False


class DSem:
    __slots__ = ("key", "count")

    def __init__(self, key):
        self.key = key
        self.count = 0


class Sched:
    def __init__(self, nc):
        self.nc = nc
        self.streams = {e: [] for e in ENGS}
        self.sems = {}
        self.ecnt = {e: 0 for e in ENGS}
        self.known = {e: {} for e in ENGS}
        self.selfsync = {"pe": False, "act": True, "dve": True, "pool": True, "sp": True}
        for e in ENGS:
            self._sem("E_" + e)
        self.dsems = []
        self.n_ops = 0
        self.dead = False
        self.stop = int(os.environ.get('K_STOP', '99'))

    def _sem(self, key):
        if key not in self.sems:
            self.sems[key] = self.nc.alloc_semaphore(name=key)
        return key

    def dsem(self, name):
        d = DSem(self._sem("D_" + name))
        self.dsems.append(d)
        return d

    def _wait(self, eng, ev):
        if ev is None:
            return
        key, val = ev
        if key == "E_" + eng and not self.selfsync[eng]:
            return
        k = self.known[eng]
        if k.get(key, 0) >= val:
            return
        k[key] = val
        self.streams[eng].append(("wait", key, val))

    def _deps(self, eng, reads, writes):
        for r in reads:
            self._wait(eng, r.w)
        for w in writes:
            self._wait(eng, w.w)
            for key, val in w.r.items():
                self._wait(eng, (key, val))

    def _record(self, ev, reads, writes):
        for r in reads:
            if r.const:
                continue
            if r.r.get(ev[0], 0) < ev[1]:
                r.r[ev[0]] = ev[1]
        for w in writes:
            w.w = ev
            w.r = {}

    def stage(self, n):
        if n >= self.stop:
            self.dead = True

    def op(self, eng, fn, reads=(), writes=()):
        if self.dead:
            return None
        self._deps(eng, reads, writes)
        self.ecnt[eng] += 1
        ev = ("E_" + eng, self.ecnt[eng])
        self.streams[eng].append(("op", fn, ev[0], 1))
        self._record(ev, reads, writes)
        self.n_ops += 1
        return ev

    def dma(self, eng, dsem, out, in_, reads=(), writes=(), **kw):
        if self.dead:
            return None
        self._deps(eng, reads, writes)
        dsem.count += 16
        ev = (dsem.key, dsem.count)

        def fn(e, out=out, in_=in_, kw=kw):
            return e.dma_start(out=out, in_=in_, **kw)
        self.streams[eng].append(("op", fn, ev[0], 16))
        self._record(ev, reads, writes)
        self.n_ops += 1
        return ev

    def finish(self):
        for d in self.dsems:
            if d.count:
                self._wait("sp", (d.key, d.count))
        for e in ENGS:
            if e != "sp" and self.ecnt[e]:
                self._wait("sp", ("E_" + e, self.ecnt[e]))

    def _replay(self, eng, e):
        for item in self.streams[eng]:
            if item[0] == "wait":
                e.wait_ge(self.sems[item[1]], item[2])
            else:
                _, fn, key, amt = item
                fn(e).then_inc(self.sems[key], amt)

    def emit(self):
        self.finish()
        with self.nc.Block() as block:
            @block.tensor
            def _(e):
                self._replay("pe", e)

            @block.scalar
            def _(e):
                self._replay("act", e)

            @block.vector
            def _(e):
                self._replay("dve", e)

            @block.gpsimd
            def _(e):
                self._replay("pool", e)

            @block.sync
            def _(e):
                self._replay("sp", e)


class Buf:
    def __init__(self, S, t, name, dma=False):
        self.t = t
        self.name_ = name
        self.res = Res(name)
        self.ds = S.dsem(name) if dma else None


NPS = SEQ // TT + 1
NSS = 3
PAIRS = [[0, 1], [2, 3], [4, 5], [6, 7]]


def build_program(n_psteps=NPS, n_ssteps=NSS, exchange=True):
    nc = bass.Bass("TRN2", target_bir_lowering=False)
    S = Sched(nc)

    def din(name, shape, dt=F32):
        return nc.dram_tensor(name, list(shape), dt, kind="ExternalInput").ap()

    def dout(name, shape, dt=F32):
        return nc.dram_tensor(name, list(shape), dt, kind="ExternalOutput").ap()

    def dscr(name, shape, dt=BF16):
        return nc.dram_tensor(name, list(shape), dt).ap()

    NROW = NPS * TT
    xp = din("xp", [NROW, D])
    xs = din("xs", [NSS, DEC, D])
    ck = din("ck", [NSS, PAST, D])
    cv = din("cv", [NSS, PAST, D])
    sc = din("sc", [NSS, 2, D])
    w_qkv = din("w_qkv", [D, 3 * D])
    w_ao = din("w_ao", [D, D])
    w_ci = din("w_ci", [D, 3 * D])
    w_cw = din("w_cw", [3, D])
    w_co = din("w_co", [D, D])
    w_up = din("w_up", [2, D, 4 * D])
    w_dn = din("w_dn", [2, 4 * D, D])
    lam4 = din("lam4", [4, 64])
    gsub = din("gsub", [128, 1])
    ln_g = din("ln_g", [4, D])
    ln_b = din("ln_b", [4, D])
    linit_d = din("linit", [128, 2])
    sel_d = din("sel", [128, 1])
    hm_d = din("hm", [128, NPS])
    ones0_d = din("ones0", [128, 128])
    ident_d = din("ident", [128, 128])
    bias_d = din("biastab", [128, NH * 64])
    mtab_d = din("mtab", [128, 2 * MTW])
    qrel_d = din("qrel", [128, TT])
    ftab_d = din("ftab", [128, NH * 4])

    yp = dout("yp", [NROW, D])
    ys = dout("ys", [NSS, DEC, D])
    kp = dout("kp", [NROW, D])
    vp = dout("vp", [NROW, D])
    cpo = dout("cpo", [2, 2, D])
    ks = dout("ks", [NSS, DEC, D])
    vs = dout("vs", [NSS, DEC, D])
    cs = dout("cs", [NSS, 2, D])

    wscr = {}
    KT_p = dscr("KT_p", [NH, 128, NROW])
    V_p = dscr("V_p", [NH, NROW, 128])
    NKS = PAST + 128
    KT_s = [dscr(f"KT_s{j}", [NH, 128, NKS]) for j in range(NSS)]
    V_s = [dscr(f"V_s{j}", [NH, NKS, 128]) for j in range(NSS)]
    send = dscr("send", [TT, D], F32)
    recv = dscr("recv", [2 * TT, D], F32)
    send_res, recv_res = Res("send"), Res("recv")
    scr_res = {}

    def sres(key):
        if key not in scr_res:
            scr_res[key] = Res("scr" + str(key))
        return scr_res[key]

    A = nc.alloc_sbuf_tensor

    def sb(name, shape, dt=F32, dma=False):
        return Buf(S, A("sb_" + name, list(shape), dt), name, dma)

    x_tm = sb("x_tm", [128, 4, D], F32, dma=True)
    xres = [Res(f"x_tm{s}") for s in range(4)]
    xT = sb("xT", [128, 8, TT], BF16)
    xTres = [Res(f"xT{k}") for k in range(8)]
    ring = [sb(f"wring{i}", [128, 8, 1024], BF16, dma=True) for i in range(NSLOT_W)]
    QT = [sb(f"QT{c}", [128, NH, TT], BF16) for c in range(2)]
    QTres = [Res(f"QTh{h}") for h in range(NH)]
    KTc = [sb(f"KTc{i}", [128, KVC], BF16, dma=True) for i in range(NSLOT_KV)]
    Vc = [sb(f"Vc{i}", [128, KVC // 128, VW], BF16, dma=True) for i in range(NSLOT_KV)]
    Pt = [sb(f"P{i}", [128, TT], BF16) for i in range(4)]
    OnT = sb("OnT", [128, NH, TT], BF16)
    OnTres = [Res(f"OnT{h}") for h in range(NH)]
    Ap = sb("Ap", [128, 8 * VW], F32)
    At = sb("At", [128, 8 * VW], F32)
    Onf2 = [sb(f"Onf{i}", [128, 4, 128], F32) for i in range(2)]
    Otm = sb("Otm", [128, 4, 128], F32)
    rr = sb("rr", [128, 8, 1], F32)
    ss4 = sb("ss4", [128, 4, 1], F32)
    glb = sb("glb", [128, 128], F32, dma=True)
    ftab = sb("ftab", [128, NH * 4], F32, dma=True)
    tmpf = [sb(f"tmpf{i}", [128, TT], F32) for i in range(2)]
    dtmp = tmpf
    rbuf = tmpf
    zc = tmpf
    ktb = sb("ktb", [128, NH, TT], BF16, dma=True)
    hT = [OnT, ktb]
    biast = sb("biast", [128, NH * 64], F32, dma=True)
    mtab = sb("mtab", [128, 2 * MTW], F32, dma=True)
    ident = sb("ident", [128, 128], F32, dma=True)
    ones = sb("ones", [128, 128], BF16)
    ones0f = sb("ones0f", [128, 128], F32, dma=True)
    ones0 = sb("ones0", [128, 128], BF16)
    epst = sb("epst", [128, 1], F32)
    gbt = [sb(f"gb{i}", [128, 2, D], F32, dma=True) for i in range(2)]
    kst = [sb(f"kst{i}", [128, 512], F32, dma=True) for i in range(2)]
    vst = [sb(f"vst{i}", [128, 512], F32, dma=True) for i in range(2)]
    vbb = [sb(f"vbb{i}", [128, 512], BF16, dma=True) for i in range(2)]
    uext = sb("uext", [128, 8, TT + 4], F32, dma=True)
    ures = [Res(f"uext{k}") for k in range(8)]
    halo = sb("halo", [128, 8, 2], F32, dma=True)
    cwt = sb("cwt", [128, 3, 8], F32, dma=True)
    lamt = sb("lamt", [128, 4, 64], F32, dma=True)
    lamw = sb("lamw", [128, 2, 64], F32)
    lams = sb("lams", [128, 2], F32)
    neglam = sb("neglam", [128, 1], F32)
    glt = sb("glt", [128, 1], F32, dma=True)
    linit = sb("linit", [128, 2], F32, dma=True)
    selt = sb("selt", [128, 1], F32, dma=True)
    hmt = sb("hmt", [128, NPS], F32, dma=True)
    stt = sb("stt", [128, 4, 12], F32)
    mvt = sb("mvt", [128, 4, 2], F32)
    rs4 = sb("rs4", [128, 4, 1], F32)
    nm4 = sb("nm4", [128, 4, 1], F32)
    print("sbuf bytes remaining:", nc.sbuf_bytes_remaining)

    class _View:
        pass
    cache_ld = []
    for i in range(2):
        v = _View()
        v.t = x_tm.t[:, i, :]
        v.res = xres[i]
        v.ds = S.dsem(f"cld{i}")
        cache_ld.append(v)
    rview = uext.t[:].rearrange("p a b -> p (a b)")[:, 0:4 * D].rearrange("p (s d) -> p s d", s=4)

    banks = [Buf(S, nc.alloc_psum_tensor(f"bank{i}", [128, 512], F32), f"bank{i}") for i in range(8)]
    bank_rr = [0]

    def next_bank():
        b = banks[bank_rr[0] % 8]
        bank_rr[0] += 1
        return b

    S.dma("sp", ident.ds, ident.t[:], ident_d, writes=[ident.res])
    S.dma("sp", biast.ds, biast.t[:], bias_d, writes=[biast.res])
    S.dma("sp", mtab.ds, mtab.t[:], mtab_d, writes=[mtab.res])
    S.dma("sp", ones0f.ds, ones0f.t[:], ones0_d, writes=[ones0f.res])
    S.dma("sp", linit.ds, linit.t[:], linit_d, writes=[linit.res])
    S.dma("sp", selt.ds, selt.t[:], sel_d, writes=[selt.res])
    S.dma("sp", hmt.ds, hmt.t[:], hm_d, writes=[hmt.res])
    S.dma("sp", glt.ds, glt.t[:], gsub, writes=[glt.res])
    S.dma("sp", ftab.ds, ftab.t[:], ftab_d, writes=[ftab.res])
    S.dma("sp", glb.ds, glb.t[:], gsub.rearrange("e o -> (e o)").partition_broadcast(128), writes=[glb.res])
    for i_ in range(NSLOT_KV):
        S.op("pool", lambda e, i_=i_: e.memset(Vc[i_].t[:, :, 128:129], 1.0), writes=[Vc[i_].res])
    S.op("dve", lambda e: e.memset(ones.t[:], 1.0), writes=[ones.res])
    S.op("dve", lambda e: e.memset(epst.t[:], LN_EPS), writes=[epst.res])
    S.op("dve", lambda e: e.tensor_copy(ones0.t[:], ones0f.t[:]), reads=[ones0f.res], writes=[ones0.res])
    S.op("pool", lambda e: e.memset(halo.t[:], 0.0), writes=[halo.res])
    for c in range(2):
        S.op("pool", lambda e, c=c: e.memset(QT[c].t[:], 0.0), writes=QTres)
    for j_ in range(3):
        S.dma("sp", cwt.ds, cwt.t[:, j_, :], w_cw[j_].rearrange("(dc p) -> p dc", p=128),
              writes=[cwt.res], allow_slow_non_contiguous=True)
    S.dma("sp", lamt.ds, lamt.t[:].rearrange("p b c -> p (b c)"),
          lam4.rearrange("b c -> (b c)").partition_broadcast(128), writes=[lamt.res])
    S.op("dve", lambda e: e.tensor_tensor(out=lamw.t[:], in0=lamt.t[:, 0:4:2, :], in1=lamt.t[:, 1:4:2, :], op=ALU.mult),
         reads=[lamt.res], writes=[lamw.res])
    S.op("dve", lambda e: e.tensor_reduce(out=lams.t[:], in_=lamw.t[:], axis=mybir.AxisListType.X, op=ALU.add),
         reads=[lamw.res], writes=[lams.res])
    S.op("act", lambda e: e.activation(out=lams.t[:], in_=lams.t[:], func=AF.Exp), reads=[lams.res], writes=[lams.res])
    S.op("dve", lambda e: e.scalar_tensor_tensor(out=neglam.t[:], in0=lams.t[:, 1:2], scalar=linit.t[:, 0:1],
                                                 in1=lams.t[:, 0:1], op0=ALU.add, op1=ALU.subtract),
         reads=[lams.res, linit.res], writes=[neglam.res])
    S.op("dve", lambda e: e.tensor_scalar(out=glt.t[:], in0=glt.t[:], scalar1=linit.t[:, 1:2], scalar2=None, op0=ALU.mult),
         reads=[glt.res, linit.res], writes=[glt.res])
    S.op("dve", lambda e: e.tensor_scalar(out=glb.t[:], in0=glb.t[:], scalar1=linit.t[:, 1:2], scalar2=None, op0=ALU.mult),
         reads=[glb.res, linit.res], writes=[glb.res])
    for r in (ident, biast, mtab, ones, ones0, epst, cwt, selt, hmt, neglam, ftab, glb):
        r.res.const = True

    layer_chunks = []
    wres = {}
    cast_jobs = []
    cast_state = {'n': 0}

    def emit_casts(upto):
        while cast_state['n'] < min(upto, len(cast_jobs)):
            ds_, dst_, src_, res_ = cast_jobs[cast_state['n']]
            S.dma('pool', ds_, dst_, src_, writes=[res_])
            cast_state['n'] += 1

    wds = S.dsem("wcast")
    for i in range(2):
        lst = []
        if i == 0:
            srcs = [(w_qkv[:, 0:D], "q"), (w_qkv[:, D:2 * D], "k"), (w_qkv[:, 2 * D:3 * D], "v"), (w_ao, "o")]
        else:
            srcs = [(w_ci[:, 2 * D:3 * D], "h"), (w_ci[:, D:2 * D], "c"), (w_ci[:, 0:D], "b"), (w_co, "o")]
        up = lambda j: (w_up[i, :, j * D:(j + 1) * D], f"u{j}")
        dn = lambda j: (w_dn[i, j * D:(j + 1) * D, :], f"d{j}")
        srcs += [up(0), up(1), dn(0), up(2), dn(1), up(3), dn(2), dn(3)]
        for src, nm in srcs:
            key = (i, nm)
            wscr[key] = dscr(f"w_{i}_{nm}", [128, 8, 1024])
            wres[key] = Res(f"w_{i}_{nm}")
            cast_jobs.append((S.dsem(f"wc_{i}_{nm}"), wscr[key], src.rearrange("(kc p) n -> p kc n", p=128), wres[key]))
            lst.append(key)
        layer_chunks.append(lst)

    steps = [("p", t) for t in range(n_psteps)] + [("s", j) for j in range(n_ssteps)]
    wseq = []
    for _ in steps:
        for i in range(2):
            wseq.extend(layer_chunks[i])
    wstate = {"loaded": 0, "use": 0}

    emit_casts(CAST_AHEAD)

    def wget():
        k = wstate["use"]
        wstate["use"] += 1
        emit_casts(k + 1 + CAST_AHEAD)
        while wstate["loaded"] < len(wseq) and wstate["loaded"] < k + NSLOT_W:
            j = wstate["loaded"]
            slot = ring[j % NSLOT_W]
            S.dma("sp", slot.ds, slot.t[:], wscr[wseq[j]], reads=[wres[wseq[j]]], writes=[slot.res])
            wstate["loaded"] += 1
        return ring[k % NSLOT_W]

    pro_blocks = []
    cv_jobs = []
    cv_state = {'n': 0}

    def emit_cv(count):
        while count > 0 and cv_state['n'] < len(cv_jobs):
            dst_, src_, res_ = cv_jobs[cv_state['n']]
            S.dma('pool', cvds, dst_, src_, writes=[res_])
            cv_state['n'] += 1
            count -= 1
        if cv_state['n'] == len(cv_jobs):
            for dst_, src_, res_ in cv_jobs:
                res_.w = (cvds.key, cvds.count)

    if n_ssteps:
        cvds = S.dsem("cvcast")
        for j in range(n_ssteps):
            for q4 in range(4):
                cv_jobs.append((V_s[j][:, q4 * 1024:(q4 + 1) * 1024, :].rearrange("h k e -> k h e"),
                                cv[j, q4 * 1024:(q4 + 1) * 1024, :].rearrange("k (h e) -> k h e", h=NH),
                                sres(("V", "s", j, "hist"))))
        for j in range(n_ssteps):
            for kb in range(PAST // 128):
                pro_blocks.append((j, kb))
    class _V2:
        def __init__(self, t, res):
            self.t, self.res = t, res
    pl_bufs = [(_V2(Ap.t[:, 0:512], Ap.res), _V2(Ap.t[:, 512:1024], Ap.res)),
               (_V2(At.t[:, 0:512], At.res), _V2(At.t[:, 512:1024], At.res))]
    pl_ds = [(S.dsem("pl00"), S.dsem("pl01")), (S.dsem("pl10"), S.dsem("pl11"))]
    kts = [sb(f"kts{i}", [128, NH, 128], BF16, dma=True) for i in range(2)]
    pro_state = {"n": 0, "loaded": []}

    def pro_prefetch():
        while len(pro_state["loaded"]) < 2 and pro_state["n"] < len(pro_blocks):
            j, kb = pro_blocks[pro_state["n"]]
            par = pro_state["n"] % 2
            pro_state["n"] += 1
            for half in range(2):
                buf = pl_bufs[par][half]
                S.dma("sp", pl_ds[par][half], buf.t[:, :], ck[j, kb * 128:(kb + 1) * 128, half * 512:(half + 1) * 512],
                      writes=[buf.res])
            pro_state["loaded"].append((j, kb, par))

    def pro_consume():
        for (j, kb, par) in pro_state["loaded"]:
            kt = kts[par]
            for half in range(2):
                buf = pl_bufs[par][half]
                b = next_bank()

                def tr(e, buf=buf, b=b):
                    ins = None
                    for hh in range(4):
                        ins = e.transpose(b.t[:, hh * 128:(hh + 1) * 128], buf.t[:, hh * 128:(hh + 1) * 128], ident.t[:])
                    return ins
                S.op("pe", tr, reads=[buf.res, ident.res], writes=[b.res])
                if half == 0:
                    S.op("act", lambda e, b=b, half=half, kt=kt: e.activation(
                        out=kt.t[:, half * 4:(half + 1) * 4, :],
                        in_=b.t[:].rearrange("p (h k) -> p h k", h=4), func=AF.Copy),
                        reads=[b.res], writes=[kt.res])
                else:
                    S.op("dve", lambda e, b=b, half=half, kt=kt: e.tensor_copy(
                        kt.t[:, half * 4:(half + 1) * 4, :],
                        b.t[:].rearrange("p (h k) -> p h k", h=4)),
                        reads=[b.res], writes=[kt.res])
            S.dma("sp", kt.ds, KT_s[j][:, :, kb * 128:(kb + 1) * 128].rearrange("h p k -> p h k"),
                  kt.t[:, :, :], reads=[kt.res], writes=[sres(("KT", "s", j, "hist"))])
        pro_state["loaded"] = []

    def pro_flush():
        while pro_state["n"] < len(pro_blocks) or pro_state["loaded"]:
            pro_prefetch()
            pro_consume()

    def make_xT(T, rows, nsub):
        for kc in range(8):
            b = next_bank()

            def tr(e, b=b, kc=kc):
                ins = None
                for s in range(nsub):
                    ins = e.transpose(b.t[:, s * 128:s * 128 + rows], x_tm.t[0:rows, s, kc * 128:(kc + 1) * 128],
                                      ident.t[0:rows, 0:rows])
                return ins
            S.op("pe", tr, reads=xres[:nsub] + [ident.res], writes=[b.res])
            if kc % 2 == 0:
                S.op("act", lambda e, b=b, kc=kc: e.activation(out=xT.t[:, kc, 0:T], in_=b.t[:, 0:T], func=AF.Copy),
                     reads=[b.res], writes=[xTres[kc]])
            else:
                S.op("dve", lambda e, b=b, kc=kc: e.tensor_copy(xT.t[:, kc, 0:T], b.t[:, 0:T]),
                     reads=[b.res], writes=[xTres[kc]])

    gb_state = {"n": 0, "gb": None}
    sttr = [Res(f"stt{i}") for i in range(4)]
    mvr = [Res(f"mv{i}") for i in range(4)]
    rsr = [Res(f"rs{i}") for i in range(4)]
    nmr = [Res(f"nm{i}") for i in range(4)]

    def ln_prepare(idx):
        gb = gbt[gb_state["n"] % 2]
        gb_state["n"] += 1
        S.dma("sp", gb.ds, gb.t[:, 0, :], ln_g[idx].partition_broadcast(128), writes=[gb.res])
        S.dma("sp", gb.ds, gb.t[:, 1, :], ln_b[idx].partition_broadcast(128), writes=[gb.res])
        gb_state["gb"] = gb

    def ln_subtile(s, rows):
        gb = gb_state["gb"]
        for hf in range(2):
            S.op("dve", lambda e, hf=hf: e.bn_stats(out=stt.t[0:rows, s, hf * 6:(hf + 1) * 6],
                                                   in_=x_tm.t[0:rows, s, hf * 512:(hf + 1) * 512]),
                 reads=[xres[s]], writes=[sttr[s]])
        S.op("dve", lambda e: e.bn_aggr(out=mvt.t[0:rows, s, :], in_=stt.t[0:rows, s, :]), reads=[sttr[s]], writes=[mvr[s]])
        S.op("act", lambda e: e.activation(out=rs4.t[0:rows, s, :], in_=mvt.t[0:rows, s, 1:2], func=AF.Ln,
                                           bias=epst.t[0:rows, 0:1]),
             reads=[mvr[s], epst.res], writes=[rsr[s]])
        S.op("act", lambda e: e.activation(out=rs4.t[0:rows, s, :], in_=rs4.t[0:rows, s, :], func=AF.Exp, scale=-0.5),
             reads=[rsr[s]], writes=[rsr[s]])
        S.op("dve", lambda e: e.scalar_tensor_tensor(out=nm4.t[0:rows, s, :], in0=mvt.t[0:rows, s, 0:1], scalar=-1.0,
                                                     in1=rs4.t[0:rows, s, :], op0=ALU.mult, op1=ALU.mult),
             reads=[mvr[s], rsr[s]], writes=[nmr[s]])
        S.op("act", lambda e: e.activation(out=x_tm.t[0:rows, s, :], in_=x_tm.t[0:rows, s, :], func=AF.Identity,
                                           scale=rs4.t[0:rows, s, :], bias=nm4.t[0:rows, s, :]),
             reads=[xres[s], rsr[s], nmr[s]], writes=[xres[s]])
        S.op("dve", lambda e: e.tensor_tensor(out=x_tm.t[0:rows, s, :], in0=x_tm.t[0:rows, s, :],
                                              in1=gb.t[0:rows, 0, :], op=ALU.mult),
             reads=[xres[s], gb.res], writes=[xres[s]])
        eng = "dve" if s % 2 == 0 else "pool"
        for hf in range(2):
            S.op(eng, lambda e, hf=hf: e.tensor_tensor(
                out=x_tm.t[0:rows, s, hf * 512:(hf + 1) * 512], in0=x_tm.t[0:rows, s, hf * 512:(hf + 1) * 512],
                in1=gb.t[0:rows, 1, hf * 512:(hf + 1) * 512], op=ALU.add),
                reads=[xres[s], gb.res], writes=[xres[s]])

    def out_proj_residual(srcT, srcres, W, T, rows, nsub, first=True, ln=False):
        for s in range(nsub):
            for n in range(2):
                b = next_bank()

                def mm(e, b=b, s=s, n=n):
                    ins = None
                    for k in range(8):
                        ins = e.matmul(b.t[0:rows, :], lhsT=srcT.t[:, k, s * 128:s * 128 + rows],
                                       rhs=W.t[:, k, n * 512:(n + 1) * 512], start=(k == 0), stop=(k == 7))
                    return ins
                S.op("pe", mm, reads=list(srcres) + [W.res], writes=[b.res])
                if first:
                    S.op("dve", lambda e, b=b, s=s, n=n: e.scalar_tensor_tensor(
                        out=x_tm.t[0:rows, s, n * 512:(n + 1) * 512], in0=x_tm.t[0:rows, s, n * 512:(n + 1) * 512],
                        scalar=ALPHA, in1=b.t[0:rows, :], op0=ALU.mult, op1=ALU.add),
                        reads=[b.res, xres[s]], writes=[xres[s]])
                else:
                    S.op("dve", lambda e, b=b, s=s, n=n: e.tensor_tensor(
                        out=x_tm.t[0:rows, s, n * 512:(n + 1) * 512], in0=x_tm.t[0:rows, s, n * 512:(n + 1) * 512],
                        in1=b.t[0:rows, :], op=ALU.add),
                        reads=[b.res, xres[s]], writes=[xres[s]])
            if ln:
                ln_subtile(s, rows)

    evac_rr = [0]

    def evac_copy(dst_ap, b, T, writes, src_ap=None):
        src = b.t[:, 0:T] if src_ap is None else src_ap
        evac_rr[0] += 1
        if evac_rr[0] % 2 == 0:
            S.op("act", lambda e: e.activation(out=dst_ap, in_=src, func=AF.Copy), reads=[b.res], writes=writes)
        else:
            S.op("dve", lambda e: e.tensor_copy(dst_ap, src), reads=[b.res], writes=writes)

    st_rr = [0]

    def proj_fm(W, T, dst_fn):
        xr = list(xTres)
        for h in range(NH):
            b = next_bank()

            def mm(e, b=b, h=h):
                ins = None
                for k in range(8):
                    ins = e.matmul(b.t[:, 0:T], lhsT=W.t[:, k, h * 128:(h + 1) * 128], rhs=xT.t[:, k, 0:T],
                                   start=(k == 0), stop=(k == 7))
                return ins
            S.op("pe", mm, reads=xr + [W.res], writes=[b.res])
            dst_fn(h, b)

    def attn_layer(step):
        kind, tidx = step
        if kind == "p":
            T, rows, nsub = TT, 128, 4
            t0 = tidx * TT
            nhist = t0
            KTd, Vd = KT_p, V_p
            kout = kp[t0:t0 + T, :]
            vout = vp[t0:t0 + T, :]
            own_key = ("p", tidx)
        else:
            T, rows, nsub = DEC, DEC, 1
            t0 = PAST
            nhist = PAST
            KTd, Vd = KT_s[tidx], V_s[tidx]
            kout = ks[tidx]
            vout = vs[tidx]
            own_key = ("s", tidx)
        xr = list(xTres)
        Wq = wget()

        def q_dst(h, b):
            S.op("act", lambda e: e.activation(out=QT[0].t[0:64, h, 0:T], in_=b.t[0:64, 0:T], func=AF.Copy),
                 reads=[b.res], writes=[QTres[h]])
            S.op("dve", lambda e: e.tensor_copy(QT[1].t[64:128, h, 0:T], b.t[64:128, 0:T]),
                 reads=[b.res], writes=[QTres[h]])
        proj_fm(Wq, T, q_dst)
        Wk = wget()
        proj_fm(Wk, T, lambda h, b: evac_copy(ktb.t[:, h, 0:T], b, T, [ktb.res]))
        ktres = sres(("KT",) + own_key)
        S.dma("sp", ktb.ds, KTd[:, :, t0:t0 + T].rearrange("h p k -> p h k"), ktb.t[:, :, 0:T],
              reads=[ktb.res], writes=[ktres])
        for s in range(nsub):
            for n in range(2):
                b = next_bank()

                def mm(e, b=b, s=s, n=n, W=Wk):
                    ins = None
                    for k in range(8):
                        ins = e.matmul(b.t[0:rows, :], lhsT=xT.t[:, k, s * 128:s * 128 + rows],
                                       rhs=W.t[:, k, n * 512:(n + 1) * 512], start=(k == 0), stop=(k == 7))
                    return ins
                S.op("pe", mm, reads=xr + [Wk.res], writes=[b.res])
                st = kst[st_rr[0] % 2]
                st_rr[0] += 1
                evac_copy(st.t[0:rows, :], b, 512, [st.res], src_ap=b.t[0:rows, :])
                S.dma("sp", st.ds, kout[s * 128:s * 128 + rows, n * 512:(n + 1) * 512], st.t[0:rows, :], reads=[st.res])
        Wv = wget()
        vres = sres(("V",) + own_key)
        for s in range(nsub):
            for n in range(2):
                b = next_bank()

                def mm(e, b=b, s=s, n=n, W=Wv):
                    ins = None
                    for k in range(8):
                        ins = e.matmul(b.t[0:rows, :], lhsT=xT.t[:, k, s * 128:s * 128 + rows],
                                       rhs=W.t[:, k, n * 512:(n + 1) * 512], start=(k == 0), stop=(k == 7))
                    return ins
                S.op("pe", mm, reads=xr + [Wv.res], writes=[b.res])
                st = vst[st_rr[0] % 2]
                vb = vbb[st_rr[0] % 2]
                st_rr[0] += 1
                S.op("act", lambda e, b=b, st=st: e.activation(out=st.t[0:rows, :], in_=b.t[0:rows, :], func=AF.Copy),
                     reads=[b.res], writes=[st.res])
                S.op("pool", lambda e, st=st, vb=vb: e.tensor_copy(vb.t[0:rows, :], st.t[0:rows, :]),
                     reads=[st.res], writes=[vb.res])
                S.dma("sp", st.ds, vout[s * 128:s * 128 + rows, n * 512:(n + 1) * 512], st.t[0:rows, :], reads=[st.res])
                S.dma("sp", vb.ds, Vd[n * 4:(n + 1) * 4, t0 + s * 128:t0 + s * 128 + rows, :].rearrange("h k e -> k h e"),
                      vb.t[0:rows, :].rearrange("k (h e) -> k h e", h=4), reads=[vb.res], writes=[vres])
        npast = nhist // 128
        keep = []
        for h in range(NH):
            kh = 0
            for r in range(1, npast + 1):
                if SLOPES[h] * (1 + (r - 1) * 128) <= ALIBI_THR:
                    kh = r
            keep.append(kh)
        loads = []
        head_chunks = []
        for h in range(NH):
            first_kb = npast - keep[h]
            chs = []
            if keep[h]:
                for c in range(first_kb // (KVC // 128), (npast + KVC // 128 - 1) // (KVC // 128)):
                    kb0 = max(c * (KVC // 128), first_kb)
                    kb1 = min((c + 1) * (KVC // 128), npast)
                    loads.append((h, "hist", kb0 * 128, (kb1 - kb0) * 128))
                    chs.append((kb0, kb1))
            head_chunks.append(chs)
            loads.append((h, "diag", t0, T))
        lstate = {"n": 0}

        def hist_res(kindKV, k0, n):
            if kind == "p":
                return [sres((kindKV, "p", tt)) for tt in range(k0 // TT, (k0 + n + TT - 1) // TT)]
            return [sres((kindKV, "s", tidx, "hist"))]

        def issue_load(idx):
            h, lk, k0, n = loads[idx]
            slot = idx % NSLOT_KV
            ktc, vc = KTc[slot], Vc[slot]
            if lk == "hist":
                rk, rv = hist_res("KT", k0, n), hist_res("V", k0, n)
            else:
                rk, rv = [ktres], [vres]
            S.dma("sp", ktc.ds, ktc.t[:, 0:n], KTd[h, :, k0:k0 + n], reads=rk, writes=[ktc.res])
            if n >= 128:
                S.dma("sp", vc.ds, vc.t[:, 0:n // 128, 0:128], Vd[h, k0:k0 + n, :].rearrange("(kb p) e -> p kb e", p=128),
                      reads=rv, writes=[vc.res])
            else:
                S.dma("sp", vc.ds, vc.t[0:n, 0, 0:128], Vd[h, k0:k0 + n, :], reads=rv, writes=[vc.res])

        def get_load(idx):
            while lstate["n"] < len(loads) and lstate["n"] < idx + NSLOT_KV - 1:
                issue_load(lstate["n"])
                lstate["n"] += 1
            slot = idx % NSLOT_KV
            return KTc[slot], Vc[slot]

        Sb = banks[0:4]

        blocks = []
        lidx = 0
        for h in range(NH):
            done = 0
            for (kb0, kb1) in head_chunks[h]:
                for kabs in range(kb0, kb1):
                    blocks.append(dict(h=h, typ="past", lidx=lidx, kb=kabs - kb0, kabs=kabs, first=(done == 0),
                                       last=(done == keep[h] - 1)))
                    done += 1
                lidx += 1
            ndb = (T + 127) // 128
            uni = (h >= UNI_H)
            for j in range(ndb):
                blocks.append(dict(h=h, typ="diag", lidx=lidx, j=j, first=(j == 0 and not (uni and keep[h])),
                                   last=(j == ndb - 1), haspast=bool(keep[h]) and not uni))
            lidx += 1

        def emit_qk(i, bl):
            ktc, vc = get_load(bl["lidx"])
            bl["ktc"], bl["vc"] = ktc, vc
            par = i % 2
            sb0, sb1 = Sb[par * 2], Sb[par * 2 + 1]
            bl["sb"] = (sb0, sb1)
            bl["p"] = (Pt[par * 2], Pt[par * 2 + 1])
            h = bl["h"]
            if bl["typ"] == "past":
                kb = bl["kb"]

                def qk(e):
                    e.matmul(sb0.t[:, 0:T], lhsT=ktc.t[:, kb * 128:(kb + 1) * 128], rhs=QT[0].t[:, h, 0:T],
                             start=True, stop=True)
                    return e.matmul(sb1.t[:, 0:T], lhsT=ktc.t[:, kb * 128:(kb + 1) * 128], rhs=QT[1].t[:, h, 0:T],
                                    start=True, stop=True)
            else:
                j = bl["j"]
                nk = min(128, T - j * 128)
                q0 = j * 128

                def qk(e):
                    e.matmul(sb0.t[0:nk, q0:T], lhsT=ktc.t[:, j * 128:j * 128 + nk], rhs=QT[0].t[:, h, q0:T],
                             start=True, stop=True)
                    return e.matmul(sb1.t[0:nk, q0:T], lhsT=ktc.t[:, j * 128:j * 128 + nk], rhs=QT[1].t[:, h, q0:T],
                                    start=True, stop=True)
            S.op("pe", qk, reads=[ktc.res, QTres[h]], writes=[sb0.res, sb1.res])

        def emit_exp(bl):
            h = bl["h"]
            slope = SLOPES[h]
            if bl["typ"] == "past":
                rel = bl["kabs"] - npast + 64
                for (sbx, px) in zip(bl["sb"], bl["p"]):
                    S.op("act", lambda e, sbx=sbx, px=px: e.activation(
                        out=px.t[:, 0:T], in_=sbx.t[:, 0:T], func=AF.Exp,
                        bias=biast.t[:, h * 64 + rel:h * 64 + rel + 1], scale=0.125),
                        reads=[sbx.res, biast.res], writes=[px.res])
            else:
                j = bl["j"]
                nk = min(128, T - j * 128)
                q0 = j * 128
                for ci, (sbx, px) in enumerate(zip(bl["sb"], bl["p"])):
                    dt_ = dtmp[ci]
                    mo = (MTW if h >= UNI_H else 0) + MTOFF[j]
                    S.op("dve", lambda e, sbx=sbx, dt_=dt_, mo=mo: e.scalar_tensor_tensor(
                        out=dt_.t[0:nk, q0:T], in0=mtab.t[0:nk, mo:mo + T - q0], scalar=slope,
                        in1=sbx.t[0:nk, q0:T], op0=ALU.mult, op1=ALU.add),
                        reads=[sbx.res, mtab.res], writes=[dt_.res])
                    S.op("act", lambda e, px=px, dt_=dt_: e.activation(
                        out=px.t[0:nk, q0:T], in_=dt_.t[0:nk, q0:T], func=AF.Exp, scale=0.125),
                        reads=[dt_.res], writes=[px.res])

        nqs = nsub
        OB = banks[4:7]
        TB = banks[7]

        def otile(t):
            return OB[t // 3], (t % 3) * 160

        def apv(buf, t):
            return buf.t[0:rows, t * VW:(t + 1) * VW]

        bank_started = {}

        def emit_pv(bl):
            vc = bl["vc"]
            pp = bl["p"]
            first, last = bl["first"], bl["last"]
            if bl["typ"] == "past":
                kb = bl["kb"]
                nk, qs0 = 128, 0
                if kind == "p" and bl["kabs"] < TT // 128:
                    pass
            else:
                kb = bl["j"]
                nk = min(128, T - kb * 128)
                qs0 = kb
            if first:
                bank_started.clear()
            use0 = (bl["typ"] == "past" and kind == "p" and bl["kabs"] < TT // 128)
            plan = []
            for c in range(2):
                for qs in range(qs0, nqs):
                    t = c * 4 + qs
                    bk, off = otile(t)
                    st = bk.name_ not in bank_started
                    bank_started[bk.name_] = True
                    plan.append((c, qs, bk, off, st))

            def pv(e):
                ins = None
                for (c, qs, bk, off, st) in plan:
                    q0 = qs * 128
                    qn = min(128, T - q0)
                    if use0:
                        ins = e.matmul(bk.t[0:qn, off:off + 128], lhsT=pp[c].t[0:nk, q0:q0 + qn], rhs=vc.t[0:nk, kb, 0:128],
                                       start=st, stop=last, skip_group_check=True)
                        ins = e.matmul(bk.t[0:qn, off + 128:off + VW], lhsT=pp[c].t[0:nk, q0:q0 + qn], rhs=ones0.t[0:nk, 0:1],
                                       start=False, stop=last, skip_group_check=True)
                    else:
                        ins = e.matmul(bk.t[0:qn, off:off + VW], lhsT=pp[c].t[0:nk, q0:q0 + qn], rhs=vc.t[0:nk, kb, :],
                                       start=st, stop=last, skip_group_check=True)
                return ins
            S.op("pe", pv, reads=[vc.res, pp[0].res, pp[1].res, ones0.res], writes=[bk.res for bk in OB])

        def emit_past_evac(h):
            for t in range(8):
                c, qs = t // 4, t % 4
                if qs >= nqs:
                    continue
                bk, off = otile(t)
                eng = "dve" if (t // 3) != 1 else "act"
                fcol = ftab.t[0:rows, h * 4 + qs:h * 4 + qs + 1]
                if eng == "dve":
                    S.op("dve", lambda e, t=t, bk=bk, off=off, fcol=fcol: e.tensor_scalar(
                        out=apv(Ap, t), in0=bk.t[0:rows, off:off + VW], scalar1=fcol, scalar2=None, op0=ALU.mult),
                        reads=[bk.res, ftab.res], writes=[Ap.res])
                else:
                    S.op("act", lambda e, t=t, bk=bk, off=off, fcol=fcol: e.activation(
                        out=apv(Ap, t), in_=bk.t[0:rows, off:off + VW], func=AF.Copy, scale=fcol),
                        reads=[bk.res, ftab.res], writes=[Ap.res])

        def emit_finalize(h, haspast):
            Onf = Onf2[h % 2]
            for t in range(8):
                c, qs = t // 4, t % 4
                if qs >= nqs:
                    continue
                bk, off = otile(t)
                if haspast:
                    S.op("dve", lambda e, t=t, bk=bk, off=off: e.tensor_tensor(
                        out=apv(At, t), in0=bk.t[0:rows, off:off + VW], in1=apv(Ap, t), op=ALU.add),
                        reads=[bk.res, Ap.res], writes=[At.res])
                elif (t // 3) != 1:
                    S.op("dve", lambda e, t=t, bk=bk, off=off: e.tensor_copy(apv(At, t), bk.t[0:rows, off:off + VW]),
                         reads=[bk.res], writes=[At.res])
                else:
                    S.op("act", lambda e, t=t, bk=bk, off=off: e.activation(out=apv(At, t), in_=bk.t[0:rows, off:off + VW],
                                                                           func=AF.Copy),
                         reads=[bk.res], writes=[At.res])
            Atv = At.t[0:rows, :].rearrange("p (t w) -> p t w", w=VW)
            S.op("dve", lambda e: e.reciprocal(rr.t[0:rows, :, :], Atv[:, :, 128:129]), reads=[At.res], writes=[rr.res])
            S.op("dve", lambda e: e.tensor_scalar(out=rr.t[0:rows, 4:8, :], in0=rr.t[0:rows, 4:8, :],
                                                  scalar1=neglam.t[0:rows, 0:1], scalar2=None, op0=ALU.mult),
                 reads=[rr.res, neglam.res], writes=[rr.res])
            for qs in range(nqs):
                S.op("dve", lambda e, qs=qs: e.tensor_scalar(out=Otm.t[0:rows, qs, :], in0=Atv[:, qs, 0:128],
                                                            scalar1=rr.t[0:rows, qs, :], scalar2=None, op0=ALU.mult),
                     reads=[At.res, rr.res], writes=[Otm.res])
            for qs in range(nqs):
                S.op("dve", lambda e, qs=qs: e.scalar_tensor_tensor(out=Otm.t[0:rows, qs, :], in0=Atv[:, 4 + qs, 0:128],
                                                                   scalar=rr.t[0:rows, 4 + qs, :], in1=Otm.t[0:rows, qs, :],
                                                                   op0=ALU.mult, op1=ALU.add),
                     reads=[At.res, rr.res, Otm.res], writes=[Otm.res])
            for qs in range(nqs):
                S.op("dve", lambda e, qs=qs: e.scalar_tensor_tensor(out=Onf.t[0:rows, qs, :], in0=Otm.t[0:rows, qs, :],
                                                                   scalar=1.0, in1=Otm.t[0:rows, qs, :],
                                                                   op0=ALU.mult, op1=ALU.mult, accum_out=ss4.t[0:rows, qs, :]),
                     reads=[Otm.res], writes=[Onf.res, ss4.res])

        def emit_finalize_B(h):
            S.op("act", lambda e: e.activation(out=ss4.t[0:rows, 0:nqs, :], in_=ss4.t[0:rows, 0:nqs, :], func=AF.Ln,
                                               scale=1.0 / 128.0, bias=epst.t[0:rows, 0:1]),
                 reads=[ss4.res, epst.res], writes=[ss4.res])
            S.op("act", lambda e: e.activation(out=ss4.t[0:rows, 0:nqs, :], in_=ss4.t[0:rows, 0:nqs, :], func=AF.Exp, scale=-0.5),
                 reads=[ss4.res], writes=[ss4.res])

        def emit_finalize_C(h):
            Onf = Onf2[h % 2]
            for qs in range(nqs):
                S.op("dve", lambda e, qs=qs: e.scalar_tensor_tensor(out=Onf.t[0:rows, qs, :], in0=Otm.t[0:rows, qs, :],
                                                                   scalar=ss4.t[0:rows, qs, :], in1=glb.t[0:rows, :],
                                                                   op0=ALU.mult, op1=ALU.mult),
                     reads=[Otm.res, ss4.res, glb.res, Onf.res], writes=[Onf.res])

        def emit_head_transpose(h):
            Onf = Onf2[h % 2]
            def tr(e):
                ins = None
                for qs in range(nqs):
                    ins = e.transpose(TB.t[:, qs * 128:qs * 128 + rows], Onf.t[0:rows, qs, :], ident.t[0:rows, 0:rows])
                return ins
            S.op("pe", tr, reads=[Onf.res, ident.res], writes=[TB.res])
            if h % 2 == 0:
                S.op("act", lambda e: e.activation(out=OnT.t[:, h, 0:T], in_=TB.t[:, 0:T], func=AF.Copy),
                     reads=[TB.res], writes=[OnTres[h]])
            else:
                S.op("dve", lambda e: e.tensor_copy(OnT.t[:, h, 0:T], TB.t[:, 0:T]), reads=[TB.res], writes=[OnTres[h]])

        nb = len(blocks)
        head_end = {}
        for i, bl in enumerate(blocks):
            head_end[bl["h"]] = i
        deferred = []
        seqc = [0]

        def defer(idx, fn):
            deferred.append((idx, seqc[0], fn))
            seqc[0] += 1

        def run_deferred(i):
            deferred.sort(key=lambda x: (x[0], x[1]))
            while deferred and deferred[0][0] <= i:
                deferred.pop(0)[2]()

        emit_qk(0, blocks[0])
        for i, bl in enumerate(blocks):
            if i + 1 < nb:
                emit_qk(i + 1, blocks[i + 1])
            emit_exp(bl)
            emit_pv(bl)
            if bl["typ"] == "past" and bl["last"] and bl["h"] < UNI_H:
                emit_past_evac(bl["h"])
            run_deferred(i)
            if bl["typ"] == "diag" and bl["last"]:
                h = bl["h"]
                run_deferred(10 ** 9)
                emit_finalize(h, bl["haspast"])
                defer(i + 3, lambda h=h: emit_finalize_B(h))
                defer(i + 5, lambda h=h: emit_finalize_C(h))
                nxt_end = head_end.get(h + 1, nb + 10)
                defer(max(nxt_end, i + 6), lambda h=h: emit_head_transpose(h))
        run_deferred(10 ** 9)
        S.stage(6)
        Wo = wget()
        out_proj_residual(OnT, OnTres, Wo, T, rows, nsub, first=True, ln=True)
        S.stage(7)

    def conv_layer(step):
        kind, tidx = step
        if kind == "p":
            T, rows, nsub = TT, 128, 4
        else:
            T, rows, nsub = DEC, DEC, 1
        hl = halo
        if kind == "s":
            for j_ in range(2):
                S.dma("sp", hl.ds, hl.t[:, :, j_], sc[tidx, j_].rearrange("(dc p) -> p dc", p=128), writes=[hl.res],
                      allow_slow_non_contiguous=True)
            S.op("pool", lambda e: e.tensor_copy(uext.t[:, :, 0:2], hl.t[:]), reads=[hl.res], writes=ures)
        else:
            S.op("pool", lambda e: e.tensor_scalar(out=uext.t[:, :, 0:2], in0=hl.t[:], scalar1=hmt.t[:, tidx:tidx + 1],
                                                   scalar2=None, op0=ALU.mult),
                 reads=[hl.res, hmt.res], writes=ures)
        Wh = wget()
        proj_fm(Wh, T, lambda dc, b: S.op("act", lambda e: e.activation(out=uext.t[:, dc, 2:2 + T], in_=b.t[:, 0:T],
                                                                         func=AF.Copy),
                                          reads=[b.res], writes=[ures[dc]]))
        Wc = wget()
        proj_fm(Wc, T, lambda dc, b: S.op("dve", lambda e: e.tensor_tensor(out=uext.t[:, dc, 2:2 + T], in0=b.t[:, 0:T],
                                                                            in1=uext.t[:, dc, 2:2 + T], op=ALU.mult),
                                          reads=[b.res, ures[dc]], writes=[ures[dc]]))
        S.op("pool", lambda e: e.tensor_copy(hl.t[:], uext.t[:, :, T:T + 2]), reads=ures, writes=[hl.res])
        dsts = []
        if kind == "p" and tidx == SEQ // TT - 1:
            dsts.append(cpo[0])
        if kind == "p" and tidx == SEQ // TT:
            dsts.append(cpo[1])
        if kind == "s":
            dsts.append(cs[tidx])
        for dst in dsts:
            for j_ in range(2):
                S.dma("sp", hl.ds, dst[j_].rearrange("(dc p) -> p dc", p=128), hl.t[:, :, j_], reads=[hl.res],
                      allow_slow_non_contiguous=True)
        Wb = wget()

        def b_dst(dc, b):
            z = zc[dc % 2]
            S.op("act", lambda e: e.activation(out=z.t[:, 0:T], in_=uext.t[:, dc, 0:T], func=AF.Copy,
                                               scale=cwt.t[:, 0, dc:dc + 1]),
                 reads=[ures[dc], cwt.res], writes=[z.res])
            for jj in (1, 2):
                S.op("dve", lambda e, jj=jj: e.scalar_tensor_tensor(
                    out=z.t[:, 0:T], in0=uext.t[:, dc, jj:jj + T], scalar=cwt.t[:, jj, dc:dc + 1],
                    in1=z.t[:, 0:T], op0=ALU.mult, op1=ALU.add),
                    reads=[ures[dc], cwt.res, z.res], writes=[z.res])
            S.op("dve", lambda e: e.tensor_tensor(out=OnT.t[:, dc, 0:T], in0=b.t[:, 0:T], in1=z.t[:, 0:T], op=ALU.mult),
                 reads=[b.res, z.res], writes=[OnTres[dc]])
        proj_fm(Wb, T, b_dst)
        Wo = wget()
        out_proj_residual(OnT, OnTres, Wo, T, rows, nsub, first=True, ln=True)

    def mlp_layer(step):
        kind, tidx = step
        if kind == "p":
            T, rows, nsub = TT, 128, 4
        else:
            T, rows, nsub = DEC, DEC, 1

        def up(j):
            Wu = wget()
            hb = hT[j % 2]

            def dst(fc, b):
                rb = rbuf[fc % 2]
                hres = OnTres[fc] if hb is OnT else ktb.res
                S.op("act", lambda e: e.activation(out=rb.t[:, 0:T], in_=b.t[:, 0:T], func=AF.Relu),
                     reads=[b.res], writes=[rb.res])
                S.op("pool", lambda e: e.tensor_tensor(out=hb.t[:, fc, 0:T], in0=rb.t[:, 0:T], in1=rb.t[:, 0:T],
                                                       op=ALU.mult),
                     reads=[rb.res], writes=[hres])
            proj_fm(Wu, T, dst)

        def down(j):
            Wd = wget()
            hb = hT[j % 2]
            out_proj_residual(hb, OnTres if hb is OnT else [ktb.res], Wd, T, rows, nsub, first=(j == 0), ln=(j == 3))
        up(0); up(1); down(0); up(2); down(1); up(3); down(2); down(3)

    xds = x_tm.ds
    ccds = S.dsem("cc")
    for si, step in enumerate(steps):
        kind, tidx = step
        if kind == "p":
            T, rows, nsub = TT, 128, 4
            src = xp[tidx * TT:(tidx + 1) * TT, :].rearrange("(s p) d -> p s d", p=128)
            S.dma("sp", xds, x_tm.t[:, :, :], src, writes=xres)
        else:
            T, rows, nsub = DEC, DEC, 1
            S.dma("sp", xds, x_tm.t[0:DEC, 0, :], xs[tidx], writes=xres)
        if exchange and si > 0:
            if kind == "p":
                S.dma("sp", uext.ds, rview, recv[0:TT, :].rearrange("(s p) d -> p s d", p=128), reads=[recv_res], writes=ures)
            else:
                S.dma("sp", uext.ds, rview[0:DEC, 0, :], recv[0:DEC, :], reads=[recv_res], writes=ures)
            for s in range(nsub):
                S.op("dve", lambda e, s=s, rows=rows: e.scalar_tensor_tensor(
                    out=x_tm.t[0:rows, s, :], in0=rview[0:rows, s, :], scalar=selt.t[0:rows, 0:1],
                    in1=x_tm.t[0:rows, s, :], op0=ALU.mult, op1=ALU.add),
                    reads=ures + [xres[s], selt.res], writes=[xres[s]])
        make_xT(T, rows, nsub)
        S.stage(3)
        pro = (kind == "p")
        ln_prepare(0)
        attn_layer(step)
        if pro:
            pro_prefetch()
        make_xT(T, rows, nsub)
        S.stage(8)
        ln_prepare(1)
        mlp_layer(step)
        S.stage(9)
        if pro:
            pro_consume()
            pro_prefetch()
        make_xT(T, rows, nsub)
        ln_prepare(2)
        conv_layer(step)
        if pro:
            pro_consume()
            pro_prefetch()
        make_xT(T, rows, nsub)
        ln_prepare(3)
        mlp_layer(step)
        if pro:
            pro_consume()
        if exchange and si < len(steps) - 1:
            if kind == "p":
                S.dma("sp", xds, send.rearrange("(s p) d -> p s d", p=128), x_tm.t[:, :, :], reads=xres, writes=[send_res])
            else:
                S.dma("sp", xds, send[0:DEC, :], x_tm.t[0:DEC, 0, :], reads=xres, writes=[send_res])
            S._deps("pool", [send_res], [recv_res])
            ccds.count += 1
            ev = (ccds.key, ccds.count)
            S.streams["pool"].append(("op", lambda e: e.collective_compute(
                "AllGather", ALU.bypass, replica_groups=PAIRS, ins=[send], outs=[recv]), ev[0], 1))
            S._record(ev, [send_res], [recv_res])
        if kind == "p":
            S.dma("sp", xds, yp[tidx * TT:(tidx + 1) * TT, :].rearrange("(s p) d -> p s d", p=128), x_tm.t[:, :, :],
                  reads=xres)
        else:
            S.dma("sp", xds, ys[tidx], x_tm.t[0:DEC, 0, :], reads=xres)
        if kind == "p":
            emit_cv(1)
            if si + 1 < len(steps) and steps[si + 1][0] == "s":
                pro_flush()
                emit_cv(10 ** 6)
    S.emit()
    return nc, S


def _tables():
    biast = np.zeros((128, NH * 64), np.float32)
    p = np.arange(128, dtype=np.float64)
    for h in range(NH):
        for r in range(64):
            rel = r - 64
            biast[:, h * 64 + r] = (SLOPES[h] * (rel * 128 + p)).astype(np.float32)
    mt = np.zeros((128, 2 * MTW), np.float32)
    for j in range(4):
        q = np.arange(j * 128, TT)
        k = j * 128 + np.arange(128)
        vis = (k[:, None] // 64) <= (q[None, :] // 64)
        m = -np.abs(q[None, :] - k[:, None]).astype(np.float64)
        mt[:, MTOFF[j]:MTOFF[j] + len(q)] = (8.0 * np.where(vis, m, -1.0e5)).astype(np.float32)
        mt[:, MTW + MTOFF[j]:MTW + MTOFF[j] + len(q)] = (8.0 * np.where(vis, m + q[None, :], -1.0e5)).astype(np.float32)
    qrel = np.broadcast_to(np.arange(TT, dtype=np.float32)[None, :], (128, TT)).copy()
    return biast, mt, qrel


def _ftab():
    ft = np.zeros((128, NH * 4), np.float32)
    p = np.arange(128, dtype=np.float64)
    for h in range(NH):
        for qs in range(4):
            ft[:, h * 4 + qs] = np.exp(-SLOPES[h] * (qs * 128 + p)).astype(np.float32)
    return ft


_CACHE = {}


def make_in_maps(x_prompt, x_sample, cache_k, cache_v, state_conv, w_qkv, lambda_q1, lambda_k1,
                 lambda_q2, lambda_k2, g_subln, w_attn_out, w_conv_in, w_conv, w_conv_out,
                 w_up, w_down, ln_g, ln_b):
    f = lambda a: np.ascontiguousarray(np.asarray(a, dtype=np.float32))
    biast, mt, qrel = _tables()
    xp = np.asarray(x_prompt); xs = np.asarray(x_sample)
    ck = np.asarray(cache_k).reshape(2, 8, PAST, D); cv = np.asarray(cache_v).reshape(2, 8, PAST, D)
    sc = np.asarray(state_conv)
    lng = np.asarray(ln_g); lnb = np.asarray(ln_b)
    lam = np.stack([np.asarray(lambda_q1), np.asarray(lambda_k1), np.asarray(lambda_q2), np.asarray(lambda_k2)], axis=1)
    in_maps = []
    for c in range(8):
        p, r = c // 2, c % 2
        m = {"ident": np.eye(128, dtype=np.float32), "biastab": biast, "mtab": mt, "qrel": qrel, "ftab": _ftab()}
        xpc = np.zeros((NPS * TT, D), np.float32)
        xsc = np.zeros((NSS, DEC, D), np.float32)
        ckc = np.zeros((NSS, PAST, D), np.float32)
        cvc = np.zeros((NSS, PAST, D), np.float32)
        scc = np.zeros((NSS, 2, D), np.float32)
        if r == 0:
            xpc[:SEQ] = xp[p]
            xsc[0:2] = xs[2 * p:2 * p + 2]
        for j in range(2):
            ckc[j + r] = ck[r, 2 * p + j]
            cvc[j + r] = cv[r, 2 * p + j]
            scc[j + r] = sc[r, 2 * p + j]
        m.update(xp=xpc, xs=xsc, ck=ckc, cv=cvc, sc=scc)
        m["w_qkv"] = f(np.asarray(w_qkv)[r]); m["w_ao"] = f(np.asarray(w_attn_out)[r])
        m["w_ci"] = f(np.asarray(w_conv_in)[r]); m["w_cw"] = f(np.asarray(w_conv)[r]); m["w_co"] = f(np.asarray(w_conv_out)[r])
        m["w_up"] = f(np.asarray(w_up)[2 * r:2 * r + 2]); m["w_dn"] = f(np.asarray(w_down)[2 * r:2 * r + 2])
        m["lam4"] = f(lam[r]); m["gsub"] = f(np.asarray(g_subln)[r].reshape(128, 1))
        m["ln_g"] = f(lng[2 * r:2 * r + 2].reshape(4, D)); m["ln_b"] = f(lnb[2 * r:2 * r + 2].reshape(4, D))
        li0 = lam_init_of(2 * r)
        m["linit"] = np.tile(np.array([[-li0, 1.0 - li0]], np.float32), (128, 1))
        m["sel"] = np.full((128, 1), float(r), np.float32)
        hm = np.ones((128, NPS), np.float32)
        hm[:, 0] = 0.0
        if r == 1:
            hm[:, 1] = 0.0
        m["hm"] = hm
        m["ones0"] = np.full((128, 128), 1.0 - r, np.float32)
        in_maps.append(m)
    return in_maps


def assemble(res):
    yp = np.zeros((4, SEQ, D), np.float32); ys = np.zeros((8, DEC, D), np.float32)
    kp = np.zeros((2, 4, SEQ, D), np.float32); vp = np.zeros((2, 4, SEQ, D), np.float32)
    cp = np.zeros((2, 4, 2, D), np.float32)
    ks = np.zeros((2, 8, DEC, D), np.float32); vs = np.zeros((2, 8, DEC, D), np.float32)
    cs = np.zeros((2, 8, 2, D), np.float32)
    for p in range(4):
        a, b = res[2 * p], res[2 * p + 1]
        yp[p] = b["yp"][TT:TT + SEQ]
        kp[0, p] = a["kp"][0:SEQ]; kp[1, p] = b["kp"][TT:TT + SEQ]
        vp[0, p] = a["vp"][0:SEQ]; vp[1, p] = b["vp"][TT:TT + SEQ]
        cp[0, p] = a["cpo"][0]; cp[1, p] = b["cpo"][1]
        for j in range(2):
            ys[2 * p + j] = b["ys"][1 + j]
            ks[0, 2 * p + j] = a["ks"][j]; ks[1, 2 * p + j] = b["ks"][1 + j]
            vs[0, 2 * p + j] = a["vs"][j]; vs[1, 2 * p + j] = b["vs"][1 + j]
            cs[0, 2 * p + j] = a["cs"][j]; cs[1, 2 * p + j] = b["cs"][1 + j]
    return (yp, ys, kp.reshape(2, 4, SEQ, NH, 2, 64), vp.reshape(2, 4, SEQ, NH, 128), cp,
            ks.reshape(2, 8, DEC, NH, 2, 64), vs.reshape(2, 8, DEC, NH, 128), cs)


def kernel(**inputs):
    if "nc" not in _CACHE:
        _CACHE["nc"] = build_program()[0]
    nc = _CACHE["nc"]
    in_maps = make_in_maps(**inputs)
    res = run_bass_kernel_spmd(nc, in_maps, core_ids=list(range(8))).results
    return assemble(res)
```
